# Optimizing a Trainium2 kernel written in Bass

```python
import math
import jax, jax.numpy as jnp
from jax import lax
import numpy as np

D_MODEL = 1024
BATCH = 8
SEQ = 2048
DEPTH = 4

N_EVEN = (DEPTH + 1) // 2
N_ODD = DEPTH // 2
EPS = 1e-6
NEG_INF = -1e30
Q_BLOCK = 128
FOX_WIDTH = D_MODEL // 2
FOX_HEAD_DIM = 64
FOX_HEADS = FOX_WIDTH // FOX_HEAD_DIM
SC_WIDTH = D_MODEL - FOX_WIDTH
SC_GROUPS = SC_WIDTH // 64
SC_K = 3
AB_IN = 3 * FOX_WIDTH + FOX_HEADS + 3 * SC_WIDTH
LRU_WIDTH = D_MODEL
LRU_BW = 256
LRU_BLOCKS = LRU_WIDTH // LRU_BW
RG_CONV_K = 4
RG_C = 8.0
RG_MIN_RAD = 0.9
RG_MAX_RAD = 0.999
MEM_LEN = 256
MEM_HEADS = 4
MEM_HEAD_DIM = D_MODEL // MEM_HEADS
D_FF = ((8 * D_MODEL // 3 + 255) // 256) * 256

kernel_name = "fox_shortconv_rglru_sandwich_hybrid"


def rmsnorm(x, g):
    x32 = x.astype(jnp.float32)
    y = x32 * lax.rsqrt(jnp.mean(x32 * x32, axis=-1, keepdims=True) + EPS)
    return y.astype(x.dtype) * g


def causal_depthwise_conv(x, w):
    k_width, ch = w.shape
    return lax.conv_general_dilated(
        x, w[:, None, :], window_strides=(1,), padding=[(k_width - 1, 0)],
        dimension_numbers=("NWC", "WIO", "NWC"), feature_group_count=ch)


def forgetting_attention(q, k, v, log_f):
    seq = q.shape[1]
    scale = q.shape[-1] ** -0.5
    cum = jnp.cumsum(log_f, axis=1).transpose(0, 2, 1)
    outs = []
    for blk in range(seq // Q_BLOCK):
        lo, hi = blk * Q_BLOCK, (blk + 1) * Q_BLOCK
        s = jnp.einsum("bqhd,bkhd->bhqk", q[:, lo:hi], k[:, :hi],
                       preferred_element_type=jnp.float32) * scale
        s = s + cum[:, :, lo:hi, None] - cum[:, :, None, :hi]
        causal = (lo + jnp.arange(Q_BLOCK))[:, None] >= jnp.arange(hi)[None, :]
        s = jnp.where(causal, s, NEG_INF)
        p = jax.nn.softmax(s, axis=-1).astype(v.dtype)
        outs.append(jnp.einsum("bhqk,bkhd->bqhd", p, v[:, :hi]))
    return jnp.concatenate(outs, axis=1)


def fox_shortconv_mixer(h, w_in, b_f, conv_w, w_out):
    bsz, seq, _ = h.shape
    proj = h @ w_in
    i1 = FOX_WIDTH
    i2 = 2 * FOX_WIDTH
    i3 = 3 * FOX_WIDTH
    i4 = i3 + FOX_HEADS
    i5 = i4 + SC_WIDTH
    i6 = i5 + SC_WIDTH
    q, k, v, f_logit, b_gate, c_gate, u = jnp.split(proj, [i1, i2, i3, i4, i5, i6], axis=-1)
    heads = lambda t: t.reshape(bsz, seq, FOX_HEADS, FOX_HEAD_DIM)
    log_f = jax.nn.log_sigmoid((f_logit + b_f).astype(jnp.float32))
    y_a = forgetting_attention(heads(q), heads(k), heads(v), log_f).reshape(bsz, seq, FOX_WIDTH)
    y_b = b_gate * causal_depthwise_conv(c_gate * u, conv_w)
    return jnp.concatenate([y_a, y_b], axis=-1) @ w_out


def _lru_combine(c1, c2):
    a1, b1 = c1
    a2, b2 = c2
    return a1 * a2, a2 * b1 + b2


def rglru_mixer(h, w_in, conv_w, conv_b, w_a, b_a, w_i, b_i, lam, w_out):
    bsz, seq, _ = h.shape
    gate, u = jnp.split(h @ w_in, 2, axis=-1)
    u = causal_depthwise_conv(u, conv_w) + conv_b
    ub = u.reshape(bsz, seq, LRU_BLOCKS, LRU_BW)
    r = jax.nn.sigmoid(jnp.einsum("bsnc,ncd->bsnd", ub, w_a) + b_a).reshape(bsz, seq, LRU_WIDTH)
    i = jax.nn.sigmoid(jnp.einsum("bsnc,ncd->bsnd", ub, w_i) + b_i).reshape(bsz, seq, LRU_WIDTH)
    log_a = -RG_C * r.astype(jnp.float32) * jax.nn.softplus(-lam.astype(jnp.float32))
    a = jnp.exp(log_a)
    b = jnp.sqrt(-jnp.expm1(2.0 * log_a)) * (i * u).astype(jnp.float32)
    _, hs = lax.associative_scan(_lru_combine, (a, b), axis=1)
    y = jax.nn.gelu(gate) * hs.astype(h.dtype)
    return y @ w_out


def memory_cross_attention(h, m, w_q, w_kv, w_o):
    bsz, seq, _ = h.shape
    mlen = m.shape[1]
    q = (h @ w_q).reshape(bsz, seq, MEM_HEADS, MEM_HEAD_DIM)
    k, v = jnp.split(m @ w_kv, 2, axis=-1)
    k = k.reshape(bsz, mlen, MEM_HEADS, MEM_HEAD_DIM)
    v = v.reshape(bsz, mlen, MEM_HEADS, MEM_HEAD_DIM)
    s = jnp.einsum("bqhd,bkhd->bhqk", q, k, preferred_element_type=jnp.float32) * (MEM_HEAD_DIM ** -0.5)
    p = jax.nn.softmax(s, axis=-1).astype(v.dtype)
    o = jnp.einsum("bhqk,bkhd->bqhd", p, v).reshape(bsz, seq, D_MODEL)
    return o @ w_o


def swiglu(h, w_gu, w_down):
    g, u = jnp.split(h @ w_gu, 2, axis=-1)
    return (jax.nn.silu(g) * u) @ w_down


def setup_inputs(seed: int = 0) -> dict:
    key = jax.random.key(seed)
    ks = iter(jax.random.split(key, 40))
    dense = lambda shape, fan_in: jax.random.normal(next(ks), shape, jnp.float32) * (fan_in ** -0.5)
    gain = lambda shape: 1.0 + 0.02 * jax.random.normal(next(ks), shape, jnp.float32)
    small = lambda shape: 0.01 * jax.random.normal(next(ks), shape, jnp.float32)
    x = jax.random.normal(next(ks), (BATCH, SEQ, D_MODEL), jnp.float32)
    mem = jax.random.normal(next(ks), (BATCH, MEM_LEN, D_MODEL), jnp.float32)
    rad = jax.random.uniform(next(ks), (N_ODD, LRU_WIDTH), jnp.float32, RG_MIN_RAD, RG_MAX_RAD)
    c_lam = -jnp.log(jnp.expm1(-jnp.log(rad) / RG_C))
    return {
        "x": x,
        "mem": mem,
        "g_mix_pre": gain((DEPTH, D_MODEL)),
        "g_mix_post": gain((DEPTH, D_MODEL)),
        "g_cross_pre": gain((DEPTH, D_MODEL)),
        "g_mem": gain((DEPTH, D_MODEL)),
        "g_cross_post": gain((DEPTH, D_MODEL)),
        "g_ffn_pre": gain((DEPTH, D_MODEL)),
        "g_ffn_post": gain((DEPTH, D_MODEL)),
        "w_xq": dense((DEPTH, D_MODEL, D_MODEL), D_MODEL),
        "w_xkv": dense((DEPTH, D_MODEL, 2 * D_MODEL), D_MODEL),
        "w_xo": dense((DEPTH, D_MODEL, D_MODEL), D_MODEL),
        "w_ffn_gu": dense((DEPTH, D_MODEL, 2 * D_FF), D_MODEL),
        "w_ffn_down": dense((DEPTH, D_FF, D_MODEL), D_FF),
        "ab_w_in": dense((N_EVEN, D_MODEL, AB_IN), D_MODEL),
        "ab_b_f": jax.random.uniform(next(ks), (N_EVEN, FOX_HEADS), jnp.float32, 2.0, 5.0),
        "ab_conv_w": dense((N_EVEN, SC_K, SC_WIDTH), SC_K),
        "ab_w_out": dense((N_EVEN, D_MODEL, D_MODEL), D_MODEL),
        "c_w_in": dense((N_ODD, D_MODEL, 2 * LRU_WIDTH), D_MODEL),
        "c_conv_w": dense((N_ODD, RG_CONV_K, LRU_WIDTH), RG_CONV_K),
        "c_conv_b": small((N_ODD, LRU_WIDTH)),
        "c_w_a": dense((N_ODD, LRU_BLOCKS, LRU_BW, LRU_BW), LRU_BW),
        "c_b_a": small((N_ODD, LRU_BLOCKS, LRU_BW)),
        "c_w_i": dense((N_ODD, LRU_BLOCKS, LRU_BW, LRU_BW), LRU_BW),
        "c_b_i": small((N_ODD, LRU_BLOCKS, LRU_BW)),
        "c_lam": c_lam,
        "c_w_out": dense((N_ODD, LRU_WIDTH, D_MODEL), LRU_WIDTH),
    }


def reference(x, mem, g_mix_pre, g_mix_post, g_cross_pre, g_mem, g_cross_post,
              g_ffn_pre, g_ffn_post, w_xq, w_xkv, w_xo, w_ffn_gu, w_ffn_down,
              ab_w_in, ab_b_f, ab_conv_w, ab_w_out,
              c_w_in, c_conv_w, c_conv_b, c_w_a, c_b_a, c_w_i, c_b_i, c_lam, c_w_out):
    for layer in range(DEPTH):
        h = rmsnorm(x, g_mix_pre[layer])
        if layer % 2 == 0:
            e = layer // 2
            y = fox_shortconv_mixer(h, ab_w_in[e], ab_b_f[e], ab_conv_w[e], ab_w_out[e])
        else:
            o = layer // 2
            y = rglru_mixer(h, c_w_in[o], c_conv_w[o], c_conv_b[o], c_w_a[o], c_b_a[o],
                            c_w_i[o], c_b_i[o], c_lam[o], c_w_out[o])
        x = x + rmsnorm(y, g_mix_post[layer])
        h = rmsnorm(x, g_cross_pre[layer])
        m = rmsnorm(mem, g_mem[layer])
        y = memory_cross_attention(h, m, w_xq[layer], w_xkv[layer], w_xo[layer])
        x = x + rmsnorm(y, g_cross_post[layer])
        h = rmsnorm(x, g_ffn_pre[layer])
        y = swiglu(h, w_ffn_gu[layer], w_ffn_down[layer])
        x = x + rmsnorm(y, g_ffn_post[layer])
    return x
```

```python
import numpy as np
from contextlib import ExitStack
import concourse.bass as bass
import concourse.mybir as mybir
from concourse.bass_utils import run_bass_kernel_spmd

F32 = mybir.dt.float32
BF16 = mybir.dt.bfloat16
AF = mybir.ActivationFunctionType
ALU = mybir.AluOpType

DEPTH = 4
D = 1024
T = 2048
NT = 1024
TC = 512
MEM = 256
DFF = 2816
NFF = 22
EPS = 1e-6
NSLOT = 3
SLOT_W = 2048
NG = 4

VCOL = {}
_c = 0
for _n in ("g_mix_pre", "g_mix_post", "g_cross_pre", "g_mem", "g_cross_post", "g_ffn_pre", "g_ffn_post"):
    VCOL[_n] = _c
    _c += 32
VCOL["abcw"] = _c; _c += 24
VCOL["ccw"] = _c; _c += 64
for _n in ("ccb", "cba", "cbi", "clam"):
    VCOL[_n] = _c
    _c += 16
VCOL["bf"] = _c; _c += 2
VCOL["one"] = _c; _c += 1
VCOL["eps"] = _c; _c += 1
VCOL["zero"] = _c; _c += 1
VCOL["id8"] = _c; _c += 8
NV = _c + (_c % 2)

CB_ID, CB_MASK, CB_ONES, CB_SEL = 0, 128, 256, 384
NCB = 384 + 1024


def panel_order(layers):
    order = []
    for L in layers:
        if L % 2 == 0:
            for hf in range(2):
                order += [(L, ("abq",), 4096), (L, ("abk",), 4096), (L, ("abv",), 4096)]
                order += [(L, ("abc", i), 8 * 392) for i in range(4)]
                order += [(L, ("abo", j), 4096) for j in range(2)]
        else:
            for hf in range(2):
                order += [(L, ("cin", 0), 4096)]
                for n in range(4):
                    if n < 3:
                        order += [(L, ("cin", n + 1), 4096)]
                    order += [(L, ("cg", n), 1024)]
                order += [(L, ("co", j), 4096) for j in range(2)]
        order += [(L, ("xk", j), 4096) for j in range(2)]
        order += [(L, ("xv", j), 4096) for j in range(2)]
        for hf in range(2):
            order += [(L, ("xq", j), 4096) for j in range(2)]
            order += [(L, ("xo", j), 4096) for j in range(2)]
        for hf in range(2):
            order += [(L, ("gu", f), 2048) for f in range(NFF)]
            order += [(L, ("dn", oc), NFF * 128) for oc in range(8)]
    return order


def panel_offsets(layers):
    offs = {}
    tot = {L: 0 for L in layers}
    for (L, key, n) in panel_order(layers):
        if (L, key) not in offs:
            offs[(L, key)] = tot[L]
            tot[L] += n
    return offs, tot


class Dep:
    __slots__ = ("w", "r")

    def __init__(self):
        self.w = None
        self.r = {}


class Eng:
    def __init__(self, name):
        self.name = name
        self.ops = []
        self.sem = None
        self.cnt = 0
        self.seen = {}


class Sched:
    def __init__(self, nc, stack):
        self.nc = nc
        self.stack = stack
        self.engs = {n: Eng(n) for n in ("pe", "act", "dve", "pool", "sp")}
        self.nsem = 0
        self.new_epoch()

    def new_sem(self, name):
        self.nsem += 1
        return self.stack.enter_context(self.nc.semaphore(f"{name}_{self.nsem}"))

    def new_epoch(self):
        for e in self.engs.values():
            e.sem = self.new_sem("e_" + e.name)
            e.cnt = 0

    def _waits(self, eng, reads, writes):
        need = {}

        def add(tok, raw):
            if tok is None:
                return
            sem, val, src = tok
            if src is eng and eng.name == "pe":
                return
            k = id(sem)
            if eng.seen.get(k, 0) >= val:
                return
            if k not in need or need[k][1] < val:
                need[k] = (sem, val)

        for d in reads:
            add(d.w, True)
        for d in writes:
            add(d.w, False)
            for tok in d.r.values():
                add(tok, False)
        for k, (sem, val) in need.items():
            eng.seen[k] = val
            eng.ops.append(("wait", sem, val))

    def op(self, engname, fn, reads=(), writes=(), inc=True):
        eng = self.engs[engname]
        self._waits(eng, reads, writes)
        if inc:
            eng.cnt += 1
            tok = (eng.sem, eng.cnt, eng)
        else:
            tok = (eng.sem, eng.cnt + 1, eng)
        eng.ops.append(("op", fn, eng.sem if inc else None, 1))
        for d in reads:
            d.r[id(eng.sem)] = tok
        for d in writes:
            d.w = tok
            d.r = {}

    def dma(self, engname, out, in_, dsem, reads=(), writes=()):
        eng = self.engs[engname]
        self._waits(eng, reads, writes)
        dsem[1] += 16
        tok = (dsem[0], dsem[1], None)
        eng.ops.append(("op", lambda e, o=out, i=in_: e.dma_start(out=o, in_=i), dsem[0], 16))
        for d in reads:
            d.r[id(dsem[0])] = tok
        for d in writes:
            d.w = tok
            d.r = {}

    def dsem(self, name):
        return [self.new_sem("d_" + name), 0]

    def wait_tok(self, engname, sem, val):
        self.engs[engname].ops.append(("wait", sem, val))

    def barrier(self):
        for e in self.engs.values():
            for e2 in self.engs.values():
                if e2 is e or e2.cnt == 0:
                    continue
                k = id(e2.sem)
                if e.seen.get(k, 0) >= e2.cnt:
                    continue
                e.seen[k] = e2.cnt
                e.ops.append(("wait", e2.sem, e2.cnt))

    def finish(self):
        nc = self.nc
        with nc.Block() as block:
            def runner(eng):
                def f(e):
                    for o in eng.ops:
                        if o[0] == "wait":
                            e.wait_ge(o[1], o[2])
                        else:
                            ins = o[1](e)
                            if o[2] is not None:
                                ins.then_inc(o[2], o[3])
                return f
            block.tensor(runner(self.engs["pe"]))
            block.scalar(runner(self.engs["act"]))
            block.vector(runner(self.engs["dve"]))
            block.gpsimd(runner(self.engs["pool"]))
            block.sync(runner(self.engs["sp"]))


def o_act(out, in_, func, **kw):
    return lambda e: e.activation(out=out, in_=in_, func=func, **kw)


def o_ts(out, in0, s1, s2, op0, op1=None):
    if op1 is None:
        return lambda e: e.tensor_scalar(out=out, in0=in0, scalar1=s1, scalar2=None, op0=op0)
    return lambda e: e.tensor_scalar(out=out, in0=in0, scalar1=s1, scalar2=s2, op0=op0, op1=op1)


def o_tt(out, in0, in1, op):
    return lambda e: e.tensor_tensor(out=out, in0=in0, in1=in1, op=op)


def o_stt(out, in0, scalar, in1, op0, op1):
    return lambda e: e.scalar_tensor_tensor(out=out, in0=in0, scalar=scalar, in1=in1, op0=op0, op1=op1)


def o_copy(out, in_):
    return lambda e: e.tensor_copy(out=out, in_=in_)


def o_recip(out, in_):
    return lambda e: e.reciprocal(out=out, in_=in_)


def o_memset(ap, v):
    return lambda e: e.memset(ap, v)


def o_scan(out, d0, d1, init):
    return lambda e: e.tensor_tensor_scan(out=out, data0=d0, data1=d1, initial=init, op0=ALU.mult, op1=ALU.add)


def o_mm(out, lhsT, rhs, start, stop):
    return lambda e: e.matmul(out, lhsT, rhs, start=start, stop=stop)


def build_program(layers):
    offs, wtot = panel_offsets(layers)
    order = panel_order(layers)
    nc = bass.Bass("TRN2", target_bir_lowering=False)
    xin = nc.dram_tensor("xT", [D, T], F32, kind="ExternalInput").ap()
    memin = nc.dram_tensor("memT", [D, MEM], F32, kind="ExternalInput").ap()
    vecs_d = nc.dram_tensor("vecs", [128, NV], F32, kind="ExternalInput").ap()
    cbf_d = nc.dram_tensor("cbf", [128, NCB], F32, kind="ExternalInput").ap()
    wl_d = {L: nc.dram_tensor(f"w{L}", [128, wtot[L]], F32, kind="ExternalInput").ap() for L in layers}
    out_d = nc.dram_tensor("outT", [D, T], F32, kind="ExternalOutput").ap()

    A_XT = 0
    A_SLOT = A_XT + 8 * T
    A_CBF = A_SLOT + NSLOT * SLOT_W
    A_VEC = A_CBF + NCB // 2
    A_ONEF = A_VEC + NV
    A_SQ = A_ONEF + 512
    A_RS = A_SQ + 2 * 256
    A_SB = A_RS + 2 * 512
    SCR = 26912
    AW = A_SB + SCR

    with ExitStack() as st:
        S = Sched(nc, st)
        arena = st.enter_context(nc.sbuf_tensor("arena", [128, AW], F32))
        psb = [st.enter_context(nc.psum_tensor(f"ps{i}", [128, 512], F32)) for i in range(8)]
        psd = [Dep() for _ in range(8)]
        rr_state = {"g": 0}

        def psum():
            i = rr_state["g"]
            rr_state["g"] = (i + 1) % NG
            return psb[i], psd[i]

        def vf(off, n):
            return arena[:, off:off + n]

        def vb(off, nwords):
            return arena[:, off:off + nwords].bitcast(BF16)

        def XT(c, t0, n):
            return arena[:, A_XT + c * T + t0: A_XT + c * T + t0 + n]

        xd = [[Dep() for _ in range(4)] for _ in range(8)]
        slot_bf = [vb(A_SLOT + s * SLOT_W, SLOT_W) for s in range(NSLOT)]
        slot_dep = [Dep() for _ in range(NSLOT)]
        slot_sem = [S.dsem(f"slot{s}") for s in range(NSLOT)]
        cbf = vb(A_CBF, NCB // 2)
        cbf_dep = Dep()
        vecs = vf(A_VEC, NV)
        vec_dep = Dep()
        onef = vf(A_ONEF, 512)
        onef_dep = Dep()
        sqb = [vb(A_SQ + i * 256, 256) for i in range(2)]
        sqd = [Dep() for _ in range(2)]
        rsb = [vf(A_RS + i * 512, 512) for i in range(2)]
        rsd = [Dep() for _ in range(2)]
        cnt = {"sq": 0, "rs": 0}

        ident = cbf[:, CB_ID:CB_ID + 128]
        negmask = cbf[:, CB_MASK:CB_MASK + 128]
        ones_bf = cbf[:, CB_ONES:CB_ONES + 128]

        def vc(name, idx=0, p0=0, p1=128):
            c = VCOL[name] + idx
            return vecs[p0:p1, c:c + 1]

        _stage = ""
        ws = {"issue": 0, "use": 0}

        def wget(L, key):
            idx = ws["use"]
            assert order[idx][0] == L and order[idx][1] == key, (order[idx], L, key)
            lim = min(len(order), idx + NSLOT)
            while ws["issue"] < lim:
                q = ws["issue"]
                Lq, kq, nq = order[q]
                s = q % NSLOT
                off = offs[(Lq, kq)]
                S.dma("pool", slot_bf[s][:, 0:nq], wl_d[Lq][:, off:off + nq], slot_sem[s], writes=[slot_dep[s]])
                ws["issue"] += 1
            ws["use"] += 1
            return slot_bf[idx % NSLOT], slot_dep[idx % NSLOT]

        def mm(items, reads, wdeps, start=True, stop=True):
            n = len(items)
            for i, (o, l, r) in enumerate(items):
                S.op("pe", o_mm(o, l, r, start and i == 0, stop and i == n - 1),
                     reads=reads if i == 0 else (), writes=wdeps if i == 0 else (), inc=(i == n - 1))

        d_in = S.dsem("in")
        S.dma("sp", vecs, vecs_d, d_in, writes=[vec_dep])
        d_cb = S.dsem("cb")
        S.dma("pool", cbf, cbf_d, d_cb, writes=[cbf_dep])
        d_x = S.dsem("x")
        for c in range(8):
            S.dma("sp", XT(c, 0, T), xin[c * 128:(c + 1) * 128, :], d_x, writes=xd[c])
        for c in range(8):
            for dd in xd[c]:
                dd.w = (d_x[0], d_x[1], None)
        S.op("dve", o_memset(onef, 1.0), writes=[onef_dep])

        def rstd_from(ps, pd, ncol):
            i = cnt["rs"] % 2
            cnt["rs"] += 1
            rs, rd = rsb[i], rsd[i]
            S.op("act", o_act(rs[:, 0:ncol], ps[:, 0:ncol], AF.Ln, scale=1.0 / D, bias=vc("eps")), reads=[pd, vec_dep], writes=[rd])
            S.op("act", o_act(rs[:, 0:ncol], rs[:, 0:ncol], AF.Exp, scale=-0.5), reads=[rd], writes=[rd])
            return rs, rd

        def prenorm(L, gname, T0, hT, hd):
            for tcl in range(2):
                tg = T0 + tcl * TC
                tcg = tg // TC
                ps, pd = psum()
                for c in range(8):
                    i = cnt["sq"] % 2
                    cnt["sq"] += 1
                    S.op("act", o_act(sqb[i], XT(c, tg, TC), AF.Square), reads=[xd[c][tcg]], writes=[sqd[i]])
                    mm([(ps[:, :], ones_bf, sqb[i])], [sqd[i], cbf_dep], [pd], start=(c == 0), stop=(c == 7))
                rs, rd = rstd_from(ps, pd, TC)
                for c in range(8):
                    S.op("dve", o_stt(hT(c, tcl), XT(c, tg, TC), vc(gname, L * 8 + c), rs, ALU.mult, ALU.mult),
                         reads=[xd[c][tcg], rd, vec_dep], writes=[hd[c][tcl]])

        def outproj_postnorm(L, T0, panels, gname, yv, yd):
            st_ps = [(psb[4], psd[4]), (psb[5], psd[5])]
            pending = []

            def flush():
                if _stage in ("op0", "op0c", "op1"):
                    pending.clear()
                while pending:
                    oc_, tcl_, sqi = pending.pop(0)
                    mm([(st_ps[tcl_][0][:, :], ones_bf, sqb[sqi])], [sqd[sqi], cbf_dep], [st_ps[tcl_][1]],
                       start=(oc_ == 0), stop=(oc_ == 7))

            for oc in range(8):
                getp = panels[oc]
                for tcl in range(2):
                    ps, pd = psum()
                    items, reads = getp(tcl, ps)
                    mm(items, reads, [pd])
                    flush()
                    if _stage == "op0":
                        continue
                    S.op("dve", o_copy(yv(oc, tcl), ps[:, :]), reads=[pd], writes=[yd[oc][tcl]])
                    if _stage == "op0c":
                        continue
                    i = cnt["sq"] % 2
                    cnt["sq"] += 1
                    S.op("pool", o_tt(sqb[i], yv(oc, tcl), yv(oc, tcl), ALU.mult), reads=[yd[oc][tcl]], writes=[sqd[i]])
                    pending.append((oc, tcl, i))
            flush()
            if _stage in ("op0", "op0c", "op1", "op2"):
                return
            for tcl in range(2):
                tg = T0 + tcl * TC
                tcg = tg // TC
                rs, rd = rstd_from(st_ps[tcl][0], st_ps[tcl][1], TC)
                for c in range(8):
                    S.op("dve", o_tt(yv(c, tcl), yv(c, tcl), rs, ALU.mult), reads=[yd[c][tcl], rd], writes=[yd[c][tcl]])
                    S.op("dve", o_stt(XT(c, tg, TC), yv(c, tcl), vc(gname, L * 8 + c), XT(c, tg, TC), ALU.mult, ALU.add),
                         reads=[yd[c][tcl], vec_dep, xd[c][tcg]], writes=[xd[c][tcg]])

        def std_panels(L, kname, src, srcd):
            cache = {}

            def mk(oc):
                def getp(tcl, ps):
                    j, ol = oc // 4, oc % 4
                    if (j) not in cache:
                        cache.clear()
                        cache[j] = wget(L, (kname, j))
                    w, wd = cache[j]
                    items = [(ps[:, :], w[:, kc * 512 + ol * 128: kc * 512 + (ol + 1) * 128], src(kc, tcl)) for kc in range(8)]
                    return items, [wd] + [srcd[kc][tcl] for kc in range(8)]
                return getp
            return [mk(oc) for oc in range(8)]

        def proj_fm(w, wd, coff, ncols_panel, hT, hd, tcl, ps, m=128):
            items = [(ps[0:m, :], w[:, kc * ncols_panel + coff: kc * ncols_panel + coff + m], hT(kc, tcl)) for kc in range(8)]
            return items, [wd] + [hd[kc][tcl] for kc in range(8)]

        def even_mixer(L):
            e = L // 2
            SB = A_SB
            kTv = vb(SB + 0, 4096)
            Vv = vb(SB + 4096, 4096)
            Nf = vf(SB + 8192, 2048)
            halo = vf(SB + 10240, 8)
            negbf = vf(SB + 10248, 2)
            biasT = [vf(SB + 10256 + i * 128, 128) for i in range(2)]
            hTv = vb(SB + 10512, 4096)
            qTv = vb(SB + 14608, 2048)
            cu = vf(SB + 16656, 1028)
            Cc = vf(SB + 17684, 512)
            tcv = vf(SB + 18196, 512)
            yv_ = vf(SB + 10512, 8192)
            yTv = vb(SB + 18708, 4096)
            PT = [vb(SB + 22804 + i * 256, 256) for i in range(2)]
            rcp = [vf(SB + 23316 + i * 512, 512) for i in range(2)]
            tmp = [vf(SB + 24340 + i * 512, 512) for i in range(4)]
            cq = [vb(SB + 26388 + i * 256, 256) for i in range(2)]
            kd = [[Dep() for _ in range(4)] for _ in range(4)]
            vd = [Dep() for _ in range(16)]
            Nd = [Dep() for _ in range(4)]
            halod = [Dep() for _ in range(4)]
            negbfd = Dep()
            biasTd = [Dep(), Dep()]
            hd = [[Dep() for _ in range(2)] for _ in range(8)]
            qd = [[Dep() for _ in range(2)] for _ in range(4)]
            cud, Ccd, tcd = Dep(), Dep(), Dep()
            yd = [[Dep() for _ in range(2)] for _ in range(8)]
            yTd = [[Dep() for _ in range(2)] for _ in range(8)]
            PTd = [Dep(), Dep()]
            rcpd = [Dep(), Dep()]
            tmpd = [Dep() for _ in range(4)]
            cqd = [Dep(), Dep()]

            def hT(kc, tcl):
                return hTv[:, kc * NT + tcl * TC: kc * NT + (tcl + 1) * TC]

            def qT(i, tcl):
                return qTv[:, i * NT + tcl * TC: i * NT + (tcl + 1) * TC]

            def kT(i, t0, n):
                return kTv[:, i * T + t0: i * T + t0 + n]

            def Vb(tb):
                return Vv[:, tb * 512:(tb + 1) * 512]

            def yT(c, tcl):
                return yTv[:, c * NT + tcl * TC: c * NT + (tcl + 1) * TC]

            def yv(c, tcl):
                return yv_[:, c * NT + tcl * TC: c * NT + (tcl + 1) * TC]

            S.barrier()
            S.op("dve", o_ts(negbf[0:8, :], vecs[0:8, VCOL["bf"]:VCOL["bf"] + 2], -1.0, None, ALU.mult),
                 reads=[vec_dep], writes=[negbfd])
            for i in range(2):
                S.op("pool", o_memset(cq[i], 0.0), writes=[cqd[i]])

            for hf in range(2):
                T0 = hf * NT
                if hf == 1:
                    S.barrier()
                prenorm(L, "g_mix_pre", T0, hT, hd)
                if _stage == "pre":
                    return
                w, wd = wget(L, ("abq",))
                for i in range(4):
                    for tcl in range(2):
                        ps, pd = psum()
                        items, reads = proj_fm(w, wd, i * 128, 512, hT, hd, tcl, ps)
                        mm(items, reads, [pd])
                        S.op("act", o_act(qT(i, tcl), ps[:, :], AF.Copy), reads=[pd], writes=[qd[i][tcl]])
                w, wd = wget(L, ("abk",))
                for i in range(4):
                    for tcl in range(2):
                        ps, pd = psum()
                        items, reads = proj_fm(w, wd, i * 128, 512, hT, hd, tcl, ps)
                        mm(items, reads, [pd])
                        S.op("dve", o_copy(kT(i, T0 + tcl * TC, TC), ps[:, :]), reads=[pd], writes=[kd[i][hf * 2 + tcl]])
                w, wd = wget(L, ("abv",))
                for tb in range(8):
                    tcl = tb // 4
                    ps, pd = psum()
                    items = [(ps[:, :], hT(kc, tcl)[:, (tb % 4) * 128:(tb % 4 + 1) * 128], w[:, kc * 512:(kc + 1) * 512]) for kc in range(8)]
                    mm(items, [wd] + [hd[kc][tcl] for kc in range(8)], [pd])
                    eng = "act" if tb % 2 == 0 else "dve"
                    if eng == "act":
                        S.op("act", o_act(Vb(hf * 8 + tb), ps[:, :], AF.Copy), reads=[pd], writes=[vd[hf * 8 + tb]])
                    else:
                        S.op("dve", o_copy(Vb(hf * 8 + tb), ps[:, :]), reads=[pd], writes=[vd[hf * 8 + tb]])
                for i in range(4):
                    w, wd = wget(L, ("abc", i))
                    if i == 0:
                        for tcl in range(2):
                            cg = hf * 2 + tcl
                            tg = T0 + tcl * TC
                            ps, pd = psum()
                            items, reads = proj_fm(w, wd, 384, 392, hT, hd, tcl, ps, m=8)
                            mm(items, reads, [pd])
                            tA, tB = tmp[0], tmp[1]
                            S.op("act", o_act(tA[0:8, :], ps[0:8, :], AF.Exp, scale=-1.0, bias=negbf[0:8, e:e + 1]),
                                 reads=[pd, negbfd], writes=[tmpd[0]])
                            S.op("act", o_act(tB[0:8, :], tA[0:8, :], AF.Ln, scale=1.0, bias=vc("one", 0, 0, 8)),
                                 reads=[tmpd[0], vec_dep], writes=[tmpd[1]])
                            init = Nf[0:8, tg - 1:tg] if cg > 0 else 0.0
                            S.op("dve", o_scan(Nf[0:8, tg:tg + TC], onef[0:8, :], tB[0:8, :], init),
                                 reads=[tmpd[1], onef_dep] + ([Nd[cg - 1]] if cg > 0 else []), writes=[Nd[cg]])
                    if hf == 0:
                        S.op("dve", o_memset(cu[:, 0:2], 0.0), writes=[cud])
                    else:
                        S.op("dve", o_copy(cu[:, 0:2], halo[:, 2 * i:2 * i + 2]), reads=[halod[i]], writes=[cud])
                    for tcl in range(2):
                        psB, pdB = psum()
                        items, reads = proj_fm(w, wd, 0, 392, hT, hd, tcl, psB)
                        mm(items, reads, [pdB])
                        psC, pdC = psum()
                        items, reads = proj_fm(w, wd, 128, 392, hT, hd, tcl, psC)
                        mm(items, reads, [pdC])
                        psU, pdU = psum()
                        items, reads = proj_fm(w, wd, 256, 392, hT, hd, tcl, psU)
                        mm(items, reads, [pdU])
                        S.op("act", o_act(Cc, psC[:, :], AF.Copy), reads=[pdC], writes=[Ccd])
                        b0 = tcl * TC
                        S.op("dve", o_tt(cu[:, 2 + b0:2 + b0 + TC], psU[:, :], Cc, ALU.mult), reads=[pdU, Ccd], writes=[cud])
                        cw = VCOL["abcw"] + e * 12 + i
                        S.op("dve", o_ts(tcv, cu[:, b0:b0 + TC], vecs[:, cw:cw + 1], None, ALU.mult),
                             reads=[cud, vec_dep], writes=[tcd])
                        S.op("dve", o_stt(tcv, cu[:, b0 + 1:b0 + 1 + TC], vecs[:, cw + 4:cw + 5], tcv, ALU.mult, ALU.add),
                             reads=[cud, tcd], writes=[tcd])
                        S.op("dve", o_stt(tcv, cu[:, b0 + 2:b0 + 2 + TC], vecs[:, cw + 8:cw + 9], tcv, ALU.mult, ALU.add),
                             reads=[cud, tcd], writes=[tcd])
                        S.op("dve", o_tt(yT(4 + i, tcl), psB[:, :], tcv, ALU.mult), reads=[pdB, tcd], writes=[yTd[4 + i][tcl]])
                    if hf == 0:
                        S.op("dve", o_copy(halo[:, 2 * i:2 * i + 2], cu[:, NT:NT + 2]), reads=[cud], writes=[halod[i]])

                if _stage == "proj":
                    return
                for tcl in range(2):
                    cg = hf * 2 + tcl
                    tg = T0 + tcl * TC
                    nkb = 4 * cg + 4
                    Rcol = Nf[0:8, tg + 255:tg + 256]
                    Rdeps = [Nd[cg]]
                    cqb, cqbd = cq[cg % 2], cqd[cg % 2]
                    bT, bTd = biasT[cg % 2], biasTd[cg % 2]
                    psT, pdT = psum()
                    for seg in range(cg + 1):
                        tb_, tbd_ = tmp[2 + seg % 2], tmpd[2 + seg % 2]
                        S.op("dve", o_ts(tb_[0:8, :], Nf[0:8, seg * TC:(seg + 1) * TC], Rcol, None, ALU.subtract),
                             reads=[Nd[seg]] + Rdeps, writes=[tbd_])
                        for jb in range(4):
                            j = seg * 4 + jb
                            mm([(psT[:, j * 8:(j + 1) * 8], tb_[0:8, jb * 128:(jb + 1) * 128], vecs[0:8, VCOL["id8"]:VCOL["id8"] + 8])],
                               [tbd_, vec_dep], [pdT])
                    S.op("dve", o_copy(bT[:, 0:nkb * 8], psT[:, 0:nkb * 8]), reads=[pdT], writes=[bTd])

                    if _stage == "attn_prep":
                        continue
                    items_l = [(h, j) for h in range(8) for j in range(nkb)]
                    state = {}

                    def qk(idx):
                        h, j = items_l[idx]
                        i, hp = h // 2, h % 2
                        jj = j - 4 * cg
                        c0 = 0 if jj <= 0 else jj * 128
                        ps, pd = psum()
                        lo, hi = hp * 64, hp * 64 + 64
                        its = [(ps[:, c0:TC], kT(i, j * 128, 128)[lo:hi, :], qT(i, tcl)[lo:hi, c0:TC])]
                        if jj >= 0 and _stage not in ("attn_qk1", "attn_qk2"):
                            its.append((ps[:, c0:c0 + 128], ident, negmask))
                        mm(its, [kd[i][j // 4], qd[i][tcl], cbf_dep], [pd])
                        pb = idx % 2
                        S.op("act", o_act(PT[pb][:, c0:TC], ps[:, c0:TC], AF.Exp, scale=0.125, bias=bT[:, j * 8 + h:j * 8 + h + 1]),
                             reads=[pd, bTd], writes=[PTd[pb]])
                        state[idx] = (pb, c0)

                    def pv(idx):
                        h, j = items_l[idx]
                        i, hp = h // 2, h % 2
                        pb, c0 = state.pop(idx)
                        so = 4 + 2 * (h % 2)
                        pso, psod, psn, psnd = psb[so], psd[so], psb[so + 1], psd[so + 1]
                        mm([(pso[:, c0:TC], Vb(j)[:, i * 128:(i + 1) * 128], PT[pb][:, c0:TC])], [vd[j], PTd[pb]], [psod],
                           start=(j == 0), stop=(j == nkb - 1))
                        mm([(psn[:, c0:TC], ones_bf, PT[pb][:, c0:TC])], [PTd[pb], cbf_dep], [psnd],
                           start=(j == 0), stop=(j == nkb - 1))
                        if j == nkb - 1:
                            rb = h % 2
                            lo, hi = hp * 64, hp * 64 + 64
                            S.op("dve", o_recip(rcp[rb][lo:hi, :], psn[lo:hi, :]), reads=[psnd], writes=[rcpd[rb]])
                            S.op("dve", o_tt(yT(i, tcl)[lo:hi, :], pso[lo:hi, :], rcp[rb][lo:hi, :], ALU.mult),
                                 reads=[psod, rcpd[rb]], writes=[yTd[i][tcl]])

                    n_it = len(items_l)
                    for idx in range(n_it + 1):
                        if idx < n_it:
                            qk(idx)
                        if idx >= 1 and not _stage.startswith("attn_qk"):
                            pv(idx - 1)

                if _stage in ("attn", "attn_prep", "attn_qk", "attn_qk1", "attn_qk2"):
                    return
                S.barrier()
                if _stage == "mixop":
                    return
                outproj_postnorm(L, T0, std_panels(L, "abo", yT, yTd), "g_mix_post", yv, yd)
                if _stage in ("mix0", "op0", "op0c", "op1", "op2"):
                    return

        def odd_mixer(L):
            o = L // 2
            SB = A_SB
            halo = vf(SB + 0, 24)
            carry = vf(SB + 32, 8)
            sc1 = vf(SB + 40, 8)
            sc2 = vf(SB + 48, 8)
            spt = vf(SB + 56, 8)
            hTv = vb(SB + 128, 4096)
            ubuf = [vf(SB + 4224 + i * 1028, 1028) for i in range(2)]
            uc = [vf(SB + 6280 + i * 1024, 1024) for i in range(2)]
            yv_ = vf(SB + 128, 8192)
            ggv = vb(SB + 8328, 4096)
            yTv = vb(SB + 12424, 4096)
            ucb = [vb(SB + 16520 + i * 512, 512) for i in range(2)]
            rr0, ii0, aa, ss, bb, hs, rr1, ii1 = [vf(SB + 17544 + i * 1024, 1024) for i in range(8)]
            rrs, iis = [rr0, rr1], [ii0, ii1]
            halod = [Dep() for _ in range(8)]
            carryd = [Dep() for _ in range(8)]
            scd = Dep()
            hd = [[Dep() for _ in range(2)] for _ in range(8)]
            ubd = [Dep(), Dep()]
            ucd = [Dep(), Dep()]
            ucbd = [Dep(), Dep()]
            ggd = [[Dep() for _ in range(2)] for _ in range(8)]
            yTd = [[Dep() for _ in range(2)] for _ in range(8)]
            yd = [[Dep() for _ in range(2)] for _ in range(8)]
            aad, ssd, bbd, hsd = [Dep() for _ in range(4)]
            rrds, iids = [Dep(), Dep()], [Dep(), Dep()]

            def hT(kc, tcl):
                return hTv[:, kc * NT + tcl * TC: kc * NT + (tcl + 1) * TC]

            def gg(c, tcl):
                return ggv[:, c * NT + tcl * TC: c * NT + (tcl + 1) * TC]

            def yT(c, tcl):
                return yTv[:, c * NT + tcl * TC: c * NT + (tcl + 1) * TC]

            def yv(c, tcl):
                return yv_[:, c * NT + tcl * TC: c * NT + (tcl + 1) * TC]

            S.barrier()
            lam = vecs[:, VCOL["clam"] + o * 8: VCOL["clam"] + o * 8 + 8]
            S.op("act", o_act(spt, lam, AF.Exp, scale=-1.0), reads=[vec_dep], writes=[scd])
            S.op("act", o_act(spt, spt, AF.Ln, scale=1.0, bias=vc("one")), reads=[scd, vec_dep], writes=[scd])
            S.op("dve", o_ts(sc1, spt, -8.0, None, ALU.mult), reads=[scd], writes=[scd])
            S.op("dve", o_ts(sc2, spt, -16.0, None, ALU.mult), reads=[scd], writes=[scd])

            for hf in range(2):
                T0 = hf * NT
                if hf == 1:
                    S.barrier()
                prenorm(L, "g_mix_pre", T0, hT, hd)
                def in_proj(n):
                    w, wd = wget(L, ("cin", n))
                    for ch in range(2):
                        c = 2 * n + ch
                        if hf == 0:
                            S.op("pool", o_memset(ubuf[ch][:, 0:3], 0.0), writes=[ubd[ch]])
                        else:
                            S.op("pool", o_copy(ubuf[ch][:, 0:3], halo[:, 3 * c:3 * c + 3]), reads=[halod[c]], writes=[ubd[ch]])
                        for tcl in range(2):
                            ps, pd = psum()
                            items, reads = proj_fm(w, wd, ch * 128, 512, hT, hd, tcl, ps)
                            mm(items, reads, [pd])
                            S.op("act", o_act(gg(c, tcl), ps[:, :], AF.Gelu_apprx_tanh), reads=[pd], writes=[ggd[c][tcl]])
                            ps, pd = psum()
                            items, reads = proj_fm(w, wd, 256 + ch * 128, 512, hT, hd, tcl, ps)
                            mm(items, reads, [pd])
                            S.op("dve", o_copy(ubuf[ch][:, 3 + tcl * TC:3 + (tcl + 1) * TC], ps[:, :]), reads=[pd], writes=[ubd[ch]])

                def conv(n):
                    for ch in range(2):
                        c = 2 * n + ch
                        cw = VCOL["ccw"] + o * 32 + c
                        cbias = vc("ccb", o * 8 + c)
                        S.op("act", o_act(uc[ch], ubuf[ch][:, 0:NT], AF.Identity, scale=vecs[:, cw:cw + 1], bias=cbias),
                             reads=[ubd[ch], vec_dep], writes=[ucd[ch]])
                        for k in range(1, 4):
                            S.op("dve", o_stt(uc[ch], ubuf[ch][:, k:k + NT], vecs[:, cw + 8 * k:cw + 8 * k + 1], uc[ch], ALU.mult, ALU.add),
                                 reads=[ubd[ch], ucd[ch]], writes=[ucd[ch]])
                        if hf == 0:
                            S.op("pool", o_copy(halo[:, 3 * c:3 * c + 3], ubuf[ch][:, NT:NT + 3]), reads=[ubd[ch]], writes=[halod[c]])
                        S.op("pool", o_copy(ucb[ch], uc[ch]), reads=[ucd[ch]], writes=[ucbd[ch]])

                def gates(n):
                    wg, wgd = wget(L, ("cg", n))
                    for dch in range(2):
                        c = 2 * n + dch
                        rr, ii, rrd, iid = rrs[dch], iis[dch], rrds[dch], iids[dch]
                        for tcl in range(2):
                            ps, pd = psum()
                            its = [(ps[:, :], wg[:, cc * 256 + dch * 128: cc * 256 + (dch + 1) * 128], ucb[cc][:, tcl * TC:(tcl + 1) * TC]) for cc in range(2)]
                            mm(its, [wgd, ucbd[0], ucbd[1]], [pd])
                            S.op("act", o_act(rr[:, tcl * TC:(tcl + 1) * TC], ps[:, :], AF.Sigmoid, scale=1.0, bias=vc("cba", o * 8 + c)),
                                 reads=[pd, vec_dep], writes=[rrd])
                            ps, pd = psum()
                            its = [(ps[:, :], wg[:, 512 + cc * 256 + dch * 128: 512 + cc * 256 + (dch + 1) * 128], ucb[cc][:, tcl * TC:(tcl + 1) * TC]) for cc in range(2)]
                            mm(its, [wgd, ucbd[0], ucbd[1]], [pd])
                            S.op("act", o_act(ii[:, tcl * TC:(tcl + 1) * TC], ps[:, :], AF.Sigmoid, scale=1.0, bias=vc("cbi", o * 8 + c)),
                                 reads=[pd, vec_dep], writes=[iid])
                        S.op("act", o_act(aa, rr, AF.Exp, scale=sc1[:, c:c + 1]), reads=[rrd, scd], writes=[aad])
                        S.op("act", o_act(ss, rr, AF.Exp, scale=sc2[:, c:c + 1]), reads=[rrd, scd], writes=[ssd])
                        S.op("act", o_act(ss, ss, AF.Sqrt, scale=-1.0, bias=vc("one")), reads=[ssd, vec_dep], writes=[ssd])
                        S.op("pool", o_tt(bb, ii, uc[dch], ALU.mult), reads=[iid, ucd[dch]], writes=[bbd])
                        S.op("dve", o_tt(bb, bb, ss, ALU.mult), reads=[bbd, ssd], writes=[bbd])
                        init = carry[:, c:c + 1] if hf == 1 else 0.0
                        S.op("dve", o_scan(hs, aa, bb, init), reads=[aad, bbd] + ([carryd[c]] if hf == 1 else []), writes=[hsd])
                        if hf == 0:
                            S.op("dve", o_copy(carry[:, c:c + 1], hs[:, NT - 1:NT]), reads=[hsd], writes=[carryd[c]])
                        for tcl in range(2):
                            S.op("dve", o_tt(yT(c, tcl), gg(c, tcl), hs[:, tcl * TC:(tcl + 1) * TC], ALU.mult),
                                 reads=[ggd[c][tcl], hsd], writes=[yTd[c][tcl]])

                in_proj(0)
                conv(0)
                for n in range(4):
                    if n < 3:
                        in_proj(n + 1)
                    gates(n)
                    if n < 3:
                        conv(n + 1)
                S.barrier()
                outproj_postnorm(L, T0, std_panels(L, "co", yT, yTd), "g_mix_post", yv, yd)

        def cross(L, dmem):
            SB = A_SB
            kxv = vb(SB + 0, 1024)
            Vxv = vb(SB + 1024, 1024)
            memv = vf(SB + 2048, 2048)
            mTv = vb(SB + 4096, 1024)
            hTv = vb(SB + 2048, 4096)
            qxv = vb(SB + 6144, 4096)
            yv_ = vf(SB + 2048, 8192)
            oTv = vb(SB + 10240, 4096)
            PT = [vb(SB + 14336 + i * 512, 512) for i in range(2)]
            rcp = [vf(SB + 15360 + i * 512, 512) for i in range(2)]
            kxd = [Dep() for _ in range(8)]
            Vxd = [[Dep() for _ in range(2)] for _ in range(2)]
            memd = [Dep() for _ in range(8)]
            mTd = [Dep() for _ in range(8)]
            hd = [[Dep() for _ in range(2)] for _ in range(8)]
            qxd = [[Dep() for _ in range(2)] for _ in range(8)]
            oTd = [[Dep() for _ in range(2)] for _ in range(8)]
            yd = [[Dep() for _ in range(2)] for _ in range(8)]
            PTd = [Dep(), Dep()]
            rcpd = [Dep(), Dep()]

            def hT(kc, tcl):
                return hTv[:, kc * NT + tcl * TC: kc * NT + (tcl + 1) * TC]

            def qx(c, tcl):
                return qxv[:, c * NT + tcl * TC: c * NT + (tcl + 1) * TC]

            def oT(c, tcl):
                return oTv[:, c * NT + tcl * TC: c * NT + (tcl + 1) * TC]

            def yv(c, tcl):
                return yv_[:, c * NT + tcl * TC: c * NT + (tcl + 1) * TC]

            def memc(c):
                return memv[:, c * MEM:(c + 1) * MEM]

            def mT(c):
                return mTv[:, c * MEM:(c + 1) * MEM]

            def kx(c):
                return kxv[:, c * MEM:(c + 1) * MEM]

            S.barrier()
            for c in range(8):
                S.dma("sp", memc(c), memin[c * 128:(c + 1) * 128, :], dmem, writes=[memd[c]])
            for c in range(8):
                memd[c].w = (dmem[0], dmem[1], None)
            ps, pd = psum()
            for c in range(8):
                i = cnt["sq"] % 2
                cnt["sq"] += 1
                S.op("act", o_act(sqb[i][:, 0:MEM], memc(c), AF.Square), reads=[memd[c]], writes=[sqd[i]])
                mm([(ps[:, 0:MEM], ones_bf, sqb[i][:, 0:MEM])], [sqd[i], cbf_dep], [pd], start=(c == 0), stop=(c == 7))
            rs, rd = rstd_from(ps, pd, MEM)
            for c in range(8):
                S.op("dve", o_stt(mT(c), memc(c), vc("g_mem", L * 8 + c), rs[:, 0:MEM], ALU.mult, ALU.mult),
                     reads=[memd[c], rd, vec_dep], writes=[mTd[c]])
            for j in range(2):
                w, wd = wget(L, ("xk", j))
                for cl in range(4):
                    c = 4 * j + cl
                    ps, pd = psum()
                    its = [(ps[:, 0:MEM], w[:, kc * 512 + cl * 128: kc * 512 + (cl + 1) * 128], mT(kc)) for kc in range(8)]
                    mm(its, [wd] + mTd, [pd])
                    S.op("act", o_act(kx(c), ps[:, 0:MEM], AF.Copy), reads=[pd], writes=[kxd[c]])
            for j in range(2):
                w, wd = wget(L, ("xv", j))
                for blk in range(2):
                    ps, pd = psum()
                    its = [(ps[:, :], mT(kc)[:, blk * 128:(blk + 1) * 128], w[:, kc * 512:(kc + 1) * 512]) for kc in range(8)]
                    mm(its, [wd] + mTd, [pd])
                    S.op("dve", o_copy(Vxv[:, blk * D + j * 512: blk * D + (j + 1) * 512], ps[:, :]), reads=[pd], writes=[Vxd[blk][j]])

            for hf in range(2):
                T0 = hf * NT
                S.barrier()
                prenorm(L, "g_cross_pre", T0, hT, hd)
                for j in range(2):
                    w, wd = wget(L, ("xq", j))
                    for cl in range(4):
                        c = 4 * j + cl
                        for tcl in range(2):
                            ps, pd = psum()
                            items, reads = proj_fm(w, wd, cl * 128, 512, hT, hd, tcl, ps)
                            mm(items, reads, [pd])
                            if (cl + tcl) % 2 == 0:
                                S.op("act", o_act(qx(c, tcl), ps[:, :], AF.Copy), reads=[pd], writes=[qxd[c][tcl]])
                            else:
                                S.op("dve", o_copy(qx(c, tcl), ps[:, :]), reads=[pd], writes=[qxd[c][tcl]])
                items_l = [(tcl, h) for tcl in range(2) for h in range(4)]

                def qk(idx):
                    tcl, h = items_l[idx]
                    pb = idx % 2
                    for kb in range(2):
                        ps, pd = psum()
                        its = [(ps[:, :], kx(2 * h + dc)[:, kb * 128:(kb + 1) * 128], qx(2 * h + dc, tcl)) for dc in range(2)]
                        mm(its, [kxd[2 * h], kxd[2 * h + 1], qxd[2 * h][tcl], qxd[2 * h + 1][tcl]], [pd])
                        S.op("act", o_act(PT[pb][:, kb * TC:(kb + 1) * TC], ps[:, :], AF.Exp, scale=1.0 / 16.0),
                             reads=[pd], writes=[PTd[pb]])

                def pv(idx):
                    tcl, h = items_l[idx]
                    pb = idx % 2
                    so = 4 + 2 * (idx % 2)
                    accs = []
                    for dc in range(2):
                        po, pod = psb[so + dc], psd[so + dc]
                        its = [(po[:, :], Vxv[:, kb * D + h * 256 + dc * 128: kb * D + h * 256 + (dc + 1) * 128], PT[pb][:, kb * TC:(kb + 1) * TC]) for kb in range(2)]
                        mm(its, [Vxd[0][h // 2], Vxd[1][h // 2], PTd[pb]], [pod])
                        accs.append((po, pod))
                    pn, pnd = psum()
                    its = [(pn[:, :], ones_bf, PT[pb][:, kb * TC:(kb + 1) * TC]) for kb in range(2)]
                    mm(its, [PTd[pb], cbf_dep], [pnd])
                    S.op("act", o_act(rcp[pb], pn[:, :], AF.Ln), reads=[pnd], writes=[rcpd[pb]])
                    S.op("act", o_act(rcp[pb], rcp[pb], AF.Exp, scale=-1.0), reads=[rcpd[pb]], writes=[rcpd[pb]])
                    for dc in range(2):
                        po, pod = accs[dc]
                        S.op("dve", o_tt(oT(2 * h + dc, tcl), po[:, :], rcp[pb], ALU.mult), reads=[pod, rcpd[pb]], writes=[oTd[2 * h + dc][tcl]])

                for idx in range(len(items_l) + 1):
                    if idx < len(items_l):
                        qk(idx)
                    if idx >= 1:
                        pv(idx - 1)
                S.barrier()
                outproj_postnorm(L, T0, std_panels(L, "xo", oT, oTd), "g_cross_post", yv, yd)

        def ffn(L):
            SB = A_SB
            hTv = vb(SB + 0, 4096)
            actv = vb(SB + 4096, NFF * NT // 2)
            sg = [vf(SB + 15360 + i * 512, 512) for i in range(2)]
            yv_ = vf(SB + 16384, 8192)
            hd = [[Dep() for _ in range(2)] for _ in range(8)]
            actd = [[Dep() for _ in range(2)] for _ in range(NFF)]
            sgd = [Dep(), Dep()]
            yd = [[Dep() for _ in range(2)] for _ in range(8)]

            def hT(kc, tcl):
                return hTv[:, kc * NT + tcl * TC: kc * NT + (tcl + 1) * TC]

            def act(f, tcl):
                return actv[:, f * NT + tcl * TC: f * NT + (tcl + 1) * TC]

            def yv(c, tcl):
                return yv_[:, c * NT + tcl * TC: c * NT + (tcl + 1) * TC]

            S.barrier()
            k = 0
            for hf in range(2):
                T0 = hf * NT
                prenorm(L, "g_ffn_pre", T0, hT, hd)
                for f in range(NFF):
                    w, wd = wget(L, ("gu", f))
                    for tcl in range(2):
                        psg, pdg = psum()
                        items, reads = proj_fm(w, wd, 0, 256, hT, hd, tcl, psg)
                        mm(items, reads, [pdg])
                        psu, pdu = psum()
                        items, reads = proj_fm(w, wd, 128, 256, hT, hd, tcl, psu)
                        mm(items, reads, [pdu])
                        b = k % 2
                        k += 1
                        S.op("act", o_act(sg[b], psg[:, :], AF.Silu), reads=[pdg], writes=[sgd[b]])
                        S.op("dve", o_tt(act(f, tcl), psu[:, :], sg[b], ALU.mult), reads=[pdu, sgd[b]], writes=[actd[f][tcl]])

                def mk(oc):
                    def getp(tcl, ps, _c={}):
                        if "w" not in _c:
                            _c["w"] = wget(L, ("dn", oc))
                        w, wd = _c["w"]
                        items = [(ps[:, :], w[:, fc * 128:(fc + 1) * 128], act(fc, tcl)) for fc in range(NFF)]
                        return items, [wd] + [actd[fc][tcl] for fc in range(NFF)]
                    return getp
                outproj_postnorm(L, T0, [mk(oc) for oc in range(8)], "g_ffn_post", yv, yd)

        dmem = S.dsem("mem")
        for li, L in enumerate(layers if _stage != "io" else []):
            if li > 0:
                S.barrier()
                S.new_epoch()
            if L % 2 == 0:
                even_mixer(L)
            else:
                odd_mixer(L)
            if _stage in ("pre", "proj", "attn", "mix", "attn_prep", "attn_qk", "attn_qk1", "attn_qk2", "mix0", "mixop", "op0", "op0c", "op1", "op2"):
                break
            cross(L, dmem)
            if _stage == "cross":
                break
            ffn(L)
        assert _stage != "" or ws["use"] == len(order)

        d_out = S.dsem("out")
        for c in range(8):
            S.dma("sp", out_d[c * 128:(c + 1) * 128, :], XT(c, 0, T), d_out, reads=xd[c])
        S.wait_tok("sp", d_out[0], d_out[1])
        S.finish()
    return nc


def _kp(Wm):
    K, n = Wm.shape
    return np.ascontiguousarray(Wm.reshape(K // 128, 128, n).transpose(1, 0, 2)).reshape(128, -1)


def _panel(inp, L, key):
    e = L // 2
    o = L // 2
    k0 = key[0]
    if k0 == "abq":
        return _kp(inp["ab_w_in"][e][:, 0:512])
    if k0 == "abk":
        return _kp(inp["ab_w_in"][e][:, 512:1024])
    if k0 == "abv":
        return _kp(inp["ab_w_in"][e][:, 1024:1536])
    if k0 == "abc":
        i = key[1]
        Wm = inp["ab_w_in"][e]
        cat = np.concatenate([Wm[:, 1544 + i * 128:1544 + (i + 1) * 128], Wm[:, 2056 + i * 128:2056 + (i + 1) * 128],
                              Wm[:, 2568 + i * 128:2568 + (i + 1) * 128], Wm[:, 1536:1544]], axis=1)
        return _kp(cat)
    if k0 == "abo":
        j = key[1]
        return _kp(inp["ab_w_out"][e][:, j * 512:(j + 1) * 512])
    if k0 == "cin":
        n = key[1]
        Wm = inp["c_w_in"][o]
        cat = np.concatenate([Wm[:, n * 256:(n + 1) * 256], Wm[:, 1024 + n * 256:1024 + (n + 1) * 256]], axis=1)
        return _kp(cat)
    if k0 == "cg":
        n = key[1]
        return np.concatenate([_kp(inp["c_w_a"][o][n]), _kp(inp["c_w_i"][o][n])], axis=1)
    if k0 == "co":
        j = key[1]
        return _kp(inp["c_w_out"][o][:, j * 512:(j + 1) * 512])
    if k0 == "xk":
        j = key[1]
        return _kp(inp["w_xkv"][L][:, j * 512:(j + 1) * 512])
    if k0 == "xv":
        j = key[1]
        return _kp(inp["w_xkv"][L][:, 1024 + j * 512:1024 + (j + 1) * 512])
    if k0 == "xq":
        j = key[1]
        return _kp(inp["w_xq"][L][:, j * 512:(j + 1) * 512])
    if k0 == "xo":
        j = key[1]
        return _kp(inp["w_xo"][L][:, j * 512:(j + 1) * 512])
    if k0 == "gu":
        f = key[1]
        Wm = inp["w_ffn_gu"][L]
        cat = np.concatenate([Wm[:, f * 128:(f + 1) * 128], Wm[:, DFF + f * 128:DFF + (f + 1) * 128]], axis=1)
        return _kp(cat)
    if k0 == "dn":
        oc = key[1]
        return _kp(inp["w_ffn_down"][L][:, oc * 128:(oc + 1) * 128])
    raise KeyError(key)


def pack_weights(inp, layers):
    offs, wtot = panel_offsets(layers)
    arrs = {L: np.empty((128, wtot[L]), np.float32) for L in layers}
    seen = set()
    for (L, key, n) in panel_order(layers):
        if (L, key) in seen:
            continue
        seen.add((L, key))
        p = _panel(inp, L, key)
        assert p.shape == (128, n), (key, p.shape, n)
        arrs[L][:, offs[(L, key)]:offs[(L, key)] + n] = p
    return arrs


def pack_vecs(inp):
    v = np.zeros((128, NV), np.float32)

    def fm(a):
        return np.asarray(a, np.float32).reshape(-1, 128).T

    for n in ("g_mix_pre", "g_mix_post", "g_cross_pre", "g_mem", "g_cross_post", "g_ffn_pre", "g_ffn_post"):
        for L in range(DEPTH):
            v[:, VCOL[n] + L * 8: VCOL[n] + L * 8 + 8] = fm(inp[n][L])
    for e in range(2):
        for k in range(3):
            v[:, VCOL["abcw"] + e * 12 + k * 4: VCOL["abcw"] + e * 12 + k * 4 + 4] = fm(inp["ab_conv_w"][e, k])
    for o in range(2):
        for k in range(4):
            v[:, VCOL["ccw"] + o * 32 + k * 8: VCOL["ccw"] + o * 32 + k * 8 + 8] = fm(inp["c_conv_w"][o, k])
        v[:, VCOL["ccb"] + o * 8: VCOL["ccb"] + o * 8 + 8] = fm(inp["c_conv_b"][o])
        v[:, VCOL["cba"] + o * 8: VCOL["cba"] + o * 8 + 8] = fm(np.asarray(inp["c_b_a"][o]).reshape(-1))
        v[:, VCOL["cbi"] + o * 8: VCOL["cbi"] + o * 8 + 8] = fm(np.asarray(inp["c_b_i"][o]).reshape(-1))
        v[:, VCOL["clam"] + o * 8: VCOL["clam"] + o * 8 + 8] = fm(inp["c_lam"][o])
    v[0:8, VCOL["bf"]:VCOL["bf"] + 2] = np.asarray(inp["ab_b_f"], np.float32).T
    v[:, VCOL["one"]] = 1.0
    v[:, VCOL["eps"]] = EPS
    v[0:8, VCOL["id8"]:VCOL["id8"] + 8] = np.eye(8, dtype=np.float32)
    return v


def pack_consts():
    c = np.zeros((128, NCB), np.float32)
    c[:, CB_ID:CB_ID + 128] = np.eye(128, dtype=np.float32)
    kk = np.arange(128)[:, None]
    qq = np.arange(128)[None, :]
    c[:, CB_MASK:CB_MASK + 128] = np.where(kk > qq, -30000.0, 0.0)
    c[:, CB_ONES:CB_ONES + 128] = 1.0
    for h in range(8):
        c[h, CB_SEL + h * 128:CB_SEL + (h + 1) * 128] = 1.0
        c[32 + h, CB_SEL + h * 128:CB_SEL + (h + 1) * 128] = 1.0
        c[64 + h, CB_SEL + h * 128:CB_SEL + (h + 1) * 128] = 1.0
        c[96 + h, CB_SEL + h * 128:CB_SEL + (h + 1) * 128] = 1.0
    return c


_PROG_CACHE = {}


def _get_prog(layers):
    key = tuple(layers)
    if key not in _PROG_CACHE:
        _PROG_CACHE[key] = build_program(list(layers))
    return _PROG_CACHE[key]


MODE = "fused"


def kernel(**inputs):
    inp = {k: np.asarray(v) for k, v in inputs.items()}
    x = inp["x"].astype(np.float32, copy=False)
    mem = inp["mem"].astype(np.float32, copy=False)
    B = x.shape[0]
    vecs = pack_vecs(inp)
    cbf = pack_consts()
    xT = [np.ascontiguousarray(x[b].T) for b in range(B)]
    memT = [np.ascontiguousarray(mem[b].T) for b in range(B)]
    groups = [[L] for L in range(DEPTH)] if MODE == "per_layer" else [list(range(DEPTH))]
    for layers in groups:
        nc = _get_prog(layers)
        warr = pack_weights(inp, layers)
        in_maps = []
        for b in range(B):
            m = {"xT": xT[b], "memT": memT[b], "vecs": vecs, "cbf": cbf}
            for L in layers:
                m[f"w{L}"] = warr[L]
            in_maps.append(m)
        res = run_bass_kernel_spmd(nc, in_maps, core_ids=list(range(B)))
        xT = [np.asarray(res.results[b]["outT"], np.float32) for b in range(B)]
    out = np.stack([xT[b].T for b in range(B)], axis=0)
    return np.ascontiguousarray(out.astype(np.float32))
```

```python
import numpy as np
from contextlib import ExitStack
import concourse.bass as bass
import concourse.mybir as mybir
from concourse.bass_utils import run_bass_kernel_spmd

F32 = mybir.dt.float32
BF16 = mybir.dt.bfloat16
AF = mybir.ActivationFunctionType
ALU = mybir.AluOpType

DEPTH = 4
D = 1024
T = 2048
NT = 1024
TC = 512
MEM = 256
DFF = 2816
NFF = 22
EPS = 1e-6
NSLOT = 3
SLOT_W = 2048
NG = 4

VCOL = {}
_c = 0
for _n in ("g_mix_pre", "g_mix_post", "g_cross_pre", "g_mem", "g_cross_post", "g_ffn_pre", "g_ffn_post"):
    VCOL[_n] = _c
    _c += 32
VCOL["abcw"] = _c; _c += 24
VCOL["ccw"] = _c; _c += 64
for _n in ("ccb", "cba", "cbi", "clam"):
    VCOL[_n] = _c
    _c += 16
VCOL["bf"] = _c; _c += 2
VCOL["one"] = _c; _c += 1
VCOL["eps"] = _c; _c += 1
VCOL["zero"] = _c; _c += 1
VCOL["id8"] = _c; _c += 8
NV = _c + (_c % 2)

CB_ID, CB_MASK, CB_ONES, CB_SEL = 0, 128, 256, 384
NCB = 384 + 1024


def panel_order(layers):
    order = []
    for L in layers:
        if L % 2 == 0:
            for hf in range(2):
                order += [(L, ("abq",), 4096), (L, ("abk",), 4096), (L, ("abv",), 4096)]
                order += [(L, ("abc", i), 8 * 392) for i in range(4)]
                order += [(L, ("abo", j), 4096) for j in range(2)]
        else:
            for hf in range(2):
                order += [(L, ("cin", 0), 4096)]
                for n in range(4):
                    if n < 3:
                        order += [(L, ("cin", n + 1), 4096)]
                    order += [(L, ("cg", n), 1024)]
                order += [(L, ("co", j), 4096) for j in range(2)]
        order += [(L, ("xk", j), 4096) for j in range(2)]
        order += [(L, ("xv", j), 4096) for j in range(2)]
        for hf in range(2):
            order += [(L, ("xq", j), 4096) for j in range(2)]
            order += [(L, ("xo", j), 4096) for j in range(2)]
        for hf in range(2):
            order += [(L, ("gu", f), 2048) for f in range(NFF)]
            order += [(L, ("dn", oc), NFF * 128) for oc in range(8)]
    return order


def panel_offsets(layers):
    offs = {}
    tot = {L: 0 for L in layers}
    for (L, key, n) in panel_order(layers):
        if (L, key) not in offs:
            offs[(L, key)] = tot[L]
            tot[L] += n
    return offs, tot


class Dep:
    __slots__ = ("w", "r")

    def __init__(self):
        self.w = None
        self.r = {}


class Eng:
    def __init__(self, name):
        self.name = name
        self.ops = []
        self.sem = None
        self.cnt = 0
        self.seen = {}


class Sched:
    def __init__(self, nc, stack):
        self.nc = nc
        self.stack = stack
        self.engs = {n: Eng(n) for n in ("pe", "act", "dve", "pool", "sp")}
        self.nsem = 0
        self.new_epoch()

    def new_sem(self, name):
        self.nsem += 1
        return self.stack.enter_context(self.nc.semaphore(f"{name}_{self.nsem}"))

    def new_epoch(self):
        for e in self.engs.values():
            e.sem = self.new_sem("e_" + e.name)
            e.cnt = 0

    def _waits(self, eng, reads, writes):
        need = {}

        def add(tok, raw):
            if tok is None:
                return
            sem, val, src = tok
            if src is eng and eng.name == "pe":
                return
            k = id(sem)
            if eng.seen.get(k, 0) >= val:
                return
            if k not in need or need[k][1] < val:
                need[k] = (sem, val)

        for d in reads:
            add(d.w, True)
        for d in writes:
            add(d.w, False)
            for tok in d.r.values():
                add(tok, False)
        for k, (sem, val) in need.items():
            eng.seen[k] = val
            eng.ops.append(("wait", sem, val))

    def op(self, engname, fn, reads=(), writes=(), inc=True):
        eng = self.engs[engname]
        self._waits(eng, reads, writes)
        if inc:
            eng.cnt += 1
            tok = (eng.sem, eng.cnt, eng)
        else:
            tok = (eng.sem, eng.cnt + 1, eng)
        eng.ops.append(("op", fn, eng.sem if inc else None, 1))
        for d in reads:
            d.r[id(eng.sem)] = tok
        for d in writes:
            d.w = tok
            d.r = {}

    def dma(self, engname, out, in_, dsem, reads=(), writes=()):
        eng = self.engs[engname]
        self._waits(eng, reads, writes)
        dsem[1] += 16
        tok = (dsem[0], dsem[1], None)
        eng.ops.append(("op", lambda e, o=out, i=in_: e.dma_start(out=o, in_=i), dsem[0], 16))
        for d in reads:
            d.r[id(dsem[0])] = tok
        for d in writes:
            d.w = tok
            d.r = {}

    def dsem(self, name):
        return [self.new_sem("d_" + name), 0]

    def wait_tok(self, engname, sem, val):
        self.engs[engname].ops.append(("wait", sem, val))

    def barrier(self):
        for e in self.engs.values():
            for e2 in self.engs.values():
                if e2 is e or e2.cnt == 0:
                    continue
                k = id(e2.sem)
                if e.seen.get(k, 0) >= e2.cnt:
                    continue
                e.seen[k] = e2.cnt
                e.ops.append(("wait", e2.sem, e2.cnt))

    def finish(self):
        nc = self.nc
        with nc.Block() as block:
            def runner(eng):
                def f(e):
                    for o in eng.ops:
                        if o[0] == "wait":
                            e.wait_ge(o[1], o[2])
                        else:
                            ins = o[1](e)
                            if o[2] is not None:
                                ins.then_inc(o[2], o[3])
                return f
            block.tensor(runner(self.engs["pe"]))
            block.scalar(runner(self.engs["act"]))
            block.vector(runner(self.engs["dve"]))
            block.gpsimd(runner(self.engs["pool"]))
            block.sync(runner(self.engs["sp"]))


def o_act(out, in_, func, **kw):
    return lambda e: e.activation(out=out, in_=in_, func=func, **kw)


def o_ts(out, in0, s1, s2, op0, op1=None):
    if op1 is None:
        return lambda e: e.tensor_scalar(out=out, in0=in0, scalar1=s1, scalar2=None, op0=op0)
    return lambda e: e.tensor_scalar(out=out, in0=in0, scalar1=s1, scalar2=s2, op0=op0, op1=op1)


def o_tt(out, in0, in1, op):
    return lambda e: e.tensor_tensor(out=out, in0=in0, in1=in1, op=op)


def o_stt(out, in0, scalar, in1, op0, op1):
    return lambda e: e.scalar_tensor_tensor(out=out, in0=in0, scalar=scalar, in1=in1, op0=op0, op1=op1)


def o_copy(out, in_):
    return lambda e: e.tensor_copy(out=out, in_=in_)


def o_recip(out, in_):
    return lambda e: e.reciprocal(out=out, in_=in_)


def o_memset(ap, v):
    return lambda e: e.memset(ap, v)


def o_scan(out, d0, d1, init):
    return lambda e: e.tensor_tensor_scan(out=out, data0=d0, data1=d1, initial=init, op0=ALU.mult, op1=ALU.add)


def o_mm(out, lhsT, rhs, start, stop):
    return lambda e: e.matmul(out, lhsT, rhs, start=start, stop=stop)


def build_program(layers):
    offs, wtot = panel_offsets(layers)
    order = panel_order(layers)
    nc = bass.Bass("TRN2", target_bir_lowering=False)
    xin = nc.dram_tensor("xT", [D, T], F32, kind="ExternalInput").ap()
    memin = nc.dram_tensor("memT", [D, MEM], F32, kind="ExternalInput").ap()
    vecs_d = nc.dram_tensor("vecs", [128, NV], F32, kind="ExternalInput").ap()
    cbf_d = nc.dram_tensor("cbf", [128, NCB], F32, kind="ExternalInput").ap()
    wl_d = {L: nc.dram_tensor(f"w{L}", [128, wtot[L]], F32, kind="ExternalInput").ap() for L in layers}
    out_d = nc.dram_tensor("outT", [D, T], F32, kind="ExternalOutput").ap()

    A_XT = 0
    A_SLOT = A_XT + 8 * T
    A_CBF = A_SLOT + NSLOT * SLOT_W
    A_VEC = A_CBF + NCB // 2
    A_ONEF = A_VEC + NV
    A_SQ = A_ONEF + 512
    A_RS = A_SQ + 2 * 256
    A_SB = A_RS + 2 * 512
    SCR = 26912
    AW = A_SB + SCR

    with ExitStack() as st:
        S = Sched(nc, st)
        arena = st.enter_context(nc.sbuf_tensor("arena", [128, AW], F32))
        psb = [st.enter_context(nc.psum_tensor(f"ps{i}", [128, 512], F32)) for i in range(8)]
        psd = [Dep() for _ in range(8)]
        rr_state = {"g": 0}

        def psum():
            i = rr_state["g"]
            rr_state["g"] = (i + 1) % NG
            return psb[i], psd[i]

        def vf(off, n):
            return arena[:, off:off + n]

        def vb(off, nwords):
            return arena[:, off:off + nwords].bitcast(BF16)

        def XT(c, t0, n):
            return arena[:, A_XT + c * T + t0: A_XT + c * T + t0 + n]

        xd = [[Dep() for _ in range(4)] for _ in range(8)]
        slot_bf = [vb(A_SLOT + s * SLOT_W, SLOT_W) for s in range(NSLOT)]
        slot_dep = [Dep() for _ in range(NSLOT)]
        slot_sem = [S.dsem(f"slot{s}") for s in range(NSLOT)]
        cbf = vb(A_CBF, NCB // 2)
        cbf_dep = Dep()
        vecs = vf(A_VEC, NV)
        vec_dep = Dep()
        onef = vf(A_ONEF, 512)
        onef_dep = Dep()
        sqb = [vb(A_SQ + i * 256, 256) for i in range(2)]
        sqd = [Dep() for _ in range(2)]
        rsb = [vf(A_RS + i * 512, 512) for i in range(2)]
        rsd = [Dep() for _ in range(2)]
        cnt = {"sq": 0, "rs": 0}

        ident = cbf[:, CB_ID:CB_ID + 128]
        negmask = cbf[:, CB_MASK:CB_MASK + 128]
        ones_bf = cbf[:, CB_ONES:CB_ONES + 128]

        def vc(name, idx=0, p0=0, p1=128):
            c = VCOL[name] + idx
            return vecs[p0:p1, c:c + 1]

        _stage = ""
        ws = {"issue": 0, "use": 0}

        def wget(L, key):
            idx = ws["use"]
            assert order[idx][0] == L and order[idx][1] == key, (order[idx], L, key)
            lim = min(len(order), idx + NSLOT)
            while ws["issue"] < lim:
                q = ws["issue"]
                Lq, kq, nq = order[q]
                s = q % NSLOT
                off = offs[(Lq, kq)]
                S.dma("pool", slot_bf[s][:, 0:nq], wl_d[Lq][:, off:off + nq], slot_sem[s], writes=[slot_dep[s]])
                ws["issue"] += 1
            ws["use"] += 1
            return slot_bf[idx % NSLOT], slot_dep[idx % NSLOT]

        def mm(items, reads, wdeps, start=True, stop=True):
            n = len(items)
            for i, (o, l, r) in enumerate(items):
                S.op("pe", o_mm(o, l, r, start and i == 0, stop and i == n - 1),
                     reads=reads if i == 0 else (), writes=wdeps if i == 0 else (), inc=(i == n - 1))

        d_in = S.dsem("in")
        S.dma("sp", vecs, vecs_d, d_in, writes=[vec_dep])
        d_cb = S.dsem("cb")
        S.dma("pool", cbf, cbf_d, d_cb, writes=[cbf_dep])
        d_x = S.dsem("x")
        for c in range(8):
            S.dma("sp", XT(c, 0, T), xin[c * 128:(c + 1) * 128, :], d_x, writes=xd[c])
        for c in range(8):
            for dd in xd[c]:
                dd.w = (d_x[0], d_x[1], None)
        S.op("dve", o_memset(onef, 1.0), writes=[onef_dep])

        def rstd_from(ps, pd, ncol):
            i = cnt["rs"] % 2
            cnt["rs"] += 1
            rs, rd = rsb[i], rsd[i]
            S.op("act", o_act(rs[:, 0:ncol], ps[:, 0:ncol], AF.Ln, scale=1.0 / D, bias=vc("eps")), reads=[pd, vec_dep], writes=[rd])
            S.op("act", o_act(rs[:, 0:ncol], rs[:, 0:ncol], AF.Exp, scale=-0.5), reads=[rd], writes=[rd])
            return rs, rd

        def prenorm(L, gname, T0, hT, hd):
            for tcl in range(2):
                tg = T0 + tcl * TC
                tcg = tg // TC
                ps, pd = psum()
                for c in range(8):
                    i = cnt["sq"] % 2
                    cnt["sq"] += 1
                    S.op("act", o_act(sqb[i], XT(c, tg, TC), AF.Square), reads=[xd[c][tcg]], writes=[sqd[i]])
                    mm([(ps[:, :], ones_bf, sqb[i])], [sqd[i], cbf_dep], [pd], start=(c == 0), stop=(c == 7))
                rs, rd = rstd_from(ps, pd, TC)
                for c in range(8):
                    S.op("dve", o_stt(hT(c, tcl), XT(c, tg, TC), vc(gname, L * 8 + c), rs, ALU.mult, ALU.mult),
                         reads=[xd[c][tcg], rd, vec_dep], writes=[hd[c][tcl]])

        def outproj_postnorm(L, T0, panels, gname, yv, yd):
            st_ps = [(psb[4], psd[4]), (psb[5], psd[5])]
            pending = []

            def flush():
                if _stage in ("op0", "op0c", "op1"):
                    pending.clear()
                while pending:
                    oc_, tcl_, sqi = pending.pop(0)
                    mm([(st_ps[tcl_][0][:, :], ones_bf, sqb[sqi])], [sqd[sqi], cbf_dep], [st_ps[tcl_][1]],
                       start=(oc_ == 0), stop=(oc_ == 7))

            for oc in range(8):
                getp = panels[oc]
                for tcl in range(2):
                    ps, pd = psum()
                    items, reads = getp(tcl, ps)
                    mm(items, reads, [pd])
                    flush()
                    if _stage == "op0":
                        continue
                    S.op("dve", o_copy(yv(oc, tcl), ps[:, :]), reads=[pd], writes=[yd[oc][tcl]])
                    if _stage == "op0c":
                        continue
                    i = cnt["sq"] % 2
                    cnt["sq"] += 1
                    S.op("pool", o_tt(sqb[i], yv(oc, tcl), yv(oc, tcl), ALU.mult), reads=[yd[oc][tcl]], writes=[sqd[i]])
                    pending.append((oc, tcl, i))
            flush()
            if _stage in ("op0", "op0c", "op1", "op2"):
                return
            for tcl in range(2):
                tg = T0 + tcl * TC
                tcg = tg // TC
                rs, rd = rstd_from(st_ps[tcl][0], st_ps[tcl][1], TC)
                for c in range(8):
                    S.op("dve", o_tt(yv(c, tcl), yv(c, tcl), rs, ALU.mult), reads=[yd[c][tcl], rd], writes=[yd[c][tcl]])
                    S.op("dve", o_stt(XT(c, tg, TC), yv(c, tcl), vc(gname, L * 8 + c), XT(c, tg, TC), ALU.mult, ALU.add),
                         reads=[yd[c][tcl], vec_dep, xd[c][tcg]], writes=[xd[c][tcg]])

        def std_panels(L, kname, src, srcd):
            cache = {}

            def mk(oc):
                def getp(tcl, ps):
                    j, ol = oc // 4, oc % 4
                    if (j) not in cache:
                        cache.clear()
                        cache[j] = wget(L, (kname, j))
                    w, wd = cache[j]
                    items = [(ps[:, :], w[:, kc * 512 + ol * 128: kc * 512 + (ol + 1) * 128], src(kc, tcl)) for kc in range(8)]
                    return items, [wd] + [srcd[kc][tcl] for kc in range(8)]
                return getp
            return [mk(oc) for oc in range(8)]

        def proj_fm(w, wd, coff, ncols_panel, hT, hd, tcl, ps, m=128):
            items = [(ps[0:m, :], w[:, kc * ncols_panel + coff: kc * ncols_panel + coff + m], hT(kc, tcl)) for kc in range(8)]
            return items, [wd] + [hd[kc][tcl] for kc in range(8)]

        def even_mixer(L):
            e = L // 2
            SB = A_SB
            kTv = vb(SB + 0, 4096)
            Vv = vb(SB + 4096, 4096)
            Nf = vf(SB + 8192, 2048)
            halo = vf(SB + 10240, 8)
            negbf = vf(SB + 10248, 2)
            biasT = [vf(SB + 10256 + i * 128, 128) for i in range(2)]
            hTv = vb(SB + 10512, 4096)
            qTv = vb(SB + 14608, 2048)
            cu = vf(SB + 16656, 1028)
            Cc = vf(SB + 17684, 512)
            tcv = vf(SB + 18196, 512)
            yv_ = vf(SB + 10512, 8192)
            yTv = vb(SB + 18708, 4096)
            PT = [vb(SB + 22804 + i * 256, 256) for i in range(2)]
            rcp = [vf(SB + 23316 + i * 512, 512) for i in range(2)]
            tmp = [vf(SB + 24340 + i * 512, 512) for i in range(4)]
            cq = [vb(SB + 26388 + i * 256, 256) for i in range(2)]
            kd = [[Dep() for _ in range(4)] for _ in range(4)]
            vd = [Dep() for _ in range(16)]
            Nd = [Dep() for _ in range(4)]
            halod = [Dep() for _ in range(4)]
            negbfd = Dep()
            biasTd = [Dep(), Dep()]
            hd = [[Dep() for _ in range(2)] for _ in range(8)]
            qd = [[Dep() for _ in range(2)] for _ in range(4)]
            cud, Ccd, tcd = Dep(), Dep(), Dep()
            yd = [[Dep() for _ in range(2)] for _ in range(8)]
            yTd = [[Dep() for _ in range(2)] for _ in range(8)]
            PTd = [Dep(), Dep()]
            rcpd = [Dep(), Dep()]
            tmpd = [Dep() for _ in range(4)]
            cqd = [Dep(), Dep()]

            def hT(kc, tcl):
                return hTv[:, kc * NT + tcl * TC: kc * NT + (tcl + 1) * TC]

            def qT(i, tcl):
                return qTv[:, i * NT + tcl * TC: i * NT + (tcl + 1) * TC]

            def kT(i, t0, n):
                return kTv[:, i * T + t0: i * T + t0 + n]

            def Vb(tb):
                return Vv[:, tb * 512:(tb + 1) * 512]

            def yT(c, tcl):
                return yTv[:, c * NT + tcl * TC: c * NT + (tcl + 1) * TC]

            def yv(c, tcl):
                return yv_[:, c * NT + tcl * TC: c * NT + (tcl + 1) * TC]

            S.barrier()
            S.op("dve", o_ts(negbf[0:8, :], vecs[0:8, VCOL["bf"]:VCOL["bf"] + 2], -1.0, None, ALU.mult),
                 reads=[vec_dep], writes=[negbfd])
            for i in range(2):
                S.op("pool", o_memset(cq[i], 0.0), writes=[cqd[i]])

            for hf in range(2):
                T0 = hf * NT
                if hf == 1:
                    S.barrier()
                prenorm(L, "g_mix_pre", T0, hT, hd)
                if _stage == "pre":
                    return
                w, wd = wget(L, ("abq",))
                for i in range(4):
                    for tcl in range(2):
                        ps, pd = psum()
                        items, reads = proj_fm(w, wd, i * 128, 512, hT, hd, tcl, ps)
                        mm(items, reads, [pd])
                        S.op("act", o_act(qT(i, tcl), ps[:, :], AF.Copy), reads=[pd], writes=[qd[i][tcl]])
                w, wd = wget(L, ("abk",))
                for i in range(4):
                    for tcl in range(2):
                        ps, pd = psum()
                        items, reads = proj_fm(w, wd, i * 128, 512, hT, hd, tcl, ps)
                        mm(items, reads, [pd])
                        S.op("dve", o_copy(kT(i, T0 + tcl * TC, TC), ps[:, :]), reads=[pd], writes=[kd[i][hf * 2 + tcl]])
                w, wd = wget(L, ("abv",))
                for tb in range(8):
                    tcl = tb // 4
                    ps, pd = psum()
                    items = [(ps[:, :], hT(kc, tcl)[:, (tb % 4) * 128:(tb % 4 + 1) * 128], w[:, kc * 512:(kc + 1) * 512]) for kc in range(8)]
                    mm(items, [wd] + [hd[kc][tcl] for kc in range(8)], [pd])
                    eng = "act" if tb % 2 == 0 else "dve"
                    if eng == "act":
                        S.op("act", o_act(Vb(hf * 8 + tb), ps[:, :], AF.Copy), reads=[pd], writes=[vd[hf * 8 + tb]])
                    else:
                        S.op("dve", o_copy(Vb(hf * 8 + tb), ps[:, :]), reads=[pd], writes=[vd[hf * 8 + tb]])
                for i in range(4):
                    w, wd = wget(L, ("abc", i))
                    if i == 0:
                        for tcl in range(2):
                            cg = hf * 2 + tcl
                            tg = T0 + tcl * TC
                            ps, pd = psum()
                            items, reads = proj_fm(w, wd, 384, 392, hT, hd, tcl, ps, m=8)
                            mm(items, reads, [pd])
                            tA, tB = tmp[0], tmp[1]
                            S.op("act", o_act(tA[0:8, :], ps[0:8, :], AF.Exp, scale=-1.0, bias=negbf[0:8, e:e + 1]),
                                 reads=[pd, negbfd], writes=[tmpd[0]])
                            S.op("act", o_act(tB[0:8, :], tA[0:8, :], AF.Ln, scale=1.0, bias=vc("one", 0, 0, 8)),
                                 reads=[tmpd[0], vec_dep], writes=[tmpd[1]])
                            init = Nf[0:8, tg - 1:tg] if cg > 0 else 0.0
                            S.op("dve", o_scan(Nf[0:8, tg:tg + TC], onef[0:8, :], tB[0:8, :], init),
                                 reads=[tmpd[1], onef_dep] + ([Nd[cg - 1]] if cg > 0 else []), writes=[Nd[cg]])
                    if hf == 0:
                        S.op("dve", o_memset(cu[:, 0:2], 0.0), writes=[cud])
                    else:
                        S.op("dve", o_copy(cu[:, 0:2], halo[:, 2 * i:2 * i + 2]), reads=[halod[i]], writes=[cud])
                    for tcl in range(2):
                        psB, pdB = psum()
                        items, reads = proj_fm(w, wd, 0, 392, hT, hd, tcl, psB)
                        mm(items, reads, [pdB])
                        psC, pdC = psum()
                        items, reads = proj_fm(w, wd, 128, 392, hT, hd, tcl, psC)
                        mm(items, reads, [pdC])
                        psU, pdU = psum()
                        items, reads = proj_fm(w, wd, 256, 392, hT, hd, tcl, psU)
                        mm(items, reads, [pdU])
                        S.op("act", o_act(Cc, psC[:, :], AF.Copy), reads=[pdC], writes=[Ccd])
                        b0 = tcl * TC
                        S.op("dve", o_tt(cu[:, 2 + b0:2 + b0 + TC], psU[:, :], Cc, ALU.mult), reads=[pdU, Ccd], writes=[cud])
                        cw = VCOL["abcw"] + e * 12 + i
                        S.op("dve", o_ts(tcv, cu[:, b0:b0 + TC], vecs[:, cw:cw + 1], None, ALU.mult),
                             reads=[cud, vec_dep], writes=[tcd])
                        S.op("dve", o_stt(tcv, cu[:, b0 + 1:b0 + 1 + TC], vecs[:, cw + 4:cw + 5], tcv, ALU.mult, ALU.add),
                             reads=[cud, tcd], writes=[tcd])
                        S.op("dve", o_stt(tcv, cu[:, b0 + 2:b0 + 2 + TC], vecs[:, cw + 8:cw + 9], tcv, ALU.mult, ALU.add),
                             reads=[cud, tcd], writes=[tcd])
                        S.op("dve", o_tt(yT(4 + i, tcl), psB[:, :], tcv, ALU.mult), reads=[pdB, tcd], writes=[yTd[4 + i][tcl]])
                    if hf == 0:
                        S.op("dve", o_copy(halo[:, 2 * i:2 * i + 2], cu[:, NT:NT + 2]), reads=[cud], writes=[halod[i]])

                if _stage == "proj":
                    return
                for tcl in range(2):
                    cg = hf * 2 + tcl
                    tg = T0 + tcl * TC
                    nkb = 4 * cg + 4
                    Rcol = Nf[0:8, tg + 255:tg + 256]
                    Rdeps = [Nd[cg]]
                    cqb, cqbd = cq[cg % 2], cqd[cg % 2]
                    bT, bTd = biasT[cg % 2], biasTd[cg % 2]
                    psT, pdT = psum()
                    for seg in range(cg + 1):
                        tb_, tbd_ = tmp[2 + seg % 2], tmpd[2 + seg % 2]
                        S.op("dve", o_ts(tb_[0:8, :], Nf[0:8, seg * TC:(seg + 1) * TC], Rcol, None, ALU.subtract),
                             reads=[Nd[seg]] + Rdeps, writes=[tbd_])
                        for jb in range(4):
                            j = seg * 4 + jb
                            mm([(psT[:, j * 8:(j + 1) * 8], tb_[0:8, jb * 128:(jb + 1) * 128], vecs[0:8, VCOL["id8"]:VCOL["id8"] + 8])],
                               [tbd_, vec_dep], [pdT])
                    S.op("dve", o_copy(bT[:, 0:nkb * 8], psT[:, 0:nkb * 8]), reads=[pdT], writes=[bTd])

                    if _stage == "attn_prep":
                        continue
                    items_l = [(h, j) for h in range(8) for j in range(nkb)]
                    state = {}

                    def qk(idx):
                        h, j = items_l[idx]
                        i, hp = h // 2, h % 2
                        jj = j - 4 * cg
                        c0 = 0 if jj <= 0 else jj * 128
                        ps, pd = psum()
                        lo, hi = hp * 64, hp * 64 + 64
                        its = [(ps[:, c0:TC], kT(i, j * 128, 128)[lo:hi, :], qT(i, tcl)[lo:hi, c0:TC])]
                        if jj >= 0 and _stage not in ("attn_qk1", "attn_qk2"):
                            its.append((ps[:, c0:c0 + 128], ident, negmask))
                        mm(its, [kd[i][j // 4], qd[i][tcl], cbf_dep], [pd])
                        pb = idx % 2
                        S.op("act", o_act(PT[pb][:, c0:TC], ps[:, c0:TC], AF.Exp, scale=0.125, bias=bT[:, j * 8 + h:j * 8 + h + 1]),
                             reads=[pd, bTd], writes=[PTd[pb]])
                        state[idx] = (pb, c0)

                    def pv(idx):
                        h, j = items_l[idx]
                        i, hp = h // 2, h % 2
                        pb, c0 = state.pop(idx)
                        so = 4 + 2 * (h % 2)
                        pso, psod, psn, psnd = psb[so], psd[so], psb[so + 1], psd[so + 1]
                        mm([(pso[:, c0:TC], Vb(j)[:, i * 128:(i + 1) * 128], PT[pb][:, c0:TC])], [vd[j], PTd[pb]], [psod],
                           start=(j == 0), stop=(j == nkb - 1))
                        mm([(psn[:, c0:TC], ones_bf, PT[pb][:, c0:TC])], [PTd[pb], cbf_dep], [psnd],
                           start=(j == 0), stop=(j == nkb - 1))
                        if j == nkb - 1:
                            rb = h % 2
                            lo, hi = hp * 64, hp * 64 + 64
                            S.op("dve", o_recip(rcp[rb][lo:hi, :], psn[lo:hi, :]), reads=[psnd], writes=[rcpd[rb]])
                            S.op("dve", o_tt(yT(i, tcl)[lo:hi, :], pso[lo:hi, :], rcp[rb][lo:hi, :], ALU.mult),
                                 reads=[psod, rcpd[rb]], writes=[yTd[i][tcl]])

                    n_it = len(items_l)
                    for idx in range(n_it + 1):
                        if idx < n_it:
                            qk(idx)
                        if idx >= 1 and not _stage.startswith("attn_qk"):
                            pv(idx - 1)

                if _stage in ("attn", "attn_prep", "attn_qk", "attn_qk1", "attn_qk2"):
                    return
                S.barrier()
                if _stage == "mixop":
                    return
                outproj_postnorm(L, T0, std_panels(L, "abo", yT, yTd), "g_mix_post", yv, yd)
                if _stage in ("mix0", "op0", "op0c", "op1", "op2"):
                    return

        def odd_mixer(L):
            o = L // 2
            SB = A_SB
            halo = vf(SB + 0, 24)
            carry = vf(SB + 32, 8)
            sc1 = vf(SB + 40, 8)
            sc2 = vf(SB + 48, 8)
            spt = vf(SB + 56, 8)
            hTv = vb(SB + 128, 4096)
            ubuf = [vf(SB + 4224 + i * 1028, 1028) for i in range(2)]
            uc = [vf(SB + 6280 + i * 1024, 1024) for i in range(2)]
            yv_ = vf(SB + 128, 8192)
            ggv = vb(SB + 8328, 4096)
            yTv = vb(SB + 12424, 4096)
            ucb = [vb(SB + 16520 + i * 512, 512) for i in range(2)]
            rr0, ii0, aa, ss, bb, hs, rr1, ii1 = [vf(SB + 17544 + i * 1024, 1024) for i in range(8)]
            rrs, iis = [rr0, rr1], [ii0, ii1]
            halod = [Dep() for _ in range(8)]
            carryd = [Dep() for _ in range(8)]
            scd = Dep()
            hd = [[Dep() for _ in range(2)] for _ in range(8)]
            ubd = [Dep(), Dep()]
            ucd = [Dep(), Dep()]
            ucbd = [Dep(), Dep()]
            ggd = [[Dep() for _ in range(2)] for _ in range(8)]
            yTd = [[Dep() for _ in range(2)] for _ in range(8)]
            yd = [[Dep() for _ in range(2)] for _ in range(8)]
            aad, ssd, bbd, hsd = [Dep() for _ in range(4)]
            rrds, iids = [Dep(), Dep()], [Dep(), Dep()]

            def hT(kc, tcl):
                return hTv[:, kc * NT + tcl * TC: kc * NT + (tcl + 1) * TC]

            def gg(c, tcl):
                return ggv[:, c * NT + tcl * TC: c * NT + (tcl + 1) * TC]

            def yT(c, tcl):
                return yTv[:, c * NT + tcl * TC: c * NT + (tcl + 1) * TC]

            def yv(c, tcl):
                return yv_[:, c * NT + tcl * TC: c * NT + (tcl + 1) * TC]

            S.barrier()
            lam = vecs[:, VCOL["clam"] + o * 8: VCOL["clam"] + o * 8 + 8]
            S.op("act", o_act(spt, lam, AF.Exp, scale=-1.0), reads=[vec_dep], writes=[scd])
            S.op("act", o_act(spt, spt, AF.Ln, scale=1.0, bias=vc("one")), reads=[scd, vec_dep], writes=[scd])
            S.op("dve", o_ts(sc1, spt, -8.0, None, ALU.mult), reads=[scd], writes=[scd])
            S.op("dve", o_ts(sc2, spt, -16.0, None, ALU.mult), reads=[scd], writes=[scd])

            for hf in range(2):
                T0 = hf * NT
                if hf == 1:
                    S.barrier()
                prenorm(L, "g_mix_pre", T0, hT, hd)
                def in_proj(n):
                    w, wd = wget(L, ("cin", n))
                    for ch in range(2):
                        c = 2 * n + ch
                        if hf == 0:
                            S.op("pool", o_memset(ubuf[ch][:, 0:3], 0.0), writes=[ubd[ch]])
                        else:
                            S.op("pool", o_copy(ubuf[ch][:, 0:3], halo[:, 3 * c:3 * c + 3]), reads=[halod[c]], writes=[ubd[ch]])
                        for tcl in range(2):
                            ps, pd = psum()
                            items, reads = proj_fm(w, wd, ch * 128, 512, hT, hd, tcl, ps)
                            mm(items, reads, [pd])
                            S.op("act", o_act(gg(c, tcl), ps[:, :], AF.Gelu_apprx_tanh), reads=[pd], writes=[ggd[c][tcl]])
                            ps, pd = psum()
                            items, reads = proj_fm(w, wd, 256 + ch * 128, 512, hT, hd, tcl, ps)
                            mm(items, reads, [pd])
                            S.op("dve", o_copy(ubuf[ch][:, 3 + tcl * TC:3 + (tcl + 1) * TC], ps[:, :]), reads=[pd], writes=[ubd[ch]])

                def conv(n):
                    for ch in range(2):
                        c = 2 * n + ch
                        cw = VCOL["ccw"] + o * 32 + c
                        cbias = vc("ccb", o * 8 + c)
                        S.op("act", o_act(uc[ch], ubuf[ch][:, 0:NT], AF.Identity, scale=vecs[:, cw:cw + 1], bias=cbias),
                             reads=[ubd[ch], vec_dep], writes=[ucd[ch]])
                        for k in range(1, 4):
                            S.op("dve", o_stt(uc[ch], ubuf[ch][:, k:k + NT], vecs[:, cw + 8 * k:cw + 8 * k + 1], uc[ch], ALU.mult, ALU.add),
                                 reads=[ubd[ch], ucd[ch]], writes=[ucd[ch]])
                        if hf == 0:
                            S.op("pool", o_copy(halo[:, 3 * c:3 * c + 3], ubuf[ch][:, NT:NT + 3]), reads=[ubd[ch]], writes=[halod[c]])
                        S.op("pool", o_copy(ucb[ch], uc[ch]), reads=[ucd[ch]], writes=[ucbd[ch]])

                def gates(n):
                    wg, wgd = wget(L, ("cg", n))
                    for dch in range(2):
                        c = 2 * n + dch
                        rr, ii, rrd, iid = rrs[dch], iis[dch], rrds[dch], iids[dch]
                        for tcl in range(2):
                            ps, pd = psum()
                            its = [(ps[:, :], wg[:, cc * 256 + dch * 128: cc * 256 + (dch + 1) * 128], ucb[cc][:, tcl * TC:(tcl + 1) * TC]) for cc in range(2)]
                            mm(its, [wgd, ucbd[0], ucbd[1]], [pd])
                            S.op("act", o_act(rr[:, tcl * TC:(tcl + 1) * TC], ps[:, :], AF.Sigmoid, scale=1.0, bias=vc("cba", o * 8 + c)),
                                 reads=[pd, vec_dep], writes=[rrd])
                            ps, pd = psum()
                            its = [(ps[:, :], wg[:, 512 + cc * 256 + dch * 128: 512 + cc * 256 + (dch + 1) * 128], ucb[cc][:, tcl * TC:(tcl + 1) * TC]) for cc in range(2)]
                            mm(its, [wgd, ucbd[0], ucbd[1]], [pd])
                            S.op("act", o_act(ii[:, tcl * TC:(tcl + 1) * TC], ps[:, :], AF.Sigmoid, scale=1.0, bias=vc("cbi", o * 8 + c)),
                                 reads=[pd, vec_dep], writes=[iid])
                        S.op("act", o_act(aa, rr, AF.Exp, scale=sc1[:, c:c + 1]), reads=[rrd, scd], writes=[aad])
                        S.op("act", o_act(ss, rr, AF.Exp, scale=sc2[:, c:c + 1]), reads=[rrd, scd], writes=[ssd])
                        S.op("act", o_act(ss, ss, AF.Sqrt, scale=-1.0, bias=vc("one")), reads=[ssd, vec_dep], writes=[ssd])
                        S.op("pool", o_tt(bb, ii, uc[dch], ALU.mult), reads=[iid, ucd[dch]], writes=[bbd])
                        S.op("dve", o_tt(bb, bb, ss, ALU.mult), reads=[bbd, ssd], writes=[bbd])
                        init = carry[:, c:c + 1] if hf == 1 else 0.0
                        S.op("dve", o_scan(hs, aa, bb, init), reads=[aad, bbd] + ([carryd[c]] if hf == 1 else []), writes=[hsd])
                        if hf == 0:
                            S.op("dve", o_copy(carry[:, c:c + 1], hs[:, NT - 1:NT]), reads=[hsd], writes=[carryd[c]])
                        for tcl in range(2):
                            S.op("dve", o_tt(yT(c, tcl), gg(c, tcl), hs[:, tcl * TC:(tcl + 1) * TC], ALU.mult),
                                 reads=[ggd[c][tcl], hsd], writes=[yTd[c][tcl]])

                in_proj(0)
                conv(0)
                for n in range(4):
                    if n < 3:
                        in_proj(n + 1)
                    gates(n)
                    if n < 3:
                        conv(n + 1)
                S.barrier()
                outproj_postnorm(L, T0, std_panels(L, "co", yT, yTd), "g_mix_post", yv, yd)

        def cross(L, dmem):
            SB = A_SB
            kxv = vb(SB + 0, 1024)
            Vxv = vb(SB + 1024, 1024)
            memv = vf(SB + 2048, 2048)
            mTv = vb(SB + 4096, 1024)
            hTv = vb(SB + 2048, 4096)
            qxv = vb(SB + 6144, 4096)
            yv_ = vf(SB + 16384, 8192)
            oTv = vb(SB + 10240, 4096)
            PT = [vb(SB + 14336 + i * 512, 512) for i in range(2)]
            rcp = [vf(SB + 15360 + i * 512, 512) for i in range(2)]
            kxd = [Dep() for _ in range(8)]
            Vxd = [[Dep() for _ in range(2)] for _ in range(2)]
            memd = [Dep() for _ in range(8)]
            mTd = [Dep() for _ in range(8)]
            hd = [[Dep() for _ in range(2)] for _ in range(8)]
            qxd = [[Dep() for _ in range(2)] for _ in range(8)]
            oTd = [[Dep() for _ in range(2)] for _ in range(8)]
            yd = [[Dep() for _ in range(2)] for _ in range(8)]
            PTd = [Dep(), Dep()]
            rcpd = [Dep(), Dep()]

            def hT(kc, tcl):
                return hTv[:, kc * NT + tcl * TC: kc * NT + (tcl + 1) * TC]

            def qx(c, tcl):
                return qxv[:, c * NT + tcl * TC: c * NT + (tcl + 1) * TC]

            def oT(c, tcl):
                return oTv[:, c * NT + tcl * TC: c * NT + (tcl + 1) * TC]

            def yv(c, tcl):
                return yv_[:, c * NT + tcl * TC: c * NT + (tcl + 1) * TC]

            def memc(c):
                return memv[:, c * MEM:(c + 1) * MEM]

            def mT(c):
                return mTv[:, c * MEM:(c + 1) * MEM]

            def kx(c):
                return kxv[:, c * MEM:(c + 1) * MEM]

            S.barrier()
            for c in range(8):
                S.dma("sp", memc(c), memin[c * 128:(c + 1) * 128, :], dmem, writes=[memd[c]])
            for c in range(8):
                memd[c].w = (dmem[0], dmem[1], None)
            ps, pd = psum()
            for c in range(8):
                i = cnt["sq"] % 2
                cnt["sq"] += 1
                S.op("act", o_act(sqb[i][:, 0:MEM], memc(c), AF.Square), reads=[memd[c]], writes=[sqd[i]])
                mm([(ps[:, 0:MEM], ones_bf, sqb[i][:, 0:MEM])], [sqd[i], cbf_dep], [pd], start=(c == 0), stop=(c == 7))
            rs, rd = rstd_from(ps, pd, MEM)
            for c in range(8):
                S.op("dve", o_stt(mT(c), memc(c), vc("g_mem", L * 8 + c), rs[:, 0:MEM], ALU.mult, ALU.mult),
                     reads=[memd[c], rd, vec_dep], writes=[mTd[c]])
            for j in range(2):
                w, wd = wget(L, ("xk", j))
                for cl in range(4):
                    c = 4 * j + cl
                    ps, pd = psum()
                    its = [(ps[:, 0:MEM], w[:, kc * 512 + cl * 128: kc * 512 + (cl + 1) * 128], mT(kc)) for kc in range(8)]
                    mm(its, [wd] + mTd, [pd])
                    S.op("act", o_act(kx(c), ps[:, 0:MEM], AF.Copy), reads=[pd], writes=[kxd[c]])
            for j in range(2):
                w, wd = wget(L, ("xv", j))
                for blk in range(2):
                    ps, pd = psum()
                    its = [(ps[:, :], mT(kc)[:, blk * 128:(blk + 1) * 128], w[:, kc * 512:(kc + 1) * 512]) for kc in range(8)]
                    mm(its, [wd] + mTd, [pd])
                    S.op("dve", o_copy(Vxv[:, blk * D + j * 512: blk * D + (j + 1) * 512], ps[:, :]), reads=[pd], writes=[Vxd[blk][j]])

            S.barrier()
            prenorm(L, "g_cross_pre", 0, hT, hd)
            for hf in range(2):
                T0 = hf * NT
                for j in range(2):
                    w, wd = wget(L, ("xq", j))
                    for cl in range(4):
                        c = 4 * j + cl
                        for tcl in range(2):
                            ps, pd = psum()
                            items, reads = proj_fm(w, wd, cl * 128, 512, hT, hd, tcl, ps)
                            mm(items, reads, [pd])
                            if (cl + tcl) % 2 == 0:
                                S.op("act", o_act(qx(c, tcl), ps[:, :], AF.Copy), reads=[pd], writes=[qxd[c][tcl]])
                            else:
                                S.op("dve", o_copy(qx(c, tcl), ps[:, :]), reads=[pd], writes=[qxd[c][tcl]])
                items_l = [(tcl, h) for tcl in range(2) for h in range(4)]

                def qk(idx):
                    tcl, h = items_l[idx]
                    pb = idx % 2
                    for kb in range(2):
                        ps, pd = psum()
                        its = [(ps[:, :], kx(2 * h + dc)[:, kb * 128:(kb + 1) * 128], qx(2 * h + dc, tcl)) for dc in range(2)]
                        mm(its, [kxd[2 * h], kxd[2 * h + 1], qxd[2 * h][tcl], qxd[2 * h + 1][tcl]], [pd])
                        S.op("act", o_act(PT[pb][:, kb * TC:(kb + 1) * TC], ps[:, :], AF.Exp, scale=1.0 / 16.0),
                             reads=[pd], writes=[PTd[pb]])

                def pv(idx):
                    tcl, h = items_l[idx]
                    pb = idx % 2
                    so = 4 + 2 * (idx % 2)
                    accs = []
                    for dc in range(2):
                        po, pod = psb[so + dc], psd[so + dc]
                        its = [(po[:, :], Vxv[:, kb * D + h * 256 + dc * 128: kb * D + h * 256 + (dc + 1) * 128], PT[pb][:, kb * TC:(kb + 1) * TC]) for kb in range(2)]
                        mm(its, [Vxd[0][h // 2], Vxd[1][h // 2], PTd[pb]], [pod])
                        accs.append((po, pod))
                    pn, pnd = psum()
                    its = [(pn[:, :], ones_bf, PT[pb][:, kb * TC:(kb + 1) * TC]) for kb in range(2)]
                    mm(its, [PTd[pb], cbf_dep], [pnd])
                    S.op("act", o_act(rcp[pb], pn[:, :], AF.Ln), reads=[pnd], writes=[rcpd[pb]])
                    S.op("act", o_act(rcp[pb], rcp[pb], AF.Exp, scale=-1.0), reads=[rcpd[pb]], writes=[rcpd[pb]])
                    for dc in range(2):
                        po, pod = accs[dc]
                        S.op("dve", o_tt(oT(2 * h + dc, tcl), po[:, :], rcp[pb], ALU.mult), reads=[pod, rcpd[pb]], writes=[oTd[2 * h + dc][tcl]])

                for idx in range(len(items_l) + 1):
                    if idx < len(items_l):
                        qk(idx)
                    if idx >= 1:
                        pv(idx - 1)
                if hf == 0:
                    prenorm(L, "g_cross_pre", NT, hT, hd)
                outproj_postnorm(L, T0, std_panels(L, "xo", oT, oTd), "g_cross_post", yv, yd)

        def ffn(L):
            SB = A_SB
            hTv = vb(SB + 0, 4096)
            actv = vb(SB + 4096, NFF * NT // 2)
            sg = [vf(SB + 15360 + i * 512, 512) for i in range(2)]
            yv_ = vf(SB + 16384, 8192)
            hd = [[Dep() for _ in range(2)] for _ in range(8)]
            actd = [[Dep() for _ in range(2)] for _ in range(NFF)]
            sgd = [Dep(), Dep()]
            yd = [[Dep() for _ in range(2)] for _ in range(8)]

            def hT(kc, tcl):
                return hTv[:, kc * NT + tcl * TC: kc * NT + (tcl + 1) * TC]

            def act(f, tcl):
                return actv[:, f * NT + tcl * TC: f * NT + (tcl + 1) * TC]

            def yv(c, tcl):
                return yv_[:, c * NT + tcl * TC: c * NT + (tcl + 1) * TC]

            S.barrier()
            k = 0
            prenorm(L, "g_ffn_pre", 0, hT, hd)
            for hf in range(2):
                T0 = hf * NT
                for f in range(NFF):
                    w, wd = wget(L, ("gu", f))
                    for tcl in range(2):
                        psg, pdg = psum()
                        items, reads = proj_fm(w, wd, 0, 256, hT, hd, tcl, psg)
                        mm(items, reads, [pdg])
                        psu, pdu = psum()
                        items, reads = proj_fm(w, wd, 128, 256, hT, hd, tcl, psu)
                        mm(items, reads, [pdu])
                        b = k % 2
                        k += 1
                        S.op("act", o_act(sg[b], psg[:, :], AF.Silu), reads=[pdg], writes=[sgd[b]])
                        S.op("dve", o_tt(act(f, tcl), psu[:, :], sg[b], ALU.mult), reads=[pdu, sgd[b]], writes=[actd[f][tcl]])

                def mk(oc):
                    def getp(tcl, ps, _c={}):
                        if "w" not in _c:
                            _c["w"] = wget(L, ("dn", oc))
                        w, wd = _c["w"]
                        items = [(ps[:, :], w[:, fc * 128:(fc + 1) * 128], act(fc, tcl)) for fc in range(NFF)]
                        return items, [wd] + [actd[fc][tcl] for fc in range(NFF)]
                    return getp
                if hf == 0:
                    prenorm(L, "g_ffn_pre", NT, hT, hd)
                outproj_postnorm(L, T0, [mk(oc) for oc in range(8)], "g_ffn_post", yv, yd)

        dmem = S.dsem("mem")
        for li, L in enumerate(layers if _stage != "io" else []):
            if li > 0:
                S.barrier()
                S.new_epoch()
            if L % 2 == 0:
                even_mixer(L)
            else:
                odd_mixer(L)
            if _stage in ("pre", "proj", "attn", "mix", "attn_prep", "attn_qk", "attn_qk1", "attn_qk2", "mix0", "mixop", "op0", "op0c", "op1", "op2"):
                break
            cross(L, dmem)
            if _stage == "cross":
                break
            ffn(L)
        assert _stage != "" or ws["use"] == len(order)

        d_out = S.dsem("out")
        for c in range(8):
            S.dma("sp", out_d[c * 128:(c + 1) * 128, :], XT(c, 0, T), d_out, reads=xd[c])
        S.wait_tok("sp", d_out[0], d_out[1])
        S.finish()
    return nc


def _kp(Wm):
    K, n = Wm.shape
    return np.ascontiguousarray(Wm.reshape(K // 128, 128, n).transpose(1, 0, 2)).reshape(128, -1)


def _panel(inp, L, key):
    e = L // 2
    o = L // 2
    k0 = key[0]
    if k0 == "abq":
        return _kp(inp["ab_w_in"][e][:, 0:512])
    if k0 == "abk":
        return _kp(inp["ab_w_in"][e][:, 512:1024])
    if k0 == "abv":
        return _kp(inp["ab_w_in"][e][:, 1024:1536])
    if k0 == "abc":
        i = key[1]
        Wm = inp["ab_w_in"][e]
        cat = np.concatenate([Wm[:, 1544 + i * 128:1544 + (i + 1) * 128], Wm[:, 2056 + i * 128:2056 + (i + 1) * 128],
                              Wm[:, 2568 + i * 128:2568 + (i + 1) * 128], Wm[:, 1536:1544]], axis=1)
        return _kp(cat)
    if k0 == "abo":
        j = key[1]
        return _kp(inp["ab_w_out"][e][:, j * 512:(j + 1) * 512])
    if k0 == "cin":
        n = key[1]
        Wm = inp["c_w_in"][o]
        cat = np.concatenate([Wm[:, n * 256:(n + 1) * 256], Wm[:, 1024 + n * 256:1024 + (n + 1) * 256]], axis=1)
        return _kp(cat)
    if k0 == "cg":
        n = key[1]
        return np.concatenate([_kp(inp["c_w_a"][o][n]), _kp(inp["c_w_i"][o][n])], axis=1)
    if k0 == "co":
        j = key[1]
        return _kp(inp["c_w_out"][o][:, j * 512:(j + 1) * 512])
    if k0 == "xk":
        j = key[1]
        return _kp(inp["w_xkv"][L][:, j * 512:(j + 1) * 512])
    if k0 == "xv":
        j = key[1]
        return _kp(inp["w_xkv"][L][:, 1024 + j * 512:1024 + (j + 1) * 512])
    if k0 == "xq":
        j = key[1]
        return _kp(inp["w_xq"][L][:, j * 512:(j + 1) * 512])
    if k0 == "xo":
        j = key[1]
        return _kp(inp["w_xo"][L][:, j * 512:(j + 1) * 512])
    if k0 == "gu":
        f = key[1]
        Wm = inp["w_ffn_gu"][L]
        cat = np.concatenate([Wm[:, f * 128:(f + 1) * 128], Wm[:, DFF + f * 128:DFF + (f + 1) * 128]], axis=1)
        return _kp(cat)
    if k0 == "dn":
        oc = key[1]
        return _kp(inp["w_ffn_down"][L][:, oc * 128:(oc + 1) * 128])
    raise KeyError(key)


def pack_weights(inp, layers):
    offs, wtot = panel_offsets(layers)
    arrs = {L: np.empty((128, wtot[L]), np.float32) for L in layers}
    seen = set()
    for (L, key, n) in panel_order(layers):
        if (L, key) in seen:
            continue
        seen.add((L, key))
        p = _panel(inp, L, key)
        assert p.shape == (128, n), (key, p.shape, n)
        arrs[L][:, offs[(L, key)]:offs[(L, key)] + n] = p
    return arrs


def pack_vecs(inp):
    v = np.zeros((128, NV), np.float32)

    def fm(a):
        return np.asarray(a, np.float32).reshape(-1, 128).T

    for n in ("g_mix_pre", "g_mix_post", "g_cross_pre", "g_mem", "g_cross_post", "g_ffn_pre", "g_ffn_post"):
        for L in range(DEPTH):
            v[:, VCOL[n] + L * 8: VCOL[n] + L * 8 + 8] = fm(inp[n][L])
    for e in range(2):
        for k in range(3):
            v[:, VCOL["abcw"] + e * 12 + k * 4: VCOL["abcw"] + e * 12 + k * 4 + 4] = fm(inp["ab_conv_w"][e, k])
    for o in range(2):
        for k in range(4):
            v[:, VCOL["ccw"] + o * 32 + k * 8: VCOL["ccw"] + o * 32 + k * 8 + 8] = fm(inp["c_conv_w"][o, k])
        v[:, VCOL["ccb"] + o * 8: VCOL["ccb"] + o * 8 + 8] = fm(inp["c_conv_b"][o])
        v[:, VCOL["cba"] + o * 8: VCOL["cba"] + o * 8 + 8] = fm(np.asarray(inp["c_b_a"][o]).reshape(-1))
        v[:, VCOL["cbi"] + o * 8: VCOL["cbi"] + o * 8 + 8] = fm(np.asarray(inp["c_b_i"][o]).reshape(-1))
        v[:, VCOL["clam"] + o * 8: VCOL["clam"] + o * 8 + 8] = fm(inp["c_lam"][o])
    v[0:8, VCOL["bf"]:VCOL["bf"] + 2] = np.asarray(inp["ab_b_f"], np.float32).T
    v[:, VCOL["one"]] = 1.0
    v[:, VCOL["eps"]] = EPS
    v[0:8, VCOL["id8"]:VCOL["id8"] + 8] = np.eye(8, dtype=np.float32)
    return v


def pack_consts():
    c = np.zeros((128, NCB), np.float32)
    c[:, CB_ID:CB_ID + 128] = np.eye(128, dtype=np.float32)
    kk = np.arange(128)[:, None]
    qq = np.arange(128)[None, :]
    c[:, CB_MASK:CB_MASK + 128] = np.where(kk > qq, -30000.0, 0.0)
    c[:, CB_ONES:CB_ONES + 128] = 1.0
    for h in range(8):
        c[h, CB_SEL + h * 128:CB_SEL + (h + 1) * 128] = 1.0
        c[32 + h, CB_SEL + h * 128:CB_SEL + (h + 1) * 128] = 1.0
        c[64 + h, CB_SEL + h * 128:CB_SEL + (h + 1) * 128] = 1.0
        c[96 + h, CB_SEL + h * 128:CB_SEL + (h + 1) * 128] = 1.0
    return c


_PROG_CACHE = {}


def _get_prog(layers):
    key = tuple(layers)
    if key not in _PROG_CACHE:
        _PROG_CACHE[key] = build_program(list(layers))
    return _PROG_CACHE[key]


MODE = "fused"


def kernel(**inputs):
    inp = {k: np.asarray(v) for k, v in inputs.items()}
    x = inp["x"].astype(np.float32, copy=False)
    mem = inp["mem"].astype(np.float32, copy=False)
    B = x.shape[0]
    vecs = pack_vecs(inp)
    cbf = pack_consts()
    xT = [np.ascontiguousarray(x[b].T) for b in range(B)]
    memT = [np.ascontiguousarray(mem[b].T) for b in range(B)]
    groups = [[L] for L in range(DEPTH)] if MODE == "per_layer" else [list(range(DEPTH))]
    for layers in groups:
        nc = _get_prog(layers)
        warr = pack_weights(inp, layers)
        in_maps = []
        for b in range(B):
            m = {"xT": xT[b], "memT": memT[b], "vecs": vecs, "cbf": cbf}
            for L in layers:
                m[f"w{L}"] = warr[L]
            in_maps.append(m)
        res = run_bass_kernel_spmd(nc, in_maps, core_ids=list(range(B)))
        xT = [np.asarray(res.results[b]["outT"], np.float32) for b in range(B)]
    out = np.stack([xT[b].T for b in range(B)], axis=0)
    return np.ascontiguousarray(out.astype(np.float32))
```

```python
import numpy as np
from contextlib import ExitStack
import concourse.bass as bass
import concourse.mybir as mybir
from concourse.bass_utils import run_bass_kernel_spmd

F32 = mybir.dt.float32
BF16 = mybir.dt.bfloat16
AF = mybir.ActivationFunctionType
ALU = mybir.AluOpType

DEPTH = 4
D = 1024
T = 2048
NT = 1024
TC = 512
MEM = 256
DFF = 2816
NFF = 22
EPS = 1e-6
NSLOT = 3
SLOT_W = 2048
NG = 4

VCOL = {}
_c = 0
for _n in ("g_mix_pre", "g_mix_post", "g_cross_pre", "g_mem", "g_cross_post", "g_ffn_pre", "g_ffn_post"):
    VCOL[_n] = _c
    _c += 32
VCOL["abcw"] = _c; _c += 24
VCOL["ccw"] = _c; _c += 64
for _n in ("ccb", "cba", "cbi", "clam"):
    VCOL[_n] = _c
    _c += 16
VCOL["bf"] = _c; _c += 2
VCOL["one"] = _c; _c += 1
VCOL["eps"] = _c; _c += 1
VCOL["zero"] = _c; _c += 1
VCOL["id8"] = _c; _c += 8
NV = _c + (_c % 2)

CB_ID, CB_MASK, CB_ONES, CB_SEL = 0, 128, 256, 384
NCB = 384 + 1024


def panel_order(layers):
    order = []
    for L in layers:
        if L % 2 == 0:
            for hf in range(2):
                order += [(L, ("abq",), 4096), (L, ("abk",), 4096), (L, ("abv",), 4096)]
                order += [(L, ("abc", i), 8 * 392) for i in range(4)]
                order += [(L, ("abo", j), 4096) for j in range(2)]
        else:
            for hf in range(2):
                order += [(L, ("cin", 0), 4096)]
                for n in range(4):
                    if n < 3:
                        order += [(L, ("cin", n + 1), 4096)]
                    order += [(L, ("cg", n), 1024)]
                order += [(L, ("co", j), 4096) for j in range(2)]
        order += [(L, ("xk", j), 4096) for j in range(2)]
        order += [(L, ("xv", j), 4096) for j in range(2)]
        for hf in range(2):
            order += [(L, ("xq", j), 4096) for j in range(2)]
            order += [(L, ("xo", j), 4096) for j in range(2)]
        for hf in range(2):
            order += [(L, ("gu", f), 2048) for f in range(NFF)]
            order += [(L, ("dn", oc), NFF * 128) for oc in range(8)]
    return order


def panel_offsets(layers):
    offs = {}
    tot = {L: 0 for L in layers}
    for (L, key, n) in panel_order(layers):
        if (L, key) not in offs:
            offs[(L, key)] = tot[L]
            tot[L] += n
    return offs, tot


class Dep:
    __slots__ = ("w", "r")

    def __init__(self):
        self.w = None
        self.r = {}


class Eng:
    def __init__(self, name):
        self.name = name
        self.ops = []
        self.sem = None
        self.cnt = 0
        self.seen = {}


class Sched:
    def __init__(self, nc, stack):
        self.nc = nc
        self.stack = stack
        self.engs = {n: Eng(n) for n in ("pe", "act", "dve", "pool", "sp")}
        self.nsem = 0
        self.new_epoch()

    def new_sem(self, name):
        self.nsem += 1
        return self.stack.enter_context(self.nc.semaphore(f"{name}_{self.nsem}"))

    def new_epoch(self):
        for e in self.engs.values():
            e.sem = self.new_sem("e_" + e.name)
            e.cnt = 0

    def _waits(self, eng, reads, writes):
        need = {}

        def add(tok, raw):
            if tok is None:
                return
            sem, val, src = tok
            if src is eng and eng.name == "pe":
                return
            k = id(sem)
            if eng.seen.get(k, 0) >= val:
                return
            if k not in need or need[k][1] < val:
                need[k] = (sem, val)

        for d in reads:
            add(d.w, True)
        for d in writes:
            add(d.w, False)
            for tok in d.r.values():
                add(tok, False)
        for k, (sem, val) in need.items():
            eng.seen[k] = val
            eng.ops.append(("wait", sem, val))

    def op(self, engname, fn, reads=(), writes=(), inc=True):
        eng = self.engs[engname]
        self._waits(eng, reads, writes)
        if inc:
            eng.cnt += 1
            tok = (eng.sem, eng.cnt, eng)
        else:
            tok = (eng.sem, eng.cnt + 1, eng)
        eng.ops.append(("op", fn, eng.sem if inc else None, 1))
        for d in reads:
            d.r[id(eng.sem)] = tok
        for d in writes:
            d.w = tok
            d.r = {}

    def dma(self, engname, out, in_, dsem, reads=(), writes=()):
        eng = self.engs[engname]
        self._waits(eng, reads, writes)
        dsem[1] += 16
        tok = (dsem[0], dsem[1], None)
        eng.ops.append(("op", lambda e, o=out, i=in_: e.dma_start(out=o, in_=i), dsem[0], 16))
        for d in reads:
            d.r[id(dsem[0])] = tok
        for d in writes:
            d.w = tok
            d.r = {}

    def dsem(self, name):
        return [self.new_sem("d_" + name), 0]

    def wait_tok(self, engname, sem, val):
        self.engs[engname].ops.append(("wait", sem, val))

    def barrier(self):
        for e in self.engs.values():
            for e2 in self.engs.values():
                if e2 is e or e2.cnt == 0:
                    continue
                k = id(e2.sem)
                if e.seen.get(k, 0) >= e2.cnt:
                    continue
                e.seen[k] = e2.cnt
                e.ops.append(("wait", e2.sem, e2.cnt))

    def finish(self):
        nc = self.nc
        with nc.Block() as block:
            def runner(eng):
                def f(e):
                    for o in eng.ops:
                        if o[0] == "wait":
                            e.wait_ge(o[1], o[2])
                        else:
                            ins = o[1](e)
                            if o[2] is not None:
                                ins.then_inc(o[2], o[3])
                return f
            block.tensor(runner(self.engs["pe"]))
            block.scalar(runner(self.engs["act"]))
            block.vector(runner(self.engs["dve"]))
            block.gpsimd(runner(self.engs["pool"]))
            block.sync(runner(self.engs["sp"]))


def o_act(out, in_, func, **kw):
    return lambda e: e.activation(out=out, in_=in_, func=func, **kw)


def o_ts(out, in0, s1, s2, op0, op1=None):
    if op1 is None:
        return lambda e: e.tensor_scalar(out=out, in0=in0, scalar1=s1, scalar2=None, op0=op0)
    return lambda e: e.tensor_scalar(out=out, in0=in0, scalar1=s1, scalar2=s2, op0=op0, op1=op1)


def o_tt(out, in0, in1, op):
    return lambda e: e.tensor_tensor(out=out, in0=in0, in1=in1, op=op)


def o_stt(out, in0, scalar, in1, op0, op1):
    return lambda e: e.scalar_tensor_tensor(out=out, in0=in0, scalar=scalar, in1=in1, op0=op0, op1=op1)


def o_copy(out, in_):
    return lambda e: e.tensor_copy(out=out, in_=in_)


def o_recip(out, in_):
    return lambda e: e.reciprocal(out=out, in_=in_)


def o_memset(ap, v):
    return lambda e: e.memset(ap, v)


def o_scan(out, d0, d1, init):
    return lambda e: e.tensor_tensor_scan(out=out, data0=d0, data1=d1, initial=init, op0=ALU.mult, op1=ALU.add)


def o_mm(out, lhsT, rhs, start, stop):
    return lambda e: e.matmul(out, lhsT, rhs, start=start, stop=stop)


def build_program(layers):
    offs, wtot = panel_offsets(layers)
    order = panel_order(layers)
    nc = bass.Bass("TRN2", target_bir_lowering=False)
    xin = nc.dram_tensor("xT", [D, T], F32, kind="ExternalInput").ap()
    memin = nc.dram_tensor("memT", [D, MEM], F32, kind="ExternalInput").ap()
    vecs_d = nc.dram_tensor("vecs", [128, NV], F32, kind="ExternalInput").ap()
    cbf_d = nc.dram_tensor("cbf", [128, NCB], F32, kind="ExternalInput").ap()
    wl_d = {L: nc.dram_tensor(f"w{L}", [128, wtot[L]], F32, kind="ExternalInput").ap() for L in layers}
    out_d = nc.dram_tensor("outT", [D, T], F32, kind="ExternalOutput").ap()

    A_XT = 0
    A_SLOT = A_XT + 8 * T
    A_CBF = A_SLOT + NSLOT * SLOT_W
    A_VEC = A_CBF + NCB // 2
    A_ONEF = A_VEC + NV
    A_SQ = A_ONEF + 512
    A_RS = A_SQ + 2 * 256
    A_SB = A_RS + 2 * 512
    SCR = 26912
    AW = A_SB + SCR

    with ExitStack() as st:
        S = Sched(nc, st)
        arena = st.enter_context(nc.sbuf_tensor("arena", [128, AW], F32))
        psb = [st.enter_context(nc.psum_tensor(f"ps{i}", [128, 512], F32)) for i in range(8)]
        psd = [Dep() for _ in range(8)]
        rr_state = {"g": 0}

        def psum():
            i = rr_state["g"]
            rr_state["g"] = (i + 1) % NG
            return psb[i], psd[i]

        def vf(off, n):
            return arena[:, off:off + n]

        def vb(off, nwords):
            return arena[:, off:off + nwords].bitcast(BF16)

        def XT(c, t0, n):
            return arena[:, A_XT + c * T + t0: A_XT + c * T + t0 + n]

        xd = [[Dep() for _ in range(4)] for _ in range(8)]
        slot_bf = [vb(A_SLOT + s * SLOT_W, SLOT_W) for s in range(NSLOT)]
        slot_dep = [Dep() for _ in range(NSLOT)]
        slot_sem = [S.dsem(f"slot{s}") for s in range(NSLOT)]
        cbf = vb(A_CBF, NCB // 2)
        cbf_dep = Dep()
        vecs = vf(A_VEC, NV)
        vec_dep = Dep()
        onef = vf(A_ONEF, 512)
        onef_dep = Dep()
        sqb = [vb(A_SQ + i * 256, 256) for i in range(2)]
        sqd = [Dep() for _ in range(2)]
        rsb = [vf(A_RS + i * 512, 512) for i in range(2)]
        rsd = [Dep() for _ in range(2)]
        cnt = {"sq": 0, "rs": 0}

        ident = cbf[:, CB_ID:CB_ID + 128]
        negmask = cbf[:, CB_MASK:CB_MASK + 128]
        ones_bf = cbf[:, CB_ONES:CB_ONES + 128]

        def vc(name, idx=0, p0=0, p1=128):
            c = VCOL[name] + idx
            return vecs[p0:p1, c:c + 1]

        _stage = ""
        ws = {"issue": 0, "use": 0}

        def wget(L, key):
            idx = ws["use"]
            assert order[idx][0] == L and order[idx][1] == key, (order[idx], L, key)
            lim = min(len(order), idx + NSLOT)
            while ws["issue"] < lim:
                q = ws["issue"]
                Lq, kq, nq = order[q]
                s = q % NSLOT
                off = offs[(Lq, kq)]
                S.dma("pool", slot_bf[s][:, 0:nq], wl_d[Lq][:, off:off + nq], slot_sem[s], writes=[slot_dep[s]])
                ws["issue"] += 1
            ws["use"] += 1
            return slot_bf[idx % NSLOT], slot_dep[idx % NSLOT]

        def mm(items, reads, wdeps, start=True, stop=True):
            n = len(items)
            for i, (o, l, r) in enumerate(items):
                S.op("pe", o_mm(o, l, r, start and i == 0, stop and i == n - 1),
                     reads=reads if i == 0 else (), writes=wdeps if i == 0 else (), inc=(i == n - 1))

        d_in = S.dsem("in")
        S.dma("sp", vecs, vecs_d, d_in, writes=[vec_dep])
        d_cb = S.dsem("cb")
        S.dma("pool", cbf, cbf_d, d_cb, writes=[cbf_dep])
        d_x = S.dsem("x")
        for c in range(8):
            S.dma("sp", XT(c, 0, T), xin[c * 128:(c + 1) * 128, :], d_x, writes=xd[c])
        for c in range(8):
            for dd in xd[c]:
                dd.w = (d_x[0], d_x[1], None)
        S.op("dve", o_memset(onef, 1.0), writes=[onef_dep])

        def rstd_from(ps, pd, ncol):
            i = cnt["rs"] % 2
            cnt["rs"] += 1
            rs, rd = rsb[i], rsd[i]
            S.op("act", o_act(rs[:, 0:ncol], ps[:, 0:ncol], AF.Ln, scale=1.0 / D, bias=vc("eps")), reads=[pd, vec_dep], writes=[rd])
            S.op("act", o_act(rs[:, 0:ncol], rs[:, 0:ncol], AF.Exp, scale=-0.5), reads=[rd], writes=[rd])
            return rs, rd

        def prenorm(L, gname, T0, hT, hd):
            for tcl in range(2):
                tg = T0 + tcl * TC
                tcg = tg // TC
                ps, pd = psum()
                for c in range(8):
                    i = cnt["sq"] % 2
                    cnt["sq"] += 1
                    S.op("act", o_act(sqb[i], XT(c, tg, TC), AF.Square), reads=[xd[c][tcg]], writes=[sqd[i]])
                    mm([(ps[:, :], ones_bf, sqb[i])], [sqd[i], cbf_dep], [pd], start=(c == 0), stop=(c == 7))
                rs, rd = rstd_from(ps, pd, TC)
                for c in range(8):
                    S.op("dve", o_stt(hT(c, tcl), XT(c, tg, TC), vc(gname, L * 8 + c), rs, ALU.mult, ALU.mult),
                         reads=[xd[c][tcg], rd, vec_dep], writes=[hd[c][tcl]])

        def outproj_postnorm(L, T0, panels, gname, yv, yd):
            st_ps = [(psb[4], psd[4]), (psb[5], psd[5])]
            pending = []

            def flush():
                if _stage in ("op0", "op0c", "op1"):
                    pending.clear()
                while pending:
                    oc_, tcl_, sqi = pending.pop(0)
                    mm([(st_ps[tcl_][0][:, :], ones_bf, sqb[sqi])], [sqd[sqi], cbf_dep], [st_ps[tcl_][1]],
                       start=(oc_ == 0), stop=(oc_ == 7))

            for oc in range(8):
                getp = panels[oc]
                for tcl in range(2):
                    ps, pd = psum()
                    items, reads = getp(tcl, ps)
                    mm(items, reads, [pd])
                    flush()
                    if _stage == "op0":
                        continue
                    S.op("dve", o_copy(yv(oc, tcl), ps[:, :]), reads=[pd], writes=[yd[oc][tcl]])
                    if _stage == "op0c":
                        continue
                    i = cnt["sq"] % 2
                    cnt["sq"] += 1
                    S.op("pool", o_tt(sqb[i], yv(oc, tcl), yv(oc, tcl), ALU.mult), reads=[yd[oc][tcl]], writes=[sqd[i]])
                    pending.append((oc, tcl, i))
            flush()
            if _stage in ("op0", "op0c", "op1", "op2"):
                return
            for tcl in range(2):
                tg = T0 + tcl * TC
                tcg = tg // TC
                rs, rd = rstd_from(st_ps[tcl][0], st_ps[tcl][1], TC)
                for c in range(8):
                    S.op("pool" if c % 2 == 0 else "dve", o_tt(yv(c, tcl), yv(c, tcl), rs, ALU.mult), reads=[yd[c][tcl], rd], writes=[yd[c][tcl]])
                    S.op("dve", o_stt(XT(c, tg, TC), yv(c, tcl), vc(gname, L * 8 + c), XT(c, tg, TC), ALU.mult, ALU.add),
                         reads=[yd[c][tcl], vec_dep, xd[c][tcg]], writes=[xd[c][tcg]])

        def std_panels(L, kname, src, srcd):
            cache = {}

            def mk(oc):
                def getp(tcl, ps):
                    j, ol = oc // 4, oc % 4
                    if (j) not in cache:
                        cache.clear()
                        cache[j] = wget(L, (kname, j))
                    w, wd = cache[j]
                    items = [(ps[:, :], w[:, kc * 512 + ol * 128: kc * 512 + (ol + 1) * 128], src(kc, tcl)) for kc in range(8)]
                    return items, [wd] + [srcd[kc][tcl] for kc in range(8)]
                return getp
            return [mk(oc) for oc in range(8)]

        def proj_fm(w, wd, coff, ncols_panel, hT, hd, tcl, ps, m=128):
            items = [(ps[0:m, :], w[:, kc * ncols_panel + coff: kc * ncols_panel + coff + m], hT(kc, tcl)) for kc in range(8)]
            return items, [wd] + [hd[kc][tcl] for kc in range(8)]

        def even_mixer(L):
            e = L // 2
            SB = A_SB
            kTv = vb(SB + 0, 4096)
            Vv = vb(SB + 4096, 4096)
            Nf = vf(SB + 8192, 2048)
            halo = vf(SB + 10240, 8)
            negbf = vf(SB + 10248, 2)
            biasT = [vf(SB + 10256 + i * 128, 128) for i in range(2)]
            hTv = vb(SB + 10512, 4096)
            qTv = vb(SB + 14608, 2048)
            cu = vf(SB + 16656, 1028)
            Cc = vf(SB + 17684, 512)
            tcv = vf(SB + 18196, 512)
            yv_ = vf(SB + 10512, 8192)
            yTv = vb(SB + 18708, 4096)
            PT = [vb(SB + 22804 + i * 256, 256) for i in range(2)]
            rcp = [vf(SB + 23316 + i * 512, 512) for i in range(2)]
            tmp = [vf(SB + 24340 + i * 512, 512) for i in range(4)]
            cq = [vb(SB + 26388 + i * 256, 256) for i in range(2)]
            kd = [[Dep() for _ in range(4)] for _ in range(4)]
            vd = [Dep() for _ in range(16)]
            Nd = [Dep() for _ in range(4)]
            halod = [Dep() for _ in range(4)]
            negbfd = Dep()
            biasTd = [Dep(), Dep()]
            hd = [[Dep() for _ in range(2)] for _ in range(8)]
            qd = [[Dep() for _ in range(2)] for _ in range(4)]
            cud, Ccd, tcd = Dep(), Dep(), Dep()
            yd = [[Dep() for _ in range(2)] for _ in range(8)]
            yTd = [[Dep() for _ in range(2)] for _ in range(8)]
            PTd = [Dep(), Dep()]
            rcpd = [Dep(), Dep()]
            tmpd = [Dep() for _ in range(4)]
            cqd = [Dep(), Dep()]

            def hT(kc, tcl):
                return hTv[:, kc * NT + tcl * TC: kc * NT + (tcl + 1) * TC]

            def qT(i, tcl):
                return qTv[:, i * NT + tcl * TC: i * NT + (tcl + 1) * TC]

            def kT(i, t0, n):
                return kTv[:, i * T + t0: i * T + t0 + n]

            def Vb(tb):
                return Vv[:, tb * 512:(tb + 1) * 512]

            def yT(c, tcl):
                return yTv[:, c * NT + tcl * TC: c * NT + (tcl + 1) * TC]

            def yv(c, tcl):
                return yv_[:, c * NT + tcl * TC: c * NT + (tcl + 1) * TC]

            S.barrier()
            S.op("dve", o_ts(negbf[0:8, :], vecs[0:8, VCOL["bf"]:VCOL["bf"] + 2], -1.0, None, ALU.mult),
                 reads=[vec_dep], writes=[negbfd])
            for i in range(2):
                S.op("pool", o_memset(cq[i], 0.0), writes=[cqd[i]])

            for hf in range(2):
                T0 = hf * NT
                if hf == 1:
                    S.barrier()
                prenorm(L, "g_mix_pre", T0, hT, hd)
                if _stage == "pre":
                    return
                w, wd = wget(L, ("abq",))
                for i in range(4):
                    for tcl in range(2):
                        ps, pd = psum()
                        items, reads = proj_fm(w, wd, i * 128, 512, hT, hd, tcl, ps)
                        mm(items, reads, [pd])
                        S.op("act", o_act(qT(i, tcl), ps[:, :], AF.Copy), reads=[pd], writes=[qd[i][tcl]])
                w, wd = wget(L, ("abk",))
                for i in range(4):
                    for tcl in range(2):
                        ps, pd = psum()
                        items, reads = proj_fm(w, wd, i * 128, 512, hT, hd, tcl, ps)
                        mm(items, reads, [pd])
                        S.op("dve", o_copy(kT(i, T0 + tcl * TC, TC), ps[:, :]), reads=[pd], writes=[kd[i][hf * 2 + tcl]])
                w, wd = wget(L, ("abv",))
                for tb in range(8):
                    tcl = tb // 4
                    ps, pd = psum()
                    items = [(ps[:, :], hT(kc, tcl)[:, (tb % 4) * 128:(tb % 4 + 1) * 128], w[:, kc * 512:(kc + 1) * 512]) for kc in range(8)]
                    mm(items, [wd] + [hd[kc][tcl] for kc in range(8)], [pd])
                    eng = "act" if tb % 2 == 0 else "dve"
                    if eng == "act":
                        S.op("act", o_act(Vb(hf * 8 + tb), ps[:, :], AF.Copy), reads=[pd], writes=[vd[hf * 8 + tb]])
                    else:
                        S.op("dve", o_copy(Vb(hf * 8 + tb), ps[:, :]), reads=[pd], writes=[vd[hf * 8 + tb]])
                for i in range(4):
                    w, wd = wget(L, ("abc", i))
                    if i == 0:
                        for tcl in range(2):
                            cg = hf * 2 + tcl
                            tg = T0 + tcl * TC
                            ps, pd = psum()
                            items, reads = proj_fm(w, wd, 384, 392, hT, hd, tcl, ps, m=8)
                            mm(items, reads, [pd])
                            tA, tB = tmp[0], tmp[1]
                            S.op("act", o_act(tA[0:8, :], ps[0:8, :], AF.Exp, scale=-1.0, bias=negbf[0:8, e:e + 1]),
                                 reads=[pd, negbfd], writes=[tmpd[0]])
                            S.op("act", o_act(tB[0:8, :], tA[0:8, :], AF.Ln, scale=1.0, bias=vc("one", 0, 0, 8)),
                                 reads=[tmpd[0], vec_dep], writes=[tmpd[1]])
                            init = Nf[0:8, tg - 1:tg] if cg > 0 else 0.0
                            S.op("dve", o_scan(Nf[0:8, tg:tg + TC], onef[0:8, :], tB[0:8, :], init),
                                 reads=[tmpd[1], onef_dep] + ([Nd[cg - 1]] if cg > 0 else []), writes=[Nd[cg]])
                    if hf == 0:
                        S.op("dve", o_memset(cu[:, 0:2], 0.0), writes=[cud])
                    else:
                        S.op("dve", o_copy(cu[:, 0:2], halo[:, 2 * i:2 * i + 2]), reads=[halod[i]], writes=[cud])
                    for tcl in range(2):
                        psB, pdB = psum()
                        items, reads = proj_fm(w, wd, 0, 392, hT, hd, tcl, psB)
                        mm(items, reads, [pdB])
                        psC, pdC = psum()
                        items, reads = proj_fm(w, wd, 128, 392, hT, hd, tcl, psC)
                        mm(items, reads, [pdC])
                        psU, pdU = psum()
                        items, reads = proj_fm(w, wd, 256, 392, hT, hd, tcl, psU)
                        mm(items, reads, [pdU])
                        S.op("act", o_act(Cc, psC[:, :], AF.Copy), reads=[pdC], writes=[Ccd])
                        b0 = tcl * TC
                        S.op("dve", o_tt(cu[:, 2 + b0:2 + b0 + TC], psU[:, :], Cc, ALU.mult), reads=[pdU, Ccd], writes=[cud])
                        cw = VCOL["abcw"] + e * 12 + i
                        S.op("dve", o_ts(tcv, cu[:, b0:b0 + TC], vecs[:, cw:cw + 1], None, ALU.mult),
                             reads=[cud, vec_dep], writes=[tcd])
                        S.op("dve", o_stt(tcv, cu[:, b0 + 1:b0 + 1 + TC], vecs[:, cw + 4:cw + 5], tcv, ALU.mult, ALU.add),
                             reads=[cud, tcd], writes=[tcd])
                        S.op("dve", o_stt(tcv, cu[:, b0 + 2:b0 + 2 + TC], vecs[:, cw + 8:cw + 9], tcv, ALU.mult, ALU.add),
                             reads=[cud, tcd], writes=[tcd])
                        S.op("dve", o_tt(yT(4 + i, tcl), psB[:, :], tcv, ALU.mult), reads=[pdB, tcd], writes=[yTd[4 + i][tcl]])
                    if hf == 0:
                        S.op("dve", o_copy(halo[:, 2 * i:2 * i + 2], cu[:, NT:NT + 2]), reads=[cud], writes=[halod[i]])

                if _stage == "proj":
                    return
                for tcl in range(2):
                    cg = hf * 2 + tcl
                    tg = T0 + tcl * TC
                    nkb = 4 * cg + 4
                    Rcol = Nf[0:8, tg + 255:tg + 256]
                    Rdeps = [Nd[cg]]
                    cqb, cqbd = cq[cg % 2], cqd[cg % 2]
                    bT, bTd = biasT[cg % 2], biasTd[cg % 2]
                    psT, pdT = psum()
                    for seg in range(cg + 1):
                        tb_, tbd_ = tmp[2 + seg % 2], tmpd[2 + seg % 2]
                        S.op("dve", o_ts(tb_[0:8, :], Nf[0:8, seg * TC:(seg + 1) * TC], Rcol, None, ALU.subtract),
                             reads=[Nd[seg]] + Rdeps, writes=[tbd_])
                        for jb in range(4):
                            j = seg * 4 + jb
                            mm([(psT[:, j * 8:(j + 1) * 8], tb_[0:8, jb * 128:(jb + 1) * 128], vecs[0:8, VCOL["id8"]:VCOL["id8"] + 8])],
                               [tbd_, vec_dep], [pdT])
                    S.op("dve", o_copy(bT[:, 0:nkb * 8], psT[:, 0:nkb * 8]), reads=[pdT], writes=[bTd])

                    if _stage == "attn_prep":
                        continue
                    items_l = [(h, j) for h in range(8) for j in range(nkb)]
                    state = {}

                    def qk(idx):
                        h, j = items_l[idx]
                        i, hp = h // 2, h % 2
                        jj = j - 4 * cg
                        c0 = 0 if jj <= 0 else jj * 128
                        ps, pd = psum()
                        lo, hi = hp * 64, hp * 64 + 64
                        its = [(ps[:, c0:TC], kT(i, j * 128, 128)[lo:hi, :], qT(i, tcl)[lo:hi, c0:TC])]
                        if jj >= 0 and _stage not in ("attn_qk1", "attn_qk2"):
                            its.append((ps[:, c0:c0 + 128], ident, negmask))
                        mm(its, [kd[i][j // 4], qd[i][tcl], cbf_dep], [pd])
                        pb = idx % 2
                        S.op("act", o_act(PT[pb][:, c0:TC], ps[:, c0:TC], AF.Exp, scale=0.125, bias=bT[:, j * 8 + h:j * 8 + h + 1]),
                             reads=[pd, bTd], writes=[PTd[pb]])
                        state[idx] = (pb, c0)

                    def pv(idx):
                        h, j = items_l[idx]
                        i, hp = h // 2, h % 2
                        pb, c0 = state.pop(idx)
                        so = 4 + 2 * (h % 2)
                        pso, psod, psn, psnd = psb[so], psd[so], psb[so + 1], psd[so + 1]
                        mm([(pso[:, c0:TC], Vb(j)[:, i * 128:(i + 1) * 128], PT[pb][:, c0:TC])], [vd[j], PTd[pb]], [psod],
                           start=(j == 0), stop=(j == nkb - 1))
                        mm([(psn[:, c0:TC], ones_bf, PT[pb][:, c0:TC])], [PTd[pb], cbf_dep], [psnd],
                           start=(j == 0), stop=(j == nkb - 1))
                        if j == nkb - 1:
                            rb = h % 2
                            lo, hi = hp * 64, hp * 64 + 64
                            S.op("dve", o_recip(rcp[rb][lo:hi, :], psn[lo:hi, :]), reads=[psnd], writes=[rcpd[rb]])
                            S.op("dve", o_tt(yT(i, tcl)[lo:hi, :], pso[lo:hi, :], rcp[rb][lo:hi, :], ALU.mult),
                                 reads=[psod, rcpd[rb]], writes=[yTd[i][tcl]])

                    n_it = len(items_l)
                    for idx in range(n_it + 1):
                        if idx < n_it:
                            qk(idx)
                        if idx >= 1 and not _stage.startswith("attn_qk"):
                            pv(idx - 1)

                if _stage in ("attn", "attn_prep", "attn_qk", "attn_qk1", "attn_qk2"):
                    return
                S.barrier()
                if _stage == "mixop":
                    return
                outproj_postnorm(L, T0, std_panels(L, "abo", yT, yTd), "g_mix_post", yv, yd)
                if _stage in ("mix0", "op0", "op0c", "op1", "op2"):
                    return

        def odd_mixer(L):
            o = L // 2
            SB = A_SB
            halo = vf(SB + 0, 24)
            carry = vf(SB + 32, 8)
            sc1 = vf(SB + 40, 8)
            sc2 = vf(SB + 48, 8)
            spt = vf(SB + 56, 8)
            hTv = vb(SB + 128, 4096)
            ubuf = [vf(SB + 4224 + i * 1028, 1028) for i in range(2)]
            uc = [vf(SB + 6280 + i * 1024, 1024) for i in range(2)]
            yv_ = vf(SB + 128, 8192)
            ggv = vb(SB + 8328, 4096)
            yTv = vb(SB + 12424, 4096)
            ucb = [vb(SB + 16520 + i * 512, 512) for i in range(2)]
            rr0, ii0, aa, ss, bb, hs, rr1, ii1 = [vf(SB + 17544 + i * 1024, 1024) for i in range(8)]
            rrs, iis = [rr0, rr1], [ii0, ii1]
            halod = [Dep() for _ in range(8)]
            carryd = [Dep() for _ in range(8)]
            scd = Dep()
            hd = [[Dep() for _ in range(2)] for _ in range(8)]
            ubd = [Dep(), Dep()]
            ucd = [Dep(), Dep()]
            ucbd = [Dep(), Dep()]
            ggd = [[Dep() for _ in range(2)] for _ in range(8)]
            yTd = [[Dep() for _ in range(2)] for _ in range(8)]
            yd = [[Dep() for _ in range(2)] for _ in range(8)]
            aad, ssd, bbd, hsd = [Dep() for _ in range(4)]
            rrds, iids = [Dep(), Dep()], [Dep(), Dep()]

            def hT(kc, tcl):
                return hTv[:, kc * NT + tcl * TC: kc * NT + (tcl + 1) * TC]

            def gg(c, tcl):
                return ggv[:, c * NT + tcl * TC: c * NT + (tcl + 1) * TC]

            def yT(c, tcl):
                return yTv[:, c * NT + tcl * TC: c * NT + (tcl + 1) * TC]

            def yv(c, tcl):
                return yv_[:, c * NT + tcl * TC: c * NT + (tcl + 1) * TC]

            S.barrier()
            lam = vecs[:, VCOL["clam"] + o * 8: VCOL["clam"] + o * 8 + 8]
            S.op("act", o_act(spt, lam, AF.Exp, scale=-1.0), reads=[vec_dep], writes=[scd])
            S.op("act", o_act(spt, spt, AF.Ln, scale=1.0, bias=vc("one")), reads=[scd, vec_dep], writes=[scd])
            S.op("dve", o_ts(sc1, spt, -8.0, None, ALU.mult), reads=[scd], writes=[scd])
            S.op("dve", o_ts(sc2, spt, -16.0, None, ALU.mult), reads=[scd], writes=[scd])

            for hf in range(2):
                T0 = hf * NT
                if hf == 1:
                    S.barrier()
                prenorm(L, "g_mix_pre", T0, hT, hd)
                def in_proj(n):
                    w, wd = wget(L, ("cin", n))
                    for ch in range(2):
                        c = 2 * n + ch
                        if hf == 0:
                            S.op("pool", o_memset(ubuf[ch][:, 0:3], 0.0), writes=[ubd[ch]])
                        else:
                            S.op("pool", o_copy(ubuf[ch][:, 0:3], halo[:, 3 * c:3 * c + 3]), reads=[halod[c]], writes=[ubd[ch]])
                        for tcl in range(2):
                            ps, pd = psum()
                            items, reads = proj_fm(w, wd, ch * 128, 512, hT, hd, tcl, ps)
                            mm(items, reads, [pd])
                            S.op("act", o_act(gg(c, tcl), ps[:, :], AF.Gelu_apprx_tanh), reads=[pd], writes=[ggd[c][tcl]])
                            ps, pd = psum()
                            items, reads = proj_fm(w, wd, 256 + ch * 128, 512, hT, hd, tcl, ps)
                            mm(items, reads, [pd])
                            S.op("dve", o_copy(ubuf[ch][:, 3 + tcl * TC:3 + (tcl + 1) * TC], ps[:, :]), reads=[pd], writes=[ubd[ch]])

                def conv(n):
                    for ch in range(2):
                        c = 2 * n + ch
                        cw = VCOL["ccw"] + o * 32 + c
                        cbias = vc("ccb", o * 8 + c)
                        S.op("act", o_act(uc[ch], ubuf[ch][:, 0:NT], AF.Identity, scale=vecs[:, cw:cw + 1], bias=cbias),
                             reads=[ubd[ch], vec_dep], writes=[ucd[ch]])
                        for k in range(1, 4):
                            S.op("dve", o_stt(uc[ch], ubuf[ch][:, k:k + NT], vecs[:, cw + 8 * k:cw + 8 * k + 1], uc[ch], ALU.mult, ALU.add),
                                 reads=[ubd[ch], ucd[ch]], writes=[ucd[ch]])
                        if hf == 0:
                            S.op("pool", o_copy(halo[:, 3 * c:3 * c + 3], ubuf[ch][:, NT:NT + 3]), reads=[ubd[ch]], writes=[halod[c]])
                        S.op("pool", o_copy(ucb[ch], uc[ch]), reads=[ucd[ch]], writes=[ucbd[ch]])

                def gates(n):
                    wg, wgd = wget(L, ("cg", n))
                    for dch in range(2):
                        c = 2 * n + dch
                        rr, ii, rrd, iid = rrs[dch], iis[dch], rrds[dch], iids[dch]
                        for tcl in range(2):
                            ps, pd = psum()
                            its = [(ps[:, :], wg[:, cc * 256 + dch * 128: cc * 256 + (dch + 1) * 128], ucb[cc][:, tcl * TC:(tcl + 1) * TC]) for cc in range(2)]
                            mm(its, [wgd, ucbd[0], ucbd[1]], [pd])
                            S.op("act", o_act(rr[:, tcl * TC:(tcl + 1) * TC], ps[:, :], AF.Sigmoid, scale=1.0, bias=vc("cba", o * 8 + c)),
                                 reads=[pd, vec_dep], writes=[rrd])
                            ps, pd = psum()
                            its = [(ps[:, :], wg[:, 512 + cc * 256 + dch * 128: 512 + cc * 256 + (dch + 1) * 128], ucb[cc][:, tcl * TC:(tcl + 1) * TC]) for cc in range(2)]
                            mm(its, [wgd, ucbd[0], ucbd[1]], [pd])
                            S.op("act", o_act(ii[:, tcl * TC:(tcl + 1) * TC], ps[:, :], AF.Sigmoid, scale=1.0, bias=vc("cbi", o * 8 + c)),
                                 reads=[pd, vec_dep], writes=[iid])
                        S.op("act", o_act(aa, rr, AF.Exp, scale=sc1[:, c:c + 1]), reads=[rrd, scd], writes=[aad])
                        S.op("act", o_act(ss, rr, AF.Exp, scale=sc2[:, c:c + 1]), reads=[rrd, scd], writes=[ssd])
                        S.op("act", o_act(ss, ss, AF.Sqrt, scale=-1.0, bias=vc("one")), reads=[ssd, vec_dep], writes=[ssd])
                        S.op("pool", o_tt(bb, ii, uc[dch], ALU.mult), reads=[iid, ucd[dch]], writes=[bbd])
                        S.op("dve", o_tt(bb, bb, ss, ALU.mult), reads=[bbd, ssd], writes=[bbd])
                        init = carry[:, c:c + 1] if hf == 1 else 0.0
                        S.op("dve", o_scan(hs, aa, bb, init), reads=[aad, bbd] + ([carryd[c]] if hf == 1 else []), writes=[hsd])
                        if hf == 0:
                            S.op("dve", o_copy(carry[:, c:c + 1], hs[:, NT - 1:NT]), reads=[hsd], writes=[carryd[c]])
                        for tcl in range(2):
                            S.op("dve", o_tt(yT(c, tcl), gg(c, tcl), hs[:, tcl * TC:(tcl + 1) * TC], ALU.mult),
                                 reads=[ggd[c][tcl], hsd], writes=[yTd[c][tcl]])

                in_proj(0)
                conv(0)
                for n in range(4):
                    if n < 3:
                        in_proj(n + 1)
                    gates(n)
                    if n < 3:
                        conv(n + 1)
                S.barrier()
                outproj_postnorm(L, T0, std_panels(L, "co", yT, yTd), "g_mix_post", yv, yd)

        def cross(L, dmem):
            SB = A_SB
            kxv = vb(SB + 0, 1024)
            Vxv = vb(SB + 1024, 1024)
            memv = vf(SB + 2048, 2048)
            mTv = vb(SB + 4096, 1024)
            hTv = vb(SB + 2048, 4096)
            qxv = vb(SB + 6144, 4096)
            yv_ = vf(SB + 16384, 8192)
            oTv = vb(SB + 10240, 4096)
            PT = [vb(SB + 14336 + i * 512, 512) for i in range(2)]
            rcp = [vf(SB + 15360 + i * 512, 512) for i in range(2)]
            kxd = [Dep() for _ in range(8)]
            Vxd = [[Dep() for _ in range(2)] for _ in range(2)]
            memd = [Dep() for _ in range(8)]
            mTd = [Dep() for _ in range(8)]
            hd = [[Dep() for _ in range(2)] for _ in range(8)]
            qxd = [[Dep() for _ in range(2)] for _ in range(8)]
            oTd = [[Dep() for _ in range(2)] for _ in range(8)]
            yd = [[Dep() for _ in range(2)] for _ in range(8)]
            PTd = [Dep(), Dep()]
            rcpd = [Dep(), Dep()]

            def hT(kc, tcl):
                return hTv[:, kc * NT + tcl * TC: kc * NT + (tcl + 1) * TC]

            def qx(c, tcl):
                return qxv[:, c * NT + tcl * TC: c * NT + (tcl + 1) * TC]

            def oT(c, tcl):
                return oTv[:, c * NT + tcl * TC: c * NT + (tcl + 1) * TC]

            def yv(c, tcl):
                return yv_[:, c * NT + tcl * TC: c * NT + (tcl + 1) * TC]

            def memc(c):
                return memv[:, c * MEM:(c + 1) * MEM]

            def mT(c):
                return mTv[:, c * MEM:(c + 1) * MEM]

            def kx(c):
                return kxv[:, c * MEM:(c + 1) * MEM]

            S.barrier()
            for c in range(8):
                S.dma("sp", memc(c), memin[c * 128:(c + 1) * 128, :], dmem, writes=[memd[c]])
            for c in range(8):
                memd[c].w = (dmem[0], dmem[1], None)
            ps, pd = psum()
            for c in range(8):
                i = cnt["sq"] % 2
                cnt["sq"] += 1
                S.op("act", o_act(sqb[i][:, 0:MEM], memc(c), AF.Square), reads=[memd[c]], writes=[sqd[i]])
                mm([(ps[:, 0:MEM], ones_bf, sqb[i][:, 0:MEM])], [sqd[i], cbf_dep], [pd], start=(c == 0), stop=(c == 7))
            rs, rd = rstd_from(ps, pd, MEM)
            for c in range(8):
                S.op("dve", o_stt(mT(c), memc(c), vc("g_mem", L * 8 + c), rs[:, 0:MEM], ALU.mult, ALU.mult),
                     reads=[memd[c], rd, vec_dep], writes=[mTd[c]])
            for j in range(2):
                w, wd = wget(L, ("xk", j))
                for cl in range(4):
                    c = 4 * j + cl
                    ps, pd = psum()
                    its = [(ps[:, 0:MEM], w[:, kc * 512 + cl * 128: kc * 512 + (cl + 1) * 128], mT(kc)) for kc in range(8)]
                    mm(its, [wd] + mTd, [pd])
                    S.op("act", o_act(kx(c), ps[:, 0:MEM], AF.Copy), reads=[pd], writes=[kxd[c]])
            for j in range(2):
                w, wd = wget(L, ("xv", j))
                for blk in range(2):
                    ps, pd = psum()
                    its = [(ps[:, :], mT(kc)[:, blk * 128:(blk + 1) * 128], w[:, kc * 512:(kc + 1) * 512]) for kc in range(8)]
                    mm(its, [wd] + mTd, [pd])
                    S.op("dve", o_copy(Vxv[:, blk * D + j * 512: blk * D + (j + 1) * 512], ps[:, :]), reads=[pd], writes=[Vxd[blk][j]])

            S.barrier()
            prenorm(L, "g_cross_pre", 0, hT, hd)
            for hf in range(2):
                T0 = hf * NT
                for j in range(2):
                    w, wd = wget(L, ("xq", j))
                    for cl in range(4):
                        c = 4 * j + cl
                        for tcl in range(2):
                            ps, pd = psum()
                            items, reads = proj_fm(w, wd, cl * 128, 512, hT, hd, tcl, ps)
                            mm(items, reads, [pd])
                            if (cl + tcl) % 2 == 0:
                                S.op("act", o_act(qx(c, tcl), ps[:, :], AF.Copy), reads=[pd], writes=[qxd[c][tcl]])
                            else:
                                S.op("dve", o_copy(qx(c, tcl), ps[:, :]), reads=[pd], writes=[qxd[c][tcl]])
                items_l = [(tcl, h) for tcl in range(2) for h in range(4)]

                def qk(idx):
                    tcl, h = items_l[idx]
                    pb = idx % 2
                    for kb in range(2):
                        ps, pd = psum()
                        its = [(ps[:, :], kx(2 * h + dc)[:, kb * 128:(kb + 1) * 128], qx(2 * h + dc, tcl)) for dc in range(2)]
                        mm(its, [kxd[2 * h], kxd[2 * h + 1], qxd[2 * h][tcl], qxd[2 * h + 1][tcl]], [pd])
                        S.op("act", o_act(PT[pb][:, kb * TC:(kb + 1) * TC], ps[:, :], AF.Exp, scale=1.0 / 16.0),
                             reads=[pd], writes=[PTd[pb]])

                def pv(idx):
                    tcl, h = items_l[idx]
                    pb = idx % 2
                    so = 4 + 2 * (idx % 2)
                    accs = []
                    for dc in range(2):
                        po, pod = psb[so + dc], psd[so + dc]
                        its = [(po[:, :], Vxv[:, kb * D + h * 256 + dc * 128: kb * D + h * 256 + (dc + 1) * 128], PT[pb][:, kb * TC:(kb + 1) * TC]) for kb in range(2)]
                        mm(its, [Vxd[0][h // 2], Vxd[1][h // 2], PTd[pb]], [pod])
                        accs.append((po, pod))
                    pn, pnd = psum()
                    its = [(pn[:, :], ones_bf, PT[pb][:, kb * TC:(kb + 1) * TC]) for kb in range(2)]
                    mm(its, [PTd[pb], cbf_dep], [pnd])
                    S.op("act", o_act(rcp[pb], pn[:, :], AF.Ln), reads=[pnd], writes=[rcpd[pb]])
                    S.op("act", o_act(rcp[pb], rcp[pb], AF.Exp, scale=-1.0), reads=[rcpd[pb]], writes=[rcpd[pb]])
                    for dc in range(2):
                        po, pod = accs[dc]
                        S.op("dve", o_tt(oT(2 * h + dc, tcl), po[:, :], rcp[pb], ALU.mult), reads=[pod, rcpd[pb]], writes=[oTd[2 * h + dc][tcl]])

                for idx in range(len(items_l) + 1):
                    if idx < len(items_l):
                        qk(idx)
                    if idx >= 1:
                        pv(idx - 1)
                if hf == 0:
                    prenorm(L, "g_cross_pre", NT, hT, hd)
                outproj_postnorm(L, T0, std_panels(L, "xo", oT, oTd), "g_cross_post", yv, yd)

        def ffn(L):
            SB = A_SB
            hTv = vb(SB + 0, 4096)
            actv = vb(SB + 4096, NFF * NT // 2)
            sg = [vf(SB + 15360 + i * 512, 512) for i in range(2)]
            yv_ = vf(SB + 16384, 8192)
            hd = [[Dep() for _ in range(2)] for _ in range(8)]
            actd = [[Dep() for _ in range(2)] for _ in range(NFF)]
            sgd = [Dep(), Dep()]
            yd = [[Dep() for _ in range(2)] for _ in range(8)]

            def hT(kc, tcl):
                return hTv[:, kc * NT + tcl * TC: kc * NT + (tcl + 1) * TC]

            def act(f, tcl):
                return actv[:, f * NT + tcl * TC: f * NT + (tcl + 1) * TC]

            def yv(c, tcl):
                return yv_[:, c * NT + tcl * TC: c * NT + (tcl + 1) * TC]

            S.barrier()
            k = 0
            prenorm(L, "g_ffn_pre", 0, hT, hd)
            for hf in range(2):
                T0 = hf * NT
                for f in range(NFF):
                    w, wd = wget(L, ("gu", f))
                    for tcl in range(2):
                        psg, pdg = psum()
                        items, reads = proj_fm(w, wd, 0, 256, hT, hd, tcl, psg)
                        mm(items, reads, [pdg])
                        psu, pdu = psum()
                        items, reads = proj_fm(w, wd, 128, 256, hT, hd, tcl, psu)
                        mm(items, reads, [pdu])
                        b = k % 2
                        k += 1
                        S.op("act", o_act(sg[b], psg[:, :], AF.Silu), reads=[pdg], writes=[sgd[b]])
                        S.op("dve", o_tt(act(f, tcl), psu[:, :], sg[b], ALU.mult), reads=[pdu, sgd[b]], writes=[actd[f][tcl]])

                def mk(oc):
                    def getp(tcl, ps, _c={}):
                        if "w" not in _c:
                            _c["w"] = wget(L, ("dn", oc))
                        w, wd = _c["w"]
                        items = [(ps[:, :], w[:, fc * 128:(fc + 1) * 128], act(fc, tcl)) for fc in range(NFF)]
                        return items, [wd] + [actd[fc][tcl] for fc in range(NFF)]
                    return getp
                if hf == 0:
                    prenorm(L, "g_ffn_pre", NT, hT, hd)
                outproj_postnorm(L, T0, [mk(oc) for oc in range(8)], "g_ffn_post", yv, yd)

        dmem = S.dsem("mem")
        for li, L in enumerate(layers if _stage != "io" else []):
            if li > 0:
                S.barrier()
                S.new_epoch()
            if L % 2 == 0:
                even_mixer(L)
            else:
                odd_mixer(L)
            if _stage in ("pre", "proj", "attn", "mix", "attn_prep", "attn_qk", "attn_qk1", "attn_qk2", "mix0", "mixop", "op0", "op0c", "op1", "op2"):
                break
            cross(L, dmem)
            if _stage == "cross":
                break
            ffn(L)
        assert _stage != "" or ws["use"] == len(order)

        d_out = S.dsem("out")
        for c in range(8):
            S.dma("sp", out_d[c * 128:(c + 1) * 128, :], XT(c, 0, T), d_out, reads=xd[c])
        S.wait_tok("sp", d_out[0], d_out[1])
        S.finish()
    return nc


def _kp(Wm):
    K, n = Wm.shape
    return np.ascontiguousarray(Wm.reshape(K // 128, 128, n).transpose(1, 0, 2)).reshape(128, -1)


def _panel(inp, L, key):
    e = L // 2
    o = L // 2
    k0 = key[0]
    if k0 == "abq":
        return _kp(inp["ab_w_in"][e][:, 0:512])
    if k0 == "abk":
        return _kp(inp["ab_w_in"][e][:, 512:1024])
    if k0 == "abv":
        return _kp(inp["ab_w_in"][e][:, 1024:1536])
    if k0 == "abc":
        i = key[1]
        Wm = inp["ab_w_in"][e]
        cat = np.concatenate([Wm[:, 1544 + i * 128:1544 + (i + 1) * 128], Wm[:, 2056 + i * 128:2056 + (i + 1) * 128],
                              Wm[:, 2568 + i * 128:2568 + (i + 1) * 128], Wm[:, 1536:1544]], axis=1)
        return _kp(cat)
    if k0 == "abo":
        j = key[1]
        return _kp(inp["ab_w_out"][e][:, j * 512:(j + 1) * 512])
    if k0 == "cin":
        n = key[1]
        Wm = inp["c_w_in"][o]
        cat = np.concatenate([Wm[:, n * 256:(n + 1) * 256], Wm[:, 1024 + n * 256:1024 + (n + 1) * 256]], axis=1)
        return _kp(cat)
    if k0 == "cg":
        n = key[1]
        return np.concatenate([_kp(inp["c_w_a"][o][n]), _kp(inp["c_w_i"][o][n])], axis=1)
    if k0 == "co":
        j = key[1]
        return _kp(inp["c_w_out"][o][:, j * 512:(j + 1) * 512])
    if k0 == "xk":
        j = key[1]
        return _kp(inp["w_xkv"][L][:, j * 512:(j + 1) * 512])
    if k0 == "xv":
        j = key[1]
        return _kp(inp["w_xkv"][L][:, 1024 + j * 512:1024 + (j + 1) * 512])
    if k0 == "xq":
        j = key[1]
        return _kp(inp["w_xq"][L][:, j * 512:(j + 1) * 512])
    if k0 == "xo":
        j = key[1]
        return _kp(inp["w_xo"][L][:, j * 512:(j + 1) * 512])
    if k0 == "gu":
        f = key[1]
        Wm = inp["w_ffn_gu"][L]
        cat = np.concatenate([Wm[:, f * 128:(f + 1) * 128], Wm[:, DFF + f * 128:DFF + (f + 1) * 128]], axis=1)
        return _kp(cat)
    if k0 == "dn":
        oc = key[1]
        return _kp(inp["w_ffn_down"][L][:, oc * 128:(oc + 1) * 128])
    raise KeyError(key)


def pack_weights(inp, layers):
    offs, wtot = panel_offsets(layers)
    arrs = {L: np.empty((128, wtot[L]), np.float32) for L in layers}
    seen = set()
    for (L, key, n) in panel_order(layers):
        if (L, key) in seen:
            continue
        seen.add((L, key))
        p = _panel(inp, L, key)
        assert p.shape == (128, n), (key, p.shape, n)
        arrs[L][:, offs[(L, key)]:offs[(L, key)] + n] = p
    return arrs


def pack_vecs(inp):
    v = np.zeros((128, NV), np.float32)

    def fm(a):
        return np.asarray(a, np.float32).reshape(-1, 128).T

    for n in ("g_mix_pre", "g_mix_post", "g_cross_pre", "g_mem", "g_cross_post", "g_ffn_pre", "g_ffn_post"):
        for L in range(DEPTH):
            v[:, VCOL[n] + L * 8: VCOL[n] + L * 8 + 8] = fm(inp[n][L])
    for e in range(2):
        for k in range(3):
            v[:, VCOL["abcw"] + e * 12 + k * 4: VCOL["abcw"] + e * 12 + k * 4 + 4] = fm(inp["ab_conv_w"][e, k])
    for o in range(2):
        for k in range(4):
            v[:, VCOL["ccw"] + o * 32 + k * 8: VCOL["ccw"] + o * 32 + k * 8 + 8] = fm(inp["c_conv_w"][o, k])
        v[:, VCOL["ccb"] + o * 8: VCOL["ccb"] + o * 8 + 8] = fm(inp["c_conv_b"][o])
        v[:, VCOL["cba"] + o * 8: VCOL["cba"] + o * 8 + 8] = fm(np.asarray(inp["c_b_a"][o]).reshape(-1))
        v[:, VCOL["cbi"] + o * 8: VCOL["cbi"] + o * 8 + 8] = fm(np.asarray(inp["c_b_i"][o]).reshape(-1))
        v[:, VCOL["clam"] + o * 8: VCOL["clam"] + o * 8 + 8] = fm(inp["c_lam"][o])
    v[0:8, VCOL["bf"]:VCOL["bf"] + 2] = np.asarray(inp["ab_b_f"], np.float32).T
    v[:, VCOL["one"]] = 1.0
    v[:, VCOL["eps"]] = EPS
    v[0:8, VCOL["id8"]:VCOL["id8"] + 8] = np.eye(8, dtype=np.float32)
    return v


def pack_consts():
    c = np.zeros((128, NCB), np.float32)
    c[:, CB_ID:CB_ID + 128] = np.eye(128, dtype=np.float32)
    kk = np.arange(128)[:, None]
    qq = np.arange(128)[None, :]
    c[:, CB_MASK:CB_MASK + 128] = np.where(kk > qq, -30000.0, 0.0)
    c[:, CB_ONES:CB_ONES + 128] = 1.0
    for h in range(8):
        c[h, CB_SEL + h * 128:CB_SEL + (h + 1) * 128] = 1.0
        c[32 + h, CB_SEL + h * 128:CB_SEL + (h + 1) * 128] = 1.0
        c[64 + h, CB_SEL + h * 128:CB_SEL + (h + 1) * 128] = 1.0
        c[96 + h, CB_SEL + h * 128:CB_SEL + (h + 1) * 128] = 1.0
    return c


_PROG_CACHE = {}


def _get_prog(layers):
    key = tuple(layers)
    if key not in _PROG_CACHE:
        _PROG_CACHE[key] = build_program(list(layers))
    return _PROG_CACHE[key]


MODE = "fused"


def kernel(**inputs):
    inp = {k: np.asarray(v) for k, v in inputs.items()}
    x = inp["x"].astype(np.float32, copy=False)
    mem = inp["mem"].astype(np.float32, copy=False)
    B = x.shape[0]
    vecs = pack_vecs(inp)
    cbf = pack_consts()
    xT = [np.ascontiguousarray(x[b].T) for b in range(B)]
    memT = [np.ascontiguousarray(mem[b].T) for b in range(B)]
    groups = [[L] for L in range(DEPTH)] if MODE == "per_layer" else [list(range(DEPTH))]
    for layers in groups:
        nc = _get_prog(layers)
        warr = pack_weights(inp, layers)
        in_maps = []
        for b in range(B):
            m = {"xT": xT[b], "memT": memT[b], "vecs": vecs, "cbf": cbf}
            for L in layers:
                m[f"w{L}"] = warr[L]
            in_maps.append(m)
        res = run_bass_kernel_spmd(nc, in_maps, core_ids=list(range(B)))
        xT = [np.asarray(res.results[b]["outT"], np.float32) for b in range(B)]
    out = np.stack([xT[b].T for b in range(B)], axis=0)
    return np.ascontiguousarray(out.astype(np.float32))
```

```python
import numpy as np
from contextlib import ExitStack
import concourse.bass as bass
import concourse.mybir as mybir
from concourse.bass_utils import run_bass_kernel_spmd

F32 = mybir.dt.float32
BF16 = mybir.dt.bfloat16
AF = mybir.ActivationFunctionType
ALU = mybir.AluOpType

DEPTH = 4
D = 1024
T = 2048
NT = 1024
TC = 512
MEM = 256
DFF = 2816
NFF = 22
EPS = 1e-6
NSLOT = 3
SLOT_W = 2048
NG = 4

VCOL = {}
_c = 0
for _n in ("g_mix_pre", "g_mix_post", "g_cross_pre", "g_mem", "g_cross_post", "g_ffn_pre", "g_ffn_post"):
    VCOL[_n] = _c
    _c += 32
VCOL["abcw"] = _c; _c += 24
VCOL["ccw"] = _c; _c += 64
for _n in ("ccb", "cba", "cbi", "clam"):
    VCOL[_n] = _c
    _c += 16
VCOL["bf"] = _c; _c += 2
VCOL["one"] = _c; _c += 1
VCOL["eps"] = _c; _c += 1
VCOL["zero"] = _c; _c += 1
VCOL["id8"] = _c; _c += 8
NV = _c + (_c % 2)

CB_ID, CB_MASK, CB_ONES, CB_SEL = 0, 128, 256, 384
NCB = 384 + 1024


def panel_order(layers):
    order = []
    for L in layers:
        if L % 2 == 0:
            for hf in range(2):
                order += [(L, ("abq",), 4096), (L, ("abk",), 4096), (L, ("abv",), 4096)]
                order += [(L, ("abc", i), 8 * 392) for i in range(4)]
                order += [(L, ("abo", j), 4096) for j in range(2)]
        else:
            for hf in range(2):
                order += [(L, ("cin", 0), 4096)]
                for n in range(4):
                    if n < 3:
                        order += [(L, ("cin", n + 1), 4096)]
                    order += [(L, ("cg", n), 1024)]
                order += [(L, ("co", j), 4096) for j in range(2)]
        order += [(L, ("xk", j), 4096) for j in range(2)]
        order += [(L, ("xv", j), 4096) for j in range(2)]
        for hf in range(2):
            order += [(L, ("xq", j), 4096) for j in range(2)]
            order += [(L, ("xo", j), 4096) for j in range(2)]
        for hf in range(2):
            order += [(L, ("gu", f), 2048) for f in range(NFF)]
            order += [(L, ("dn", oc), NFF * 128) for oc in range(8)]
    return order


def panel_offsets(layers):
    offs = {}
    tot = {L: 0 for L in layers}
    for (L, key, n) in panel_order(layers):
        if (L, key) not in offs:
            offs[(L, key)] = tot[L]
            tot[L] += n
    return offs, tot


class Dep:
    __slots__ = ("w", "r")

    def __init__(self):
        self.w = None
        self.r = {}


class Eng:
    def __init__(self, name):
        self.name = name
        self.ops = []
        self.sem = None
        self.cnt = 0
        self.seen = {}


class Sched:
    def __init__(self, nc, stack):
        self.nc = nc
        self.stack = stack
        self.engs = {n: Eng(n) for n in ("pe", "act", "dve", "pool", "sp")}
        self.nsem = 0
        self.new_epoch()

    def new_sem(self, name):
        self.nsem += 1
        return self.stack.enter_context(self.nc.semaphore(f"{name}_{self.nsem}"))

    def new_epoch(self):
        for e in self.engs.values():
            e.sem = self.new_sem("e_" + e.name)
            e.cnt = 0

    def _waits(self, eng, reads, writes):
        need = {}

        def add(tok, raw):
            if tok is None:
                return
            sem, val, src = tok
            if src is eng and eng.name == "pe":
                return
            k = id(sem)
            if eng.seen.get(k, 0) >= val:
                return
            if k not in need or need[k][1] < val:
                need[k] = (sem, val)

        for d in reads:
            add(d.w, True)
        for d in writes:
            add(d.w, False)
            for tok in d.r.values():
                add(tok, False)
        for k, (sem, val) in need.items():
            eng.seen[k] = val
            eng.ops.append(("wait", sem, val))

    def op(self, engname, fn, reads=(), writes=(), inc=True):
        eng = self.engs[engname]
        self._waits(eng, reads, writes)
        if inc:
            eng.cnt += 1
            tok = (eng.sem, eng.cnt, eng)
        else:
            tok = (eng.sem, eng.cnt + 1, eng)
        eng.ops.append(("op", fn, eng.sem if inc else None, 1))
        for d in reads:
            d.r[id(eng.sem)] = tok
        for d in writes:
            d.w = tok
            d.r = {}

    def dma(self, engname, out, in_, dsem, reads=(), writes=()):
        eng = self.engs[engname]
        self._waits(eng, reads, writes)
        dsem[1] += 16
        tok = (dsem[0], dsem[1], None)
        eng.ops.append(("op", lambda e, o=out, i=in_: e.dma_start(out=o, in_=i), dsem[0], 16))
        for d in reads:
            d.r[id(dsem[0])] = tok
        for d in writes:
            d.w = tok
            d.r = {}

    def dsem(self, name):
        return [self.new_sem("d_" + name), 0]

    def wait_tok(self, engname, sem, val):
        self.engs[engname].ops.append(("wait", sem, val))

    def barrier(self):
        for e in self.engs.values():
            for e2 in self.engs.values():
                if e2 is e or e2.cnt == 0:
                    continue
                k = id(e2.sem)
                if e.seen.get(k, 0) >= e2.cnt:
                    continue
                e.seen[k] = e2.cnt
                e.ops.append(("wait", e2.sem, e2.cnt))

    def finish(self):
        nc = self.nc
        with nc.Block() as block:
            def runner(eng):
                def f(e):
                    for o in eng.ops:
                        if o[0] == "wait":
                            e.wait_ge(o[1], o[2])
                        else:
                            ins = o[1](e)
                            if o[2] is not None:
                                ins.then_inc(o[2], o[3])
                return f
            block.tensor(runner(self.engs["pe"]))
            block.scalar(runner(self.engs["act"]))
            block.vector(runner(self.engs["dve"]))
            block.gpsimd(runner(self.engs["pool"]))
            block.sync(runner(self.engs["sp"]))


def o_act(out, in_, func, **kw):
    return lambda e: e.activation(out=out, in_=in_, func=func, **kw)


def o_ts(out, in0, s1, s2, op0, op1=None):
    if op1 is None:
        return lambda e: e.tensor_scalar(out=out, in0=in0, scalar1=s1, scalar2=None, op0=op0)
    return lambda e: e.tensor_scalar(out=out, in0=in0, scalar1=s1, scalar2=s2, op0=op0, op1=op1)


def o_tt(out, in0, in1, op):
    return lambda e: e.tensor_tensor(out=out, in0=in0, in1=in1, op=op)


def o_stt(out, in0, scalar, in1, op0, op1):
    return lambda e: e.scalar_tensor_tensor(out=out, in0=in0, scalar=scalar, in1=in1, op0=op0, op1=op1)


def o_copy(out, in_):
    return lambda e: e.tensor_copy(out=out, in_=in_)


def o_recip(out, in_):
    return lambda e: e.reciprocal(out=out, in_=in_)


def o_memset(ap, v):
    return lambda e: e.memset(ap, v)


def o_scan(out, d0, d1, init):
    return lambda e: e.tensor_tensor_scan(out=out, data0=d0, data1=d1, initial=init, op0=ALU.mult, op1=ALU.add)


def o_mm(out, lhsT, rhs, start, stop):
    return lambda e: e.matmul(out, lhsT, rhs, start=start, stop=stop)


def build_program(layers):
    offs, wtot = panel_offsets(layers)
    order = panel_order(layers)
    nc = bass.Bass("TRN2", target_bir_lowering=False)
    xin = nc.dram_tensor("xT", [D, T], F32, kind="ExternalInput").ap()
    memin = nc.dram_tensor("memT", [D, MEM], F32, kind="ExternalInput").ap()
    vecs_d = nc.dram_tensor("vecs", [128, NV], F32, kind="ExternalInput").ap()
    cbf_d = nc.dram_tensor("cbf", [128, NCB], F32, kind="ExternalInput").ap()
    wl_d = {L: nc.dram_tensor(f"w{L}", [128, wtot[L]], F32, kind="ExternalInput").ap() for L in layers}
    out_d = nc.dram_tensor("outT", [D, T], F32, kind="ExternalOutput").ap()

    A_XT = 0
    A_SLOT = A_XT + 8 * T
    A_CBF = A_SLOT + NSLOT * SLOT_W
    A_VEC = A_CBF + NCB // 2
    A_ONEF = A_VEC + NV
    A_SQ = A_ONEF + 512
    A_RS = A_SQ + 2 * 256
    A_SB = A_RS + 2 * 512
    SCR = 26912
    AW = A_SB + SCR

    with ExitStack() as st:
        S = Sched(nc, st)
        arena = st.enter_context(nc.sbuf_tensor("arena", [128, AW], F32))
        psb = [st.enter_context(nc.psum_tensor(f"ps{i}", [128, 512], F32)) for i in range(8)]
        psd = [Dep() for _ in range(8)]
        rr_state = {"g": 0}

        def psum():
            i = rr_state["g"]
            rr_state["g"] = (i + 1) % NG
            return psb[i], psd[i]

        def vf(off, n):
            return arena[:, off:off + n]

        def vb(off, nwords):
            return arena[:, off:off + nwords].bitcast(BF16)

        def XT(c, t0, n):
            return arena[:, A_XT + c * T + t0: A_XT + c * T + t0 + n]

        xd = [[Dep() for _ in range(4)] for _ in range(8)]
        slot_bf = [vb(A_SLOT + s * SLOT_W, SLOT_W) for s in range(NSLOT)]
        slot_dep = [Dep() for _ in range(NSLOT)]
        slot_sem = [S.dsem(f"slot{s}") for s in range(NSLOT)]
        cbf = vb(A_CBF, NCB // 2)
        cbf_dep = Dep()
        vecs = vf(A_VEC, NV)
        vec_dep = Dep()
        onef = vf(A_ONEF, 512)
        onef_dep = Dep()
        sqb = [vb(A_SQ + i * 256, 256) for i in range(2)]
        sqd = [Dep() for _ in range(2)]
        rsb = [vf(A_RS + i * 512, 512) for i in range(2)]
        rsd = [Dep() for _ in range(2)]
        cnt = {"sq": 0, "rs": 0}

        ident = cbf[:, CB_ID:CB_ID + 128]
        negmask = cbf[:, CB_MASK:CB_MASK + 128]
        ones_bf = cbf[:, CB_ONES:CB_ONES + 128]

        def vc(name, idx=0, p0=0, p1=128):
            c = VCOL[name] + idx
            return vecs[p0:p1, c:c + 1]

        _stage = ""
        ws = {"issue": 0, "use": 0}

        def wget(L, key):
            idx = ws["use"]
            assert order[idx][0] == L and order[idx][1] == key, (order[idx], L, key)
            lim = min(len(order), idx + NSLOT)
            while ws["issue"] < lim:
                q = ws["issue"]
                Lq, kq, nq = order[q]
                s = q % NSLOT
                off = offs[(Lq, kq)]
                S.dma("pool", slot_bf[s][:, 0:nq], wl_d[Lq][:, off:off + nq], slot_sem[s], writes=[slot_dep[s]])
                ws["issue"] += 1
            ws["use"] += 1
            return slot_bf[idx % NSLOT], slot_dep[idx % NSLOT]

        def mm(items, reads, wdeps, start=True, stop=True):
            n = len(items)
            for i, (o, l, r) in enumerate(items):
                S.op("pe", o_mm(o, l, r, start and i == 0, stop and i == n - 1),
                     reads=reads if i == 0 else (), writes=wdeps if i == 0 else (), inc=(i == n - 1))

        d_in = S.dsem("in")
        S.dma("sp", vecs, vecs_d, d_in, writes=[vec_dep])
        d_cb = S.dsem("cb")
        S.dma("pool", cbf, cbf_d, d_cb, writes=[cbf_dep])
        d_x = S.dsem("x")
        for c in range(8):
            S.dma("sp", XT(c, 0, T), xin[c * 128:(c + 1) * 128, :], d_x, writes=xd[c])
        for c in range(8):
            for dd in xd[c]:
                dd.w = (d_x[0], d_x[1], None)
        S.op("dve", o_memset(onef, 1.0), writes=[onef_dep])

        def rstd_from(ps, pd, ncol):
            i = cnt["rs"] % 2
            cnt["rs"] += 1
            rs, rd = rsb[i], rsd[i]
            S.op("act", o_act(rs[:, 0:ncol], ps[:, 0:ncol], AF.Ln, scale=1.0 / D, bias=vc("eps")), reads=[pd, vec_dep], writes=[rd])
            S.op("act", o_act(rs[:, 0:ncol], rs[:, 0:ncol], AF.Exp, scale=-0.5), reads=[rd], writes=[rd])
            return rs, rd

        def prenorm(L, gname, T0, hT, hd):
            for tcl in range(2):
                tg = T0 + tcl * TC
                tcg = tg // TC
                ps, pd = psum()
                for c in range(8):
                    i = cnt["sq"] % 2
                    cnt["sq"] += 1
                    S.op("act", o_act(sqb[i], XT(c, tg, TC), AF.Square), reads=[xd[c][tcg]], writes=[sqd[i]])
                    mm([(ps[:, :], ones_bf, sqb[i])], [sqd[i], cbf_dep], [pd], start=(c == 0), stop=(c == 7))
                rs, rd = rstd_from(ps, pd, TC)
                for c in range(8):
                    S.op("dve", o_stt(hT(c, tcl), XT(c, tg, TC), vc(gname, L * 8 + c), rs, ALU.mult, ALU.mult),
                         reads=[xd[c][tcg], rd, vec_dep], writes=[hd[c][tcl]])

        def outproj_postnorm(L, T0, panels, gname, yv, yd):
            st_ps = [(psb[4], psd[4]), (psb[5], psd[5])]
            pending = []

            def flush():
                if _stage in ("op0", "op0c", "op1"):
                    pending.clear()
                while pending:
                    oc_, tcl_, sqi = pending.pop(0)
                    mm([(st_ps[tcl_][0][:, :], ones_bf, sqb[sqi])], [sqd[sqi], cbf_dep], [st_ps[tcl_][1]],
                       start=(oc_ == 0), stop=(oc_ == 7))

            for oc in range(8):
                getp = panels[oc]
                for tcl in range(2):
                    ps, pd = psum()
                    items, reads = getp(tcl, ps)
                    mm(items, reads, [pd])
                    flush()
                    if _stage == "op0":
                        continue
                    S.op("dve", o_copy(yv(oc, tcl), ps[:, :]), reads=[pd], writes=[yd[oc][tcl]])
                    if _stage == "op0c":
                        continue
                    i = cnt["sq"] % 2
                    cnt["sq"] += 1
                    S.op("act", o_act(sqb[i], yv(oc, tcl), AF.Square), reads=[yd[oc][tcl]], writes=[sqd[i]])
                    pending.append((oc, tcl, i))
            flush()
            if _stage in ("op0", "op0c", "op1", "op2"):
                return
            for tcl in range(2):
                tg = T0 + tcl * TC
                tcg = tg // TC
                rs, rd = rstd_from(st_ps[tcl][0], st_ps[tcl][1], TC)
                for c in range(8):
                    S.op("dve", o_tt(yv(c, tcl), yv(c, tcl), rs, ALU.mult), reads=[yd[c][tcl], rd], writes=[yd[c][tcl]])
                    S.op("dve", o_stt(XT(c, tg, TC), yv(c, tcl), vc(gname, L * 8 + c), XT(c, tg, TC), ALU.mult, ALU.add),
                         reads=[yd[c][tcl], vec_dep, xd[c][tcg]], writes=[xd[c][tcg]])

        def std_panels(L, kname, src, srcd):
            cache = {}

            def mk(oc):
                def getp(tcl, ps):
                    j, ol = oc // 4, oc % 4
                    if (j) not in cache:
                        cache.clear()
                        cache[j] = wget(L, (kname, j))
                    w, wd = cache[j]
                    items = [(ps[:, :], w[:, kc * 512 + ol * 128: kc * 512 + (ol + 1) * 128], src(kc, tcl)) for kc in range(8)]
                    return items, [wd] + [srcd[kc][tcl] for kc in range(8)]
                return getp
            return [mk(oc) for oc in range(8)]

        def proj_fm(w, wd, coff, ncols_panel, hT, hd, tcl, ps, m=128):
            items = [(ps[0:m, :], w[:, kc * ncols_panel + coff: kc * ncols_panel + coff + m], hT(kc, tcl)) for kc in range(8)]
            return items, [wd] + [hd[kc][tcl] for kc in range(8)]

        def even_mixer(L):
            e = L // 2
            SB = A_SB
            kTv = vb(SB + 0, 4096)
            Vv = vb(SB + 4096, 4096)
            Nf = vf(SB + 8192, 2048)
            halo = vf(SB + 10240, 8)
            negbf = vf(SB + 10248, 2)
            biasT = [vf(SB + 10256 + i * 128, 128) for i in range(2)]
            hTv = vb(SB + 10512, 4096)
            qTv = vb(SB + 14608, 2048)
            cu = vf(SB + 16656, 1028)
            Cc = vf(SB + 17684, 512)
            tcv = vf(SB + 18196, 512)
            yv_ = vf(SB + 10512, 8192)
            yTv = vb(SB + 18708, 4096)
            PT = [vb(SB + 22804 + i * 256, 256) for i in range(2)]
            rcp = [vf(SB + 23316 + i * 512, 512) for i in range(2)]
            tmp = [vf(SB + 24340 + i * 512, 512) for i in range(4)]
            cq = [vb(SB + 26388 + i * 256, 256) for i in range(2)]
            kd = [[Dep() for _ in range(4)] for _ in range(4)]
            vd = [Dep() for _ in range(16)]
            Nd = [Dep() for _ in range(4)]
            halod = [Dep() for _ in range(4)]
            negbfd = Dep()
            biasTd = [Dep(), Dep()]
            hd = [[Dep() for _ in range(2)] for _ in range(8)]
            qd = [[Dep() for _ in range(2)] for _ in range(4)]
            cud, Ccd, tcd = Dep(), Dep(), Dep()
            yd = [[Dep() for _ in range(2)] for _ in range(8)]
            yTd = [[Dep() for _ in range(2)] for _ in range(8)]
            PTd = [Dep(), Dep()]
            rcpd = [Dep(), Dep()]
            tmpd = [Dep() for _ in range(4)]
            cqd = [Dep(), Dep()]

            def hT(kc, tcl):
                return hTv[:, kc * NT + tcl * TC: kc * NT + (tcl + 1) * TC]

            def qT(i, tcl):
                return qTv[:, i * NT + tcl * TC: i * NT + (tcl + 1) * TC]

            def kT(i, t0, n):
                return kTv[:, i * T + t0: i * T + t0 + n]

            def Vb(tb):
                return Vv[:, tb * 512:(tb + 1) * 512]

            def yT(c, tcl):
                return yTv[:, c * NT + tcl * TC: c * NT + (tcl + 1) * TC]

            def yv(c, tcl):
                return yv_[:, c * NT + tcl * TC: c * NT + (tcl + 1) * TC]

            S.barrier()
            S.op("dve", o_ts(negbf[0:8, :], vecs[0:8, VCOL["bf"]:VCOL["bf"] + 2], -1.0, None, ALU.mult),
                 reads=[vec_dep], writes=[negbfd])
            for i in range(2):
                S.op("pool", o_memset(cq[i], 0.0), writes=[cqd[i]])

            for hf in range(2):
                T0 = hf * NT
                if hf == 1:
                    S.barrier()
                prenorm(L, "g_mix_pre", T0, hT, hd)
                if _stage == "pre":
                    return
                w, wd = wget(L, ("abq",))
                for i in range(4):
                    for tcl in range(2):
                        ps, pd = psum()
                        items, reads = proj_fm(w, wd, i * 128, 512, hT, hd, tcl, ps)
                        mm(items, reads, [pd])
                        S.op("act", o_act(qT(i, tcl), ps[:, :], AF.Copy), reads=[pd], writes=[qd[i][tcl]])
                w, wd = wget(L, ("abk",))
                for i in range(4):
                    for tcl in range(2):
                        ps, pd = psum()
                        items, reads = proj_fm(w, wd, i * 128, 512, hT, hd, tcl, ps)
                        mm(items, reads, [pd])
                        S.op("dve", o_copy(kT(i, T0 + tcl * TC, TC), ps[:, :]), reads=[pd], writes=[kd[i][hf * 2 + tcl]])
                w, wd = wget(L, ("abv",))
                for tb in range(8):
                    tcl = tb // 4
                    ps, pd = psum()
                    items = [(ps[:, :], hT(kc, tcl)[:, (tb % 4) * 128:(tb % 4 + 1) * 128], w[:, kc * 512:(kc + 1) * 512]) for kc in range(8)]
                    mm(items, [wd] + [hd[kc][tcl] for kc in range(8)], [pd])
                    eng = "act" if tb % 2 == 0 else "dve"
                    if eng == "act":
                        S.op("act", o_act(Vb(hf * 8 + tb), ps[:, :], AF.Copy), reads=[pd], writes=[vd[hf * 8 + tb]])
                    else:
                        S.op("dve", o_copy(Vb(hf * 8 + tb), ps[:, :]), reads=[pd], writes=[vd[hf * 8 + tb]])
                for i in range(4):
                    w, wd = wget(L, ("abc", i))
                    if i == 0:
                        for tcl in range(2):
                            cg = hf * 2 + tcl
                            tg = T0 + tcl * TC
                            ps, pd = psum()
                            items, reads = proj_fm(w, wd, 384, 392, hT, hd, tcl, ps, m=8)
                            mm(items, reads, [pd])
                            tA, tB = tmp[0], tmp[1]
                            S.op("act", o_act(tA[0:8, :], ps[0:8, :], AF.Exp, scale=-1.0, bias=negbf[0:8, e:e + 1]),
                                 reads=[pd, negbfd], writes=[tmpd[0]])
                            S.op("act", o_act(tB[0:8, :], tA[0:8, :], AF.Ln, scale=1.0, bias=vc("one", 0, 0, 8)),
                                 reads=[tmpd[0], vec_dep], writes=[tmpd[1]])
                            init = Nf[0:8, tg - 1:tg] if cg > 0 else 0.0
                            S.op("dve", o_scan(Nf[0:8, tg:tg + TC], onef[0:8, :], tB[0:8, :], init),
                                 reads=[tmpd[1], onef_dep] + ([Nd[cg - 1]] if cg > 0 else []), writes=[Nd[cg]])
                    if hf == 0:
                        S.op("dve", o_memset(cu[:, 0:2], 0.0), writes=[cud])
                    else:
                        S.op("dve", o_copy(cu[:, 0:2], halo[:, 2 * i:2 * i + 2]), reads=[halod[i]], writes=[cud])
                    for tcl in range(2):
                        psB, pdB = psum()
                        items, reads = proj_fm(w, wd, 0, 392, hT, hd, tcl, psB)
                        mm(items, reads, [pdB])
                        psC, pdC = psum()
                        items, reads = proj_fm(w, wd, 128, 392, hT, hd, tcl, psC)
                        mm(items, reads, [pdC])
                        psU, pdU = psum()
                        items, reads = proj_fm(w, wd, 256, 392, hT, hd, tcl, psU)
                        mm(items, reads, [pdU])
                        S.op("act", o_act(Cc, psC[:, :], AF.Copy), reads=[pdC], writes=[Ccd])
                        b0 = tcl * TC
                        S.op("dve", o_tt(cu[:, 2 + b0:2 + b0 + TC], psU[:, :], Cc, ALU.mult), reads=[pdU, Ccd], writes=[cud])
                        cw = VCOL["abcw"] + e * 12 + i
                        S.op("dve", o_ts(tcv, cu[:, b0:b0 + TC], vecs[:, cw:cw + 1], None, ALU.mult),
                             reads=[cud, vec_dep], writes=[tcd])
                        S.op("dve", o_stt(tcv, cu[:, b0 + 1:b0 + 1 + TC], vecs[:, cw + 4:cw + 5], tcv, ALU.mult, ALU.add),
                             reads=[cud, tcd], writes=[tcd])
                        S.op("dve", o_stt(tcv, cu[:, b0 + 2:b0 + 2 + TC], vecs[:, cw + 8:cw + 9], tcv, ALU.mult, ALU.add),
                             reads=[cud, tcd], writes=[tcd])
                        S.op("dve", o_tt(yT(4 + i, tcl), psB[:, :], tcv, ALU.mult), reads=[pdB, tcd], writes=[yTd[4 + i][tcl]])
                    if hf == 0:
                        S.op("dve", o_copy(halo[:, 2 * i:2 * i + 2], cu[:, NT:NT + 2]), reads=[cud], writes=[halod[i]])

                if _stage == "proj":
                    return
                for tcl in range(2):
                    cg = hf * 2 + tcl
                    tg = T0 + tcl * TC
                    nkb = 4 * cg + 4
                    Rcol = Nf[0:8, tg + 255:tg + 256]
                    Rdeps = [Nd[cg]]
                    cqb, cqbd = cq[cg % 2], cqd[cg % 2]
                    bT, bTd = biasT[cg % 2], biasTd[cg % 2]
                    psT, pdT = psum()
                    for seg in range(cg + 1):
                        tb_, tbd_ = tmp[2 + seg % 2], tmpd[2 + seg % 2]
                        S.op("dve", o_ts(tb_[0:8, :], Nf[0:8, seg * TC:(seg + 1) * TC], Rcol, None, ALU.subtract),
                             reads=[Nd[seg]] + Rdeps, writes=[tbd_])
                        for jb in range(4):
                            j = seg * 4 + jb
                            mm([(psT[:, j * 8:(j + 1) * 8], tb_[0:8, jb * 128:(jb + 1) * 128], vecs[0:8, VCOL["id8"]:VCOL["id8"] + 8])],
                               [tbd_, vec_dep], [pdT])
                    S.op("dve", o_copy(bT[:, 0:nkb * 8], psT[:, 0:nkb * 8]), reads=[pdT], writes=[bTd])

                    if _stage == "attn_prep":
                        continue
                    items_l = [(h, j) for h in range(8) for j in range(nkb)]
                    state = {}

                    def qk(idx):
                        h, j = items_l[idx]
                        i, hp = h // 2, h % 2
                        jj = j - 4 * cg
                        c0 = 0 if jj <= 0 else jj * 128
                        ps, pd = psum()
                        lo, hi = hp * 64, hp * 64 + 64
                        its = [(ps[:, c0:TC], kT(i, j * 128, 128)[lo:hi, :], qT(i, tcl)[lo:hi, c0:TC])]
                        if jj >= 0 and _stage not in ("attn_qk1", "attn_qk2"):
                            its.append((ps[:, c0:c0 + 128], ident, negmask))
                        mm(its, [kd[i][j // 4], qd[i][tcl], cbf_dep], [pd])
                        pb = idx % 2
                        S.op("act", o_act(PT[pb][:, c0:TC], ps[:, c0:TC], AF.Exp, scale=0.125, bias=bT[:, j * 8 + h:j * 8 + h + 1]),
                             reads=[pd, bTd], writes=[PTd[pb]])
                        state[idx] = (pb, c0)

                    def pv(idx):
                        h, j = items_l[idx]
                        i, hp = h // 2, h % 2
                        pb, c0 = state.pop(idx)
                        so = 4 + 2 * (h % 2)
                        pso, psod, psn, psnd = psb[so], psd[so], psb[so + 1], psd[so + 1]
                        mm([(pso[:, c0:TC], Vb(j)[:, i * 128:(i + 1) * 128], PT[pb][:, c0:TC])], [vd[j], PTd[pb]], [psod],
                           start=(j == 0), stop=(j == nkb - 1))
                        mm([(psn[:, c0:TC], ones_bf, PT[pb][:, c0:TC])], [PTd[pb], cbf_dep], [psnd],
                           start=(j == 0), stop=(j == nkb - 1))
                        if j == nkb - 1:
                            rb = h % 2
                            lo, hi = hp * 64, hp * 64 + 64
                            S.op("dve", o_recip(rcp[rb][lo:hi, :], psn[lo:hi, :]), reads=[psnd], writes=[rcpd[rb]])
                            S.op("dve", o_tt(yT(i, tcl)[lo:hi, :], pso[lo:hi, :], rcp[rb][lo:hi, :], ALU.mult),
                                 reads=[psod, rcpd[rb]], writes=[yTd[i][tcl]])

                    n_it = len(items_l)
                    for idx in range(n_it + 1):
                        if idx < n_it:
                            qk(idx)
                        if idx >= 1 and not _stage.startswith("attn_qk"):
                            pv(idx - 1)

                if _stage in ("attn", "attn_prep", "attn_qk", "attn_qk1", "attn_qk2"):
                    return
                S.barrier()
                if _stage == "mixop":
                    return
                outproj_postnorm(L, T0, std_panels(L, "abo", yT, yTd), "g_mix_post", yv, yd)
                if _stage in ("mix0", "op0", "op0c", "op1", "op2"):
                    return

        def odd_mixer(L):
            o = L // 2
            SB = A_SB
            halo = vf(SB + 0, 24)
            carry = vf(SB + 32, 8)
            sc1 = vf(SB + 40, 8)
            sc2 = vf(SB + 48, 8)
            spt = vf(SB + 56, 8)
            hTv = vb(SB + 128, 4096)
            ubuf = [vf(SB + 4224 + i * 1028, 1028) for i in range(2)]
            uc = [vf(SB + 6280 + i * 1024, 1024) for i in range(2)]
            yv_ = vf(SB + 128, 8192)
            ggv = vb(SB + 8328, 4096)
            yTv = vb(SB + 12424, 4096)
            ucb = [vb(SB + 16520 + i * 512, 512) for i in range(2)]
            rr0, ii0, aa, ss, bb, hs, rr1, ii1 = [vf(SB + 17544 + i * 1024, 1024) for i in range(8)]
            rrs, iis = [rr0, rr1], [ii0, ii1]
            halod = [Dep() for _ in range(8)]
            carryd = [Dep() for _ in range(8)]
            scd = Dep()
            hd = [[Dep() for _ in range(2)] for _ in range(8)]
            ubd = [Dep(), Dep()]
            ucd = [Dep(), Dep()]
            ucbd = [Dep(), Dep()]
            ggd = [[Dep() for _ in range(2)] for _ in range(8)]
            yTd = [[Dep() for _ in range(2)] for _ in range(8)]
            yd = [[Dep() for _ in range(2)] for _ in range(8)]
            aad, ssd, bbd, hsd = [Dep() for _ in range(4)]
            rrds, iids = [Dep(), Dep()], [Dep(), Dep()]

            def hT(kc, tcl):
                return hTv[:, kc * NT + tcl * TC: kc * NT + (tcl + 1) * TC]

            def gg(c, tcl):
                return ggv[:, c * NT + tcl * TC: c * NT + (tcl + 1) * TC]

            def yT(c, tcl):
                return yTv[:, c * NT + tcl * TC: c * NT + (tcl + 1) * TC]

            def yv(c, tcl):
                return yv_[:, c * NT + tcl * TC: c * NT + (tcl + 1) * TC]

            S.barrier()
            lam = vecs[:, VCOL["clam"] + o * 8: VCOL["clam"] + o * 8 + 8]
            S.op("act", o_act(spt, lam, AF.Exp, scale=-1.0), reads=[vec_dep], writes=[scd])
            S.op("act", o_act(spt, spt, AF.Ln, scale=1.0, bias=vc("one")), reads=[scd, vec_dep], writes=[scd])
            S.op("dve", o_ts(sc1, spt, -8.0, None, ALU.mult), reads=[scd], writes=[scd])
            S.op("dve", o_ts(sc2, spt, -16.0, None, ALU.mult), reads=[scd], writes=[scd])

            for hf in range(2):
                T0 = hf * NT
                if hf == 1:
                    S.barrier()
                prenorm(L, "g_mix_pre", T0, hT, hd)
                def in_proj(n):
                    w, wd = wget(L, ("cin", n))
                    for ch in range(2):
                        c = 2 * n + ch
                        if hf == 0:
                            S.op("pool", o_memset(ubuf[ch][:, 0:3], 0.0), writes=[ubd[ch]])
                        else:
                            S.op("pool", o_copy(ubuf[ch][:, 0:3], halo[:, 3 * c:3 * c + 3]), reads=[halod[c]], writes=[ubd[ch]])
                        for tcl in range(2):
                            ps, pd = psum()
                            items, reads = proj_fm(w, wd, ch * 128, 512, hT, hd, tcl, ps)
                            mm(items, reads, [pd])
                            S.op("act", o_act(gg(c, tcl), ps[:, :], AF.Gelu_apprx_tanh), reads=[pd], writes=[ggd[c][tcl]])
                            ps, pd = psum()
                            items, reads = proj_fm(w, wd, 256 + ch * 128, 512, hT, hd, tcl, ps)
                            mm(items, reads, [pd])
                            S.op("dve", o_copy(ubuf[ch][:, 3 + tcl * TC:3 + (tcl + 1) * TC], ps[:, :]), reads=[pd], writes=[ubd[ch]])

                def conv(n):
                    for ch in range(2):
                        c = 2 * n + ch
                        cw = VCOL["ccw"] + o * 32 + c
                        cbias = vc("ccb", o * 8 + c)
                        S.op("act", o_act(uc[ch], ubuf[ch][:, 0:NT], AF.Identity, scale=vecs[:, cw:cw + 1], bias=cbias),
                             reads=[ubd[ch], vec_dep], writes=[ucd[ch]])
                        for k in range(1, 4):
                            S.op("dve", o_stt(uc[ch], ubuf[ch][:, k:k + NT], vecs[:, cw + 8 * k:cw + 8 * k + 1], uc[ch], ALU.mult, ALU.add),
                                 reads=[ubd[ch], ucd[ch]], writes=[ucd[ch]])
                        if hf == 0:
                            S.op("pool", o_copy(halo[:, 3 * c:3 * c + 3], ubuf[ch][:, NT:NT + 3]), reads=[ubd[ch]], writes=[halod[c]])
                        S.op("pool", o_copy(ucb[ch], uc[ch]), reads=[ucd[ch]], writes=[ucbd[ch]])

                def gates(n):
                    wg, wgd = wget(L, ("cg", n))
                    for dch in range(2):
                        c = 2 * n + dch
                        rr, ii, rrd, iid = rrs[dch], iis[dch], rrds[dch], iids[dch]
                        for tcl in range(2):
                            ps, pd = psum()
                            its = [(ps[:, :], wg[:, cc * 256 + dch * 128: cc * 256 + (dch + 1) * 128], ucb[cc][:, tcl * TC:(tcl + 1) * TC]) for cc in range(2)]
                            mm(its, [wgd, ucbd[0], ucbd[1]], [pd])
                            S.op("act", o_act(rr[:, tcl * TC:(tcl + 1) * TC], ps[:, :], AF.Sigmoid, scale=1.0, bias=vc("cba", o * 8 + c)),
                                 reads=[pd, vec_dep], writes=[rrd])
                            ps, pd = psum()
                            its = [(ps[:, :], wg[:, 512 + cc * 256 + dch * 128: 512 + cc * 256 + (dch + 1) * 128], ucb[cc][:, tcl * TC:(tcl + 1) * TC]) for cc in range(2)]
                            mm(its, [wgd, ucbd[0], ucbd[1]], [pd])
                            S.op("act", o_act(ii[:, tcl * TC:(tcl + 1) * TC], ps[:, :], AF.Sigmoid, scale=1.0, bias=vc("cbi", o * 8 + c)),
                                 reads=[pd, vec_dep], writes=[iid])
                        S.op("act", o_act(aa, rr, AF.Exp, scale=sc1[:, c:c + 1]), reads=[rrd, scd], writes=[aad])
                        S.op("act", o_act(ss, rr, AF.Exp, scale=sc2[:, c:c + 1]), reads=[rrd, scd], writes=[ssd])
                        S.op("act", o_act(ss, ss, AF.Sqrt, scale=-1.0, bias=vc("one")), reads=[ssd, vec_dep], writes=[ssd])
                        S.op("pool", o_tt(bb, ii, uc[dch], ALU.mult), reads=[iid, ucd[dch]], writes=[bbd])
                        S.op("dve", o_tt(bb, bb, ss, ALU.mult), reads=[bbd, ssd], writes=[bbd])
                        init = carry[:, c:c + 1] if hf == 1 else 0.0
                        S.op("dve", o_scan(hs, aa, bb, init), reads=[aad, bbd] + ([carryd[c]] if hf == 1 else []), writes=[hsd])
                        if hf == 0:
                            S.op("dve", o_copy(carry[:, c:c + 1], hs[:, NT - 1:NT]), reads=[hsd], writes=[carryd[c]])
                        for tcl in range(2):
                            S.op("dve", o_tt(yT(c, tcl), gg(c, tcl), hs[:, tcl * TC:(tcl + 1) * TC], ALU.mult),
                                 reads=[ggd[c][tcl], hsd], writes=[yTd[c][tcl]])

                in_proj(0)
                conv(0)
                for n in range(4):
                    if n < 3:
                        in_proj(n + 1)
                    gates(n)
                    if n < 3:
                        conv(n + 1)
                S.barrier()
                outproj_postnorm(L, T0, std_panels(L, "co", yT, yTd), "g_mix_post", yv, yd)

        def cross(L, dmem):
            SB = A_SB
            kxv = vb(SB + 0, 1024)
            Vxv = vb(SB + 1024, 1024)
            memv = vf(SB + 2048, 2048)
            mTv = vb(SB + 4096, 1024)
            hTv = vb(SB + 2048, 4096)
            qxv = vb(SB + 6144, 4096)
            yv_ = vf(SB + 16384, 8192)
            oTv = vb(SB + 10240, 4096)
            PT = [vb(SB + 14336 + i * 512, 512) for i in range(2)]
            rcp = [vf(SB + 15360 + i * 512, 512) for i in range(2)]
            kxd = [Dep() for _ in range(8)]
            Vxd = [[Dep() for _ in range(2)] for _ in range(2)]
            memd = [Dep() for _ in range(8)]
            mTd = [Dep() for _ in range(8)]
            hd = [[Dep() for _ in range(2)] for _ in range(8)]
            qxd = [[Dep() for _ in range(2)] for _ in range(8)]
            oTd = [[Dep() for _ in range(2)] for _ in range(8)]
            yd = [[Dep() for _ in range(2)] for _ in range(8)]
            PTd = [Dep(), Dep()]
            rcpd = [Dep(), Dep()]

            def hT(kc, tcl):
                return hTv[:, kc * NT + tcl * TC: kc * NT + (tcl + 1) * TC]

            def qx(c, tcl):
                return qxv[:, c * NT + tcl * TC: c * NT + (tcl + 1) * TC]

            def oT(c, tcl):
                return oTv[:, c * NT + tcl * TC: c * NT + (tcl + 1) * TC]

            def yv(c, tcl):
                return yv_[:, c * NT + tcl * TC: c * NT + (tcl + 1) * TC]

            def memc(c):
                return memv[:, c * MEM:(c + 1) * MEM]

            def mT(c):
                return mTv[:, c * MEM:(c + 1) * MEM]

            def kx(c):
                return kxv[:, c * MEM:(c + 1) * MEM]

            S.barrier()
            for c in range(8):
                S.dma("sp", memc(c), memin[c * 128:(c + 1) * 128, :], dmem, writes=[memd[c]])
            for c in range(8):
                memd[c].w = (dmem[0], dmem[1], None)
            ps, pd = psum()
            for c in range(8):
                i = cnt["sq"] % 2
                cnt["sq"] += 1
                S.op("act", o_act(sqb[i][:, 0:MEM], memc(c), AF.Square), reads=[memd[c]], writes=[sqd[i]])
                mm([(ps[:, 0:MEM], ones_bf, sqb[i][:, 0:MEM])], [sqd[i], cbf_dep], [pd], start=(c == 0), stop=(c == 7))
            rs, rd = rstd_from(ps, pd, MEM)
            for c in range(8):
                S.op("dve", o_stt(mT(c), memc(c), vc("g_mem", L * 8 + c), rs[:, 0:MEM], ALU.mult, ALU.mult),
                     reads=[memd[c], rd, vec_dep], writes=[mTd[c]])
            for j in range(2):
                w, wd = wget(L, ("xk", j))
                for cl in range(4):
                    c = 4 * j + cl
                    ps, pd = psum()
                    its = [(ps[:, 0:MEM], w[:, kc * 512 + cl * 128: kc * 512 + (cl + 1) * 128], mT(kc)) for kc in range(8)]
                    mm(its, [wd] + mTd, [pd])
                    S.op("act", o_act(kx(c), ps[:, 0:MEM], AF.Copy), reads=[pd], writes=[kxd[c]])
            for j in range(2):
                w, wd = wget(L, ("xv", j))
                for blk in range(2):
                    ps, pd = psum()
                    its = [(ps[:, :], mT(kc)[:, blk * 128:(blk + 1) * 128], w[:, kc * 512:(kc + 1) * 512]) for kc in range(8)]
                    mm(its, [wd] + mTd, [pd])
                    S.op("dve", o_copy(Vxv[:, blk * D + j * 512: blk * D + (j + 1) * 512], ps[:, :]), reads=[pd], writes=[Vxd[blk][j]])

            S.barrier()
            prenorm(L, "g_cross_pre", 0, hT, hd)
            for hf in range(2):
                T0 = hf * NT
                for j in range(2):
                    w, wd = wget(L, ("xq", j))
                    for cl in range(4):
                        c = 4 * j + cl
                        for tcl in range(2):
                            ps, pd = psum()
                            items, reads = proj_fm(w, wd, cl * 128, 512, hT, hd, tcl, ps)
                            mm(items, reads, [pd])
                            if (cl + tcl) % 2 == 0:
                                S.op("act", o_act(qx(c, tcl), ps[:, :], AF.Copy), reads=[pd], writes=[qxd[c][tcl]])
                            else:
                                S.op("dve", o_copy(qx(c, tcl), ps[:, :]), reads=[pd], writes=[qxd[c][tcl]])
                items_l = [(tcl, h) for tcl in range(2) for h in range(4)]

                def qk(idx):
                    tcl, h = items_l[idx]
                    pb = idx % 2
                    for kb in range(2):
                        ps, pd = psum()
                        its = [(ps[:, :], kx(2 * h + dc)[:, kb * 128:(kb + 1) * 128], qx(2 * h + dc, tcl)) for dc in range(2)]
                        mm(its, [kxd[2 * h], kxd[2 * h + 1], qxd[2 * h][tcl], qxd[2 * h + 1][tcl]], [pd])
                        S.op("act", o_act(PT[pb][:, kb * TC:(kb + 1) * TC], ps[:, :], AF.Exp, scale=1.0 / 16.0),
                             reads=[pd], writes=[PTd[pb]])

                def pv(idx):
                    tcl, h = items_l[idx]
                    pb = idx % 2
                    so = 4 + 2 * (idx % 2)
                    accs = []
                    for dc in range(2):
                        po, pod = psb[so + dc], psd[so + dc]
                        its = [(po[:, :], Vxv[:, kb * D + h * 256 + dc * 128: kb * D + h * 256 + (dc + 1) * 128], PT[pb][:, kb * TC:(kb + 1) * TC]) for kb in range(2)]
                        mm(its, [Vxd[0][h // 2], Vxd[1][h // 2], PTd[pb]], [pod])
                        accs.append((po, pod))
                    pn, pnd = psum()
                    its = [(pn[:, :], ones_bf, PT[pb][:, kb * TC:(kb + 1) * TC]) for kb in range(2)]
                    mm(its, [PTd[pb], cbf_dep], [pnd])
                    S.op("act", o_act(rcp[pb], pn[:, :], AF.Ln), reads=[pnd], writes=[rcpd[pb]])
                    S.op("act", o_act(rcp[pb], rcp[pb], AF.Exp, scale=-1.0), reads=[rcpd[pb]], writes=[rcpd[pb]])
                    for dc in range(2):
                        po, pod = accs[dc]
                        S.op("dve", o_tt(oT(2 * h + dc, tcl), po[:, :], rcp[pb], ALU.mult), reads=[pod, rcpd[pb]], writes=[oTd[2 * h + dc][tcl]])

                for idx in range(len(items_l) + 1):
                    if idx < len(items_l):
                        qk(idx)
                    if idx >= 1:
                        pv(idx - 1)
                if hf == 0:
                    prenorm(L, "g_cross_pre", NT, hT, hd)
                outproj_postnorm(L, T0, std_panels(L, "xo", oT, oTd), "g_cross_post", yv, yd)

        def ffn(L):
            SB = A_SB
            hTv = vb(SB + 0, 4096)
            actv = vb(SB + 4096, NFF * NT // 2)
            sg = [vf(SB + 15360 + i * 512, 512) for i in range(2)]
            yv_ = vf(SB + 16384, 8192)
            hd = [[Dep() for _ in range(2)] for _ in range(8)]
            actd = [[Dep() for _ in range(2)] for _ in range(NFF)]
            sgd = [Dep(), Dep()]
            yd = [[Dep() for _ in range(2)] for _ in range(8)]

            def hT(kc, tcl):
                return hTv[:, kc * NT + tcl * TC: kc * NT + (tcl + 1) * TC]

            def act(f, tcl):
                return actv[:, f * NT + tcl * TC: f * NT + (tcl + 1) * TC]

            def yv(c, tcl):
                return yv_[:, c * NT + tcl * TC: c * NT + (tcl + 1) * TC]

            S.barrier()
            k = 0
            prenorm(L, "g_ffn_pre", 0, hT, hd)
            for hf in range(2):
                T0 = hf * NT
                for f in range(NFF):
                    w, wd = wget(L, ("gu", f))
                    for tcl in range(2):
                        psg, pdg = psum()
                        items, reads = proj_fm(w, wd, 0, 256, hT, hd, tcl, psg)
                        mm(items, reads, [pdg])
                        psu, pdu = psum()
                        items, reads = proj_fm(w, wd, 128, 256, hT, hd, tcl, psu)
                        mm(items, reads, [pdu])
                        b = k % 2
                        k += 1
                        S.op("act", o_act(sg[b], psg[:, :], AF.Silu), reads=[pdg], writes=[sgd[b]])
                        S.op("dve", o_tt(act(f, tcl), psu[:, :], sg[b], ALU.mult), reads=[pdu, sgd[b]], writes=[actd[f][tcl]])

                def mk(oc):
                    def getp(tcl, ps, _c={}):
                        if "w" not in _c:
                            _c["w"] = wget(L, ("dn", oc))
                        w, wd = _c["w"]
                        items = [(ps[:, :], w[:, fc * 128:(fc + 1) * 128], act(fc, tcl)) for fc in range(NFF)]
                        return items, [wd] + [actd[fc][tcl] for fc in range(NFF)]
                    return getp
                if hf == 0:
                    prenorm(L, "g_ffn_pre", NT, hT, hd)
                outproj_postnorm(L, T0, [mk(oc) for oc in range(8)], "g_ffn_post", yv, yd)

        dmem = S.dsem("mem")
        for li, L in enumerate(layers if _stage != "io" else []):
            if li > 0:
                S.barrier()
                S.new_epoch()
            if L % 2 == 0:
                even_mixer(L)
            else:
                odd_mixer(L)
            if _stage in ("pre", "proj", "attn", "mix", "attn_prep", "attn_qk", "attn_qk1", "attn_qk2", "mix0", "mixop", "op0", "op0c", "op1", "op2"):
                break
            cross(L, dmem)
            if _stage == "cross":
                break
            ffn(L)
        assert _stage != "" or ws["use"] == len(order)

        d_out = S.dsem("out")
        for c in range(8):
            S.dma("sp", out_d[c * 128:(c + 1) * 128, :], XT(c, 0, T), d_out, reads=xd[c])
        S.wait_tok("sp", d_out[0], d_out[1])
        S.finish()
    return nc


def _kp(Wm):
    K, n = Wm.shape
    return np.ascontiguousarray(Wm.reshape(K // 128, 128, n).transpose(1, 0, 2)).reshape(128, -1)


def _panel(inp, L, key):
    e = L // 2
    o = L // 2
    k0 = key[0]
    if k0 == "abq":
        return _kp(inp["ab_w_in"][e][:, 0:512])
    if k0 == "abk":
        return _kp(inp["ab_w_in"][e][:, 512:1024])
    if k0 == "abv":
        return _kp(inp["ab_w_in"][e][:, 1024:1536])
    if k0 == "abc":
        i = key[1]
        Wm = inp["ab_w_in"][e]
        cat = np.concatenate([Wm[:, 1544 + i * 128:1544 + (i + 1) * 128], Wm[:, 2056 + i * 128:2056 + (i + 1) * 128],
                              Wm[:, 2568 + i * 128:2568 + (i + 1) * 128], Wm[:, 1536:1544]], axis=1)
        return _kp(cat)
    if k0 == "abo":
        j = key[1]
        return _kp(inp["ab_w_out"][e][:, j * 512:(j + 1) * 512])
    if k0 == "cin":
        n = key[1]
        Wm = inp["c_w_in"][o]
        cat = np.concatenate([Wm[:, n * 256:(n + 1) * 256], Wm[:, 1024 + n * 256:1024 + (n + 1) * 256]], axis=1)
        return _kp(cat)
    if k0 == "cg":
        n = key[1]
        return np.concatenate([_kp(inp["c_w_a"][o][n]), _kp(inp["c_w_i"][o][n])], axis=1)
    if k0 == "co":
        j = key[1]
        return _kp(inp["c_w_out"][o][:, j * 512:(j + 1) * 512])
    if k0 == "xk":
        j = key[1]
        return _kp(inp["w_xkv"][L][:, j * 512:(j + 1) * 512])
    if k0 == "xv":
        j = key[1]
        return _kp(inp["w_xkv"][L][:, 1024 + j * 512:1024 + (j + 1) * 512])
    if k0 == "xq":
        j = key[1]
        return _kp(inp["w_xq"][L][:, j * 512:(j + 1) * 512])
    if k0 == "xo":
        j = key[1]
        return _kp(inp["w_xo"][L][:, j * 512:(j + 1) * 512])
    if k0 == "gu":
        f = key[1]
        Wm = inp["w_ffn_gu"][L]
        cat = np.concatenate([Wm[:, f * 128:(f + 1) * 128], Wm[:, DFF + f * 128:DFF + (f + 1) * 128]], axis=1)
        return _kp(cat)
    if k0 == "dn":
        oc = key[1]
        return _kp(inp["w_ffn_down"][L][:, oc * 128:(oc + 1) * 128])
    raise KeyError(key)


def pack_weights(inp, layers):
    offs, wtot = panel_offsets(layers)
    arrs = {L: np.empty((128, wtot[L]), np.float32) for L in layers}
    seen = set()
    for (L, key, n) in panel_order(layers):
        if (L, key) in seen:
            continue
        seen.add((L, key))
        p = _panel(inp, L, key)
        assert p.shape == (128, n), (key, p.shape, n)
        arrs[L][:, offs[(L, key)]:offs[(L, key)] + n] = p
    return arrs


def pack_vecs(inp):
    v = np.zeros((128, NV), np.float32)

    def fm(a):
        return np.asarray(a, np.float32).reshape(-1, 128).T

    for n in ("g_mix_pre", "g_mix_post", "g_cross_pre", "g_mem", "g_cross_post", "g_ffn_pre", "g_ffn_post"):
        for L in range(DEPTH):
            v[:, VCOL[n] + L * 8: VCOL[n] + L * 8 + 8] = fm(inp[n][L])
    for e in range(2):
        for k in range(3):
            v[:, VCOL["abcw"] + e * 12 + k * 4: VCOL["abcw"] + e * 12 + k * 4 + 4] = fm(inp["ab_conv_w"][e, k])
    for o in range(2):
        for k in range(4):
            v[:, VCOL["ccw"] + o * 32 + k * 8: VCOL["ccw"] + o * 32 + k * 8 + 8] = fm(inp["c_conv_w"][o, k])
        v[:, VCOL["ccb"] + o * 8: VCOL["ccb"] + o * 8 + 8] = fm(inp["c_conv_b"][o])
        v[:, VCOL["cba"] + o * 8: VCOL["cba"] + o * 8 + 8] = fm(np.asarray(inp["c_b_a"][o]).reshape(-1))
        v[:, VCOL["cbi"] + o * 8: VCOL["cbi"] + o * 8 + 8] = fm(np.asarray(inp["c_b_i"][o]).reshape(-1))
        v[:, VCOL["clam"] + o * 8: VCOL["clam"] + o * 8 + 8] = fm(inp["c_lam"][o])
    v[0:8, VCOL["bf"]:VCOL["bf"] + 2] = np.asarray(inp["ab_b_f"], np.float32).T
    v[:, VCOL["one"]] = 1.0
    v[:, VCOL["eps"]] = EPS
    v[0:8, VCOL["id8"]:VCOL["id8"] + 8] = np.eye(8, dtype=np.float32)
    return v


def pack_consts():
    c = np.zeros((128, NCB), np.float32)
    c[:, CB_ID:CB_ID + 128] = np.eye(128, dtype=np.float32)
    kk = np.arange(128)[:, None]
    qq = np.arange(128)[None, :]
    c[:, CB_MASK:CB_MASK + 128] = np.where(kk > qq, -30000.0, 0.0)
    c[:, CB_ONES:CB_ONES + 128] = 1.0
    for h in range(8):
        c[h, CB_SEL + h * 128:CB_SEL + (h + 1) * 128] = 1.0
        c[32 + h, CB_SEL + h * 128:CB_SEL + (h + 1) * 128] = 1.0
        c[64 + h, CB_SEL + h * 128:CB_SEL + (h + 1) * 128] = 1.0
        c[96 + h, CB_SEL + h * 128:CB_SEL + (h + 1) * 128] = 1.0
    return c


_PROG_CACHE = {}


def _get_prog(layers):
    key = tuple(layers)
    if key not in _PROG_CACHE:
        _PROG_CACHE[key] = build_program(list(layers))
    return _PROG_CACHE[key]


MODE = "fused"


def kernel(**inputs):
    inp = {k: np.asarray(v) for k, v in inputs.items()}
    x = inp["x"].astype(np.float32, copy=False)
    mem = inp["mem"].astype(np.float32, copy=False)
    B = x.shape[0]
    vecs = pack_vecs(inp)
    cbf = pack_consts()
    xT = [np.ascontiguousarray(x[b].T) for b in range(B)]
    memT = [np.ascontiguousarray(mem[b].T) for b in range(B)]
    groups = [[L] for L in range(DEPTH)] if MODE == "per_layer" else [list(range(DEPTH))]
    for layers in groups:
        nc = _get_prog(layers)
        warr = pack_weights(inp, layers)
        in_maps = []
        for b in range(B):
            m = {"xT": xT[b], "memT": memT[b], "vecs": vecs, "cbf": cbf}
            for L in layers:
                m[f"w{L}"] = warr[L]
            in_maps.append(m)
        res = run_bass_kernel_spmd(nc, in_maps, core_ids=list(range(B)))
        xT = [np.asarray(res.results[b]["outT"], np.float32) for b in range(B)]
    out = np.stack([xT[b].T for b in range(B)], axis=0)
    return np.ascontiguousarray(out.astype(np.float32))
```

```python
import numpy as np
from contextlib import ExitStack
import concourse.bass as bass
import concourse.mybir as mybir
from concourse.bass_utils import run_bass_kernel_spmd

F32 = mybir.dt.float32
BF16 = mybir.dt.bfloat16
AF = mybir.ActivationFunctionType
ALU = mybir.AluOpType

DEPTH = 4
D = 1024
T = 2048
NT = 1024
TC = 512
MEM = 256
DFF = 2816
NFF = 22
EPS = 1e-6
NSLOT = 3
SLOT_W = 2048
NG = 4

VCOL = {}
_c = 0
for _n in ("g_mix_pre", "g_mix_post", "g_cross_pre", "g_mem", "g_cross_post", "g_ffn_pre", "g_ffn_post"):
    VCOL[_n] = _c
    _c += 32
VCOL["abcw"] = _c; _c += 24
VCOL["ccw"] = _c; _c += 64
for _n in ("ccb", "cba", "cbi", "clam"):
    VCOL[_n] = _c
    _c += 16
VCOL["bf"] = _c; _c += 2
VCOL["one"] = _c; _c += 1
VCOL["eps"] = _c; _c += 1
VCOL["zero"] = _c; _c += 1
VCOL["id8"] = _c; _c += 8
NV = _c + (_c % 2)

CB_ID, CB_MASK, CB_ONES, CB_SEL = 0, 128, 256, 384
NCB = 384 + 1024


def panel_order(layers):
    order = []
    for L in layers:
        if L % 2 == 0:
            for hf in range(2):
                order += [(L, ("abq",), 4096), (L, ("abk",), 4096), (L, ("abv",), 4096)]
                order += [(L, ("abc", i), 8 * 392) for i in range(4)]
                order += [(L, ("abo", j), 4096) for j in range(2)]
        else:
            for hf in range(2):
                order += [(L, ("cin", 0), 4096)]
                for n in range(4):
                    if n < 3:
                        order += [(L, ("cin", n + 1), 4096)]
                    order += [(L, ("cg", n), 1024)]
                order += [(L, ("co", j), 4096) for j in range(2)]
        order += [(L, ("xk", j), 4096) for j in range(2)]
        order += [(L, ("xv", j), 4096) for j in range(2)]
        for hf in range(2):
            order += [(L, ("xq", j), 4096) for j in range(2)]
            order += [(L, ("xo", j), 4096) for j in range(2)]
        for hf in range(2):
            order += [(L, ("gu", f), 2048) for f in range(NFF)]
            order += [(L, ("dn", oc), NFF * 128) for oc in range(8)]
    return order


def panel_offsets(layers):
    offs = {}
    tot = {L: 0 for L in layers}
    for (L, key, n) in panel_order(layers):
        if (L, key) not in offs:
            offs[(L, key)] = tot[L]
            tot[L] += n
    return offs, tot


class Dep:
    __slots__ = ("w", "r")

    def __init__(self):
        self.w = None
        self.r = {}


class Eng:
    def __init__(self, name):
        self.name = name
        self.ops = []
        self.sem = None
        self.cnt = 0
        self.seen = {}


class Sched:
    def __init__(self, nc, stack):
        self.nc = nc
        self.stack = stack
        self.engs = {n: Eng(n) for n in ("pe", "act", "dve", "pool", "sp")}
        self.nsem = 0
        self.new_epoch()

    def new_sem(self, name):
        self.nsem += 1
        return self.stack.enter_context(self.nc.semaphore(f"{name}_{self.nsem}"))

    def new_epoch(self):
        for e in self.engs.values():
            e.sem = self.new_sem("e_" + e.name)
            e.cnt = 0

    def _waits(self, eng, reads, writes):
        need = {}

        def add(tok, raw):
            if tok is None:
                return
            sem, val, src = tok
            if src is eng and eng.name == "pe":
                return
            k = id(sem)
            if eng.seen.get(k, 0) >= val:
                return
            if k not in need or need[k][1] < val:
                need[k] = (sem, val)

        for d in reads:
            add(d.w, True)
        for d in writes:
            add(d.w, False)
            for tok in d.r.values():
                add(tok, False)
        for k, (sem, val) in need.items():
            eng.seen[k] = val
            eng.ops.append(("wait", sem, val))

    def op(self, engname, fn, reads=(), writes=(), inc=True):
        eng = self.engs[engname]
        self._waits(eng, reads, writes)
        if inc:
            eng.cnt += 1
            tok = (eng.sem, eng.cnt, eng)
        else:
            tok = (eng.sem, eng.cnt + 1, eng)
        eng.ops.append(("op", fn, eng.sem if inc else None, 1))
        for d in reads:
            d.r[id(eng.sem)] = tok
        for d in writes:
            d.w = tok
            d.r = {}

    def dma(self, engname, out, in_, dsem, reads=(), writes=()):
        eng = self.engs[engname]
        self._waits(eng, reads, writes)
        dsem[1] += 16
        tok = (dsem[0], dsem[1], None)
        eng.ops.append(("op", lambda e, o=out, i=in_: e.dma_start(out=o, in_=i), dsem[0], 16))
        for d in reads:
            d.r[id(dsem[0])] = tok
        for d in writes:
            d.w = tok
            d.r = {}

    def dsem(self, name):
        return [self.new_sem("d_" + name), 0]

    def wait_tok(self, engname, sem, val):
        self.engs[engname].ops.append(("wait", sem, val))

    def barrier(self):
        for e in self.engs.values():
            for e2 in self.engs.values():
                if e2 is e or e2.cnt == 0:
                    continue
                k = id(e2.sem)
                if e.seen.get(k, 0) >= e2.cnt:
                    continue
                e.seen[k] = e2.cnt
                e.ops.append(("wait", e2.sem, e2.cnt))

    def finish(self):
        nc = self.nc
        with nc.Block() as block:
            def runner(eng):
                def f(e):
                    for o in eng.ops:
                        if o[0] == "wait":
                            e.wait_ge(o[1], o[2])
                        else:
                            ins = o[1](e)
                            if o[2] is not None:
                                ins.then_inc(o[2], o[3])
                return f
            block.tensor(runner(self.engs["pe"]))
            block.scalar(runner(self.engs["act"]))
            block.vector(runner(self.engs["dve"]))
            block.gpsimd(runner(self.engs["pool"]))
            block.sync(runner(self.engs["sp"]))


def o_act(out, in_, func, **kw):
    return lambda e: e.activation(out=out, in_=in_, func=func, **kw)


def o_ts(out, in0, s1, s2, op0, op1=None):
    if op1 is None:
        return lambda e: e.tensor_scalar(out=out, in0=in0, scalar1=s1, scalar2=None, op0=op0)
    return lambda e: e.tensor_scalar(out=out, in0=in0, scalar1=s1, scalar2=s2, op0=op0, op1=op1)


def o_tt(out, in0, in1, op):
    return lambda e: e.tensor_tensor(out=out, in0=in0, in1=in1, op=op)


def o_stt(out, in0, scalar, in1, op0, op1):
    return lambda e: e.scalar_tensor_tensor(out=out, in0=in0, scalar=scalar, in1=in1, op0=op0, op1=op1)


def o_copy(out, in_):
    return lambda e: e.tensor_copy(out=out, in_=in_)


def o_recip(out, in_):
    return lambda e: e.reciprocal(out=out, in_=in_)


def o_memset(ap, v):
    return lambda e: e.memset(ap, v)


def o_scan(out, d0, d1, init):
    return lambda e: e.tensor_tensor_scan(out=out, data0=d0, data1=d1, initial=init, op0=ALU.mult, op1=ALU.add)


def o_mm(out, lhsT, rhs, start, stop):
    return lambda e: e.matmul(out, lhsT, rhs, start=start, stop=stop)


def build_program(layers):
    offs, wtot = panel_offsets(layers)
    order = panel_order(layers)
    nc = bass.Bass("TRN2", target_bir_lowering=False)
    xin = nc.dram_tensor("xT", [D, T], F32, kind="ExternalInput").ap()
    memin = nc.dram_tensor("memT", [D, MEM], F32, kind="ExternalInput").ap()
    vecs_d = nc.dram_tensor("vecs", [128, NV], F32, kind="ExternalInput").ap()
    cbf_d = nc.dram_tensor("cbf", [128, NCB], F32, kind="ExternalInput").ap()
    wl_d = {L: nc.dram_tensor(f"w{L}", [128, wtot[L]], F32, kind="ExternalInput").ap() for L in layers}
    out_d = nc.dram_tensor("outT", [D, T], F32, kind="ExternalOutput").ap()

    A_XT = 0
    A_SLOT = A_XT + 8 * T
    A_CBF = A_SLOT + NSLOT * SLOT_W
    A_VEC = A_CBF + NCB // 2
    A_ONEF = A_VEC + NV
    A_SQ = A_ONEF + 512
    A_RS = A_SQ + 2 * 256
    A_SB = A_RS + 2 * 512
    SCR = 26912
    AW = A_SB + SCR

    with ExitStack() as st:
        S = Sched(nc, st)
        arena = st.enter_context(nc.sbuf_tensor("arena", [128, AW], F32))
        psb = [st.enter_context(nc.psum_tensor(f"ps{i}", [128, 512], F32)) for i in range(8)]
        psd = [Dep() for _ in range(8)]
        rr_state = {"g": 0}

        def psum():
            i = rr_state["g"]
            rr_state["g"] = (i + 1) % NG
            return psb[i], psd[i]

        def vf(off, n):
            return arena[:, off:off + n]

        def vb(off, nwords):
            return arena[:, off:off + nwords].bitcast(BF16)

        def XT(c, t0, n):
            return arena[:, A_XT + c * T + t0: A_XT + c * T + t0 + n]

        xd = [[Dep() for _ in range(4)] for _ in range(8)]
        slot_bf = [vb(A_SLOT + s * SLOT_W, SLOT_W) for s in range(NSLOT)]
        slot_dep = [Dep() for _ in range(NSLOT)]
        slot_sem = [S.dsem(f"slot{s}") for s in range(NSLOT)]
        cbf = vb(A_CBF, NCB // 2)
        cbf_dep = Dep()
        vecs = vf(A_VEC, NV)
        vec_dep = Dep()
        onef = vf(A_ONEF, 512)
        onef_dep = Dep()
        sqb = [vb(A_SQ + i * 256, 256) for i in range(2)]
        sqd = [Dep() for _ in range(2)]
        rsb = [vf(A_RS + i * 512, 512) for i in range(2)]
        rsd = [Dep() for _ in range(2)]
        cnt = {"sq": 0, "rs": 0}

        ident = cbf[:, CB_ID:CB_ID + 128]
        negmask = cbf[:, CB_MASK:CB_MASK + 128]
        ones_bf = cbf[:, CB_ONES:CB_ONES + 128]

        def vc(name, idx=0, p0=0, p1=128):
            c = VCOL[name] + idx
            return vecs[p0:p1, c:c + 1]

        _stage = ""
        ws = {"issue": 0, "use": 0}

        def wget(L, key):
            idx = ws["use"]
            assert order[idx][0] == L and order[idx][1] == key, (order[idx], L, key)
            lim = min(len(order), idx + NSLOT)
            while ws["issue"] < lim:
                q = ws["issue"]
                Lq, kq, nq = order[q]
                s = q % NSLOT
                off = offs[(Lq, kq)]
                S.dma("pool", slot_bf[s][:, 0:nq], wl_d[Lq][:, off:off + nq], slot_sem[s], writes=[slot_dep[s]])
                ws["issue"] += 1
            ws["use"] += 1
            return slot_bf[idx % NSLOT], slot_dep[idx % NSLOT]

        def mm(items, reads, wdeps, start=True, stop=True):
            n = len(items)
            for i, (o, l, r) in enumerate(items):
                S.op("pe", o_mm(o, l, r, start and i == 0, stop and i == n - 1),
                     reads=reads if i == 0 else (), writes=wdeps if i == 0 else (), inc=(i == n - 1))

        d_in = S.dsem("in")
        S.dma("sp", vecs, vecs_d, d_in, writes=[vec_dep])
        d_cb = S.dsem("cb")
        S.dma("pool", cbf, cbf_d, d_cb, writes=[cbf_dep])
        d_x = S.dsem("x")
        for c in range(8):
            S.dma("sp", XT(c, 0, T), xin[c * 128:(c + 1) * 128, :], d_x, writes=xd[c])
        for c in range(8):
            for dd in xd[c]:
                dd.w = (d_x[0], d_x[1], None)
        S.op("dve", o_memset(onef, 1.0), writes=[onef_dep])

        def rstd_from(ps, pd, ncol):
            i = cnt["rs"] % 2
            cnt["rs"] += 1
            rs, rd = rsb[i], rsd[i]
            S.op("act", o_act(rs[:, 0:ncol], ps[:, 0:ncol], AF.Ln, scale=1.0 / D, bias=vc("eps")), reads=[pd, vec_dep], writes=[rd])
            S.op("act", o_act(rs[:, 0:ncol], rs[:, 0:ncol], AF.Exp, scale=-0.5), reads=[rd], writes=[rd])
            return rs, rd

        def prenorm(L, gname, T0, hT, hd):
            for tcl in range(2):
                tg = T0 + tcl * TC
                tcg = tg // TC
                ps, pd = psum()
                for c in range(8):
                    i = cnt["sq"] % 2
                    cnt["sq"] += 1
                    S.op("act", o_act(sqb[i], XT(c, tg, TC), AF.Square), reads=[xd[c][tcg]], writes=[sqd[i]])
                    mm([(ps[:, :], ones_bf, sqb[i])], [sqd[i], cbf_dep], [pd], start=(c == 0), stop=(c == 7))
                rs, rd = rstd_from(ps, pd, TC)
                for c in range(8):
                    S.op("dve", o_stt(hT(c, tcl), XT(c, tg, TC), vc(gname, L * 8 + c), rs, ALU.mult, ALU.mult),
                         reads=[xd[c][tcg], rd, vec_dep], writes=[hd[c][tcl]])

        def outproj_postnorm(L, T0, panels, gname, yv, yd):
            st_ps = [(psb[4], psd[4]), (psb[5], psd[5])]
            pending = []

            def flush():
                if _stage in ("op0", "op0c", "op1"):
                    pending.clear()
                while pending:
                    oc_, tcl_, sqi = pending.pop(0)
                    mm([(st_ps[tcl_][0][:, :], ones_bf, sqb[sqi])], [sqd[sqi], cbf_dep], [st_ps[tcl_][1]],
                       start=(oc_ == 0), stop=(oc_ == 7))

            for oc in range(8):
                getp = panels[oc]
                for tcl in range(2):
                    ps, pd = psum()
                    items, reads = getp(tcl, ps)
                    mm(items, reads, [pd])
                    flush()
                    if _stage == "op0":
                        continue
                    S.op("dve", o_copy(yv(oc, tcl), ps[:, :]), reads=[pd], writes=[yd[oc][tcl]])
                    if _stage == "op0c":
                        continue
                    i = cnt["sq"] % 2
                    cnt["sq"] += 1
                    S.op("pool", o_tt(sqb[i], yv(oc, tcl), yv(oc, tcl), ALU.mult), reads=[yd[oc][tcl]], writes=[sqd[i]])
                    pending.append((oc, tcl, i))
            flush()
            if _stage in ("op0", "op0c", "op1", "op2"):
                return
            for tcl in range(2):
                tg = T0 + tcl * TC
                tcg = tg // TC
                rs, rd = rstd_from(st_ps[tcl][0], st_ps[tcl][1], TC)
                for c in range(8):
                    S.op("dve", o_tt(yv(c, tcl), yv(c, tcl), rs, ALU.mult), reads=[yd[c][tcl], rd], writes=[yd[c][tcl]])
                    S.op("dve", o_stt(XT(c, tg, TC), yv(c, tcl), vc(gname, L * 8 + c), XT(c, tg, TC), ALU.mult, ALU.add),
                         reads=[yd[c][tcl], vec_dep, xd[c][tcg]], writes=[xd[c][tcg]])

        def std_panels(L, kname, src, srcd):
            cache = {}

            def mk(oc):
                def getp(tcl, ps):
                    j, ol = oc // 4, oc % 4
                    if (j) not in cache:
                        cache.clear()
                        cache[j] = wget(L, (kname, j))
                    w, wd = cache[j]
                    items = [(ps[:, :], w[:, kc * 512 + ol * 128: kc * 512 + (ol + 1) * 128], src(kc, tcl)) for kc in range(8)]
                    return items, [wd] + [srcd[kc][tcl] for kc in range(8)]
                return getp
            return [mk(oc) for oc in range(8)]

        def proj_fm(w, wd, coff, ncols_panel, hT, hd, tcl, ps, m=128):
            items = [(ps[0:m, :], w[:, kc * ncols_panel + coff: kc * ncols_panel + coff + m], hT(kc, tcl)) for kc in range(8)]
            return items, [wd] + [hd[kc][tcl] for kc in range(8)]

        def even_mixer(L):
            e = L // 2
            SB = A_SB
            kTv = vb(SB + 0, 4096)
            Vv = vb(SB + 4096, 4096)
            Nf = vf(SB + 8192, 2048)
            halo = vf(SB + 10240, 8)
            negbf = vf(SB + 10248, 2)
            biasT = [vf(SB + 10256 + i * 128, 128) for i in range(2)]
            hTv = vb(SB + 10512, 4096)
            qTv = vb(SB + 14608, 2048)
            cu = vf(SB + 16656, 1028)
            Cc = vf(SB + 17684, 512)
            tcv = vf(SB + 18196, 512)
            yv_ = vf(SB + 10512, 8192)
            yTv = vb(SB + 18708, 4096)
            PT = [vb(SB + 22804 + i * 256, 256) for i in range(2)]
            rcp = [vf(SB + 23316 + i * 512, 512) for i in range(2)]
            tmp = [vf(SB + 24340 + i * 512, 512) for i in range(4)]
            cq = [vb(SB + 26388 + i * 256, 256) for i in range(2)]
            kd = [[Dep() for _ in range(4)] for _ in range(4)]
            vd = [Dep() for _ in range(16)]
            Nd = [Dep() for _ in range(4)]
            halod = [Dep() for _ in range(4)]
            negbfd = Dep()
            biasTd = [Dep(), Dep()]
            hd = [[Dep() for _ in range(2)] for _ in range(8)]
            qd = [[Dep() for _ in range(2)] for _ in range(4)]
            cud, Ccd, tcd = Dep(), Dep(), Dep()
            yd = [[Dep() for _ in range(2)] for _ in range(8)]
            yTd = [[Dep() for _ in range(2)] for _ in range(8)]
            PTd = [Dep(), Dep()]
            rcpd = [Dep(), Dep()]
            tmpd = [Dep() for _ in range(4)]
            cqd = [Dep(), Dep()]

            def hT(kc, tcl):
                return hTv[:, kc * NT + tcl * TC: kc * NT + (tcl + 1) * TC]

            def qT(i, tcl):
                return qTv[:, i * NT + tcl * TC: i * NT + (tcl + 1) * TC]

            def kT(i, t0, n):
                return kTv[:, i * T + t0: i * T + t0 + n]

            def Vb(tb):
                return Vv[:, tb * 512:(tb + 1) * 512]

            def yT(c, tcl):
                return yTv[:, c * NT + tcl * TC: c * NT + (tcl + 1) * TC]

            def yv(c, tcl):
                return yv_[:, c * NT + tcl * TC: c * NT + (tcl + 1) * TC]

            S.barrier()
            S.op("dve", o_ts(negbf[0:8, :], vecs[0:8, VCOL["bf"]:VCOL["bf"] + 2], -1.0, None, ALU.mult),
                 reads=[vec_dep], writes=[negbfd])
            for i in range(2):
                S.op("pool", o_memset(cq[i], 0.0), writes=[cqd[i]])

            for hf in range(2):
                T0 = hf * NT
                if hf == 1:
                    S.barrier()
                prenorm(L, "g_mix_pre", T0, hT, hd)
                if _stage == "pre":
                    return
                w, wd = wget(L, ("abq",))
                for i in range(4):
                    for tcl in range(2):
                        ps, pd = psum()
                        items, reads = proj_fm(w, wd, i * 128, 512, hT, hd, tcl, ps)
                        mm(items, reads, [pd])
                        S.op("act", o_act(qT(i, tcl), ps[:, :], AF.Copy), reads=[pd], writes=[qd[i][tcl]])
                w, wd = wget(L, ("abk",))
                for i in range(4):
                    for tcl in range(2):
                        ps, pd = psum()
                        items, reads = proj_fm(w, wd, i * 128, 512, hT, hd, tcl, ps)
                        mm(items, reads, [pd])
                        S.op("dve", o_copy(kT(i, T0 + tcl * TC, TC), ps[:, :]), reads=[pd], writes=[kd[i][hf * 2 + tcl]])
                w, wd = wget(L, ("abv",))
                for tb in range(8):
                    tcl = tb // 4
                    ps, pd = psum()
                    items = [(ps[:, :], hT(kc, tcl)[:, (tb % 4) * 128:(tb % 4 + 1) * 128], w[:, kc * 512:(kc + 1) * 512]) for kc in range(8)]
                    mm(items, [wd] + [hd[kc][tcl] for kc in range(8)], [pd])
                    eng = "act" if tb % 2 == 0 else "dve"
                    if eng == "act":
                        S.op("act", o_act(Vb(hf * 8 + tb), ps[:, :], AF.Copy), reads=[pd], writes=[vd[hf * 8 + tb]])
                    else:
                        S.op("dve", o_copy(Vb(hf * 8 + tb), ps[:, :]), reads=[pd], writes=[vd[hf * 8 + tb]])
                for i in range(4):
                    w, wd = wget(L, ("abc", i))
                    if i == 0:
                        for tcl in range(2):
                            cg = hf * 2 + tcl
                            tg = T0 + tcl * TC
                            ps, pd = psum()
                            items, reads = proj_fm(w, wd, 384, 392, hT, hd, tcl, ps, m=8)
                            mm(items, reads, [pd])
                            tA, tB = tmp[0], tmp[1]
                            S.op("act", o_act(tA[0:8, :], ps[0:8, :], AF.Exp, scale=-1.0, bias=negbf[0:8, e:e + 1]),
                                 reads=[pd, negbfd], writes=[tmpd[0]])
                            S.op("act", o_act(tB[0:8, :], tA[0:8, :], AF.Ln, scale=1.0, bias=vc("one", 0, 0, 8)),
                                 reads=[tmpd[0], vec_dep], writes=[tmpd[1]])
                            init = Nf[0:8, tg - 1:tg] if cg > 0 else 0.0
                            S.op("dve", o_scan(Nf[0:8, tg:tg + TC], onef[0:8, :], tB[0:8, :], init),
                                 reads=[tmpd[1], onef_dep] + ([Nd[cg - 1]] if cg > 0 else []), writes=[Nd[cg]])
                    if hf == 0:
                        S.op("dve", o_memset(cu[:, 0:2], 0.0), writes=[cud])
                    else:
                        S.op("dve", o_copy(cu[:, 0:2], halo[:, 2 * i:2 * i + 2]), reads=[halod[i]], writes=[cud])
                    for tcl in range(2):
                        psB, pdB = psum()
                        items, reads = proj_fm(w, wd, 0, 392, hT, hd, tcl, psB)
                        mm(items, reads, [pdB])
                        psC, pdC = psum()
                        items, reads = proj_fm(w, wd, 128, 392, hT, hd, tcl, psC)
                        mm(items, reads, [pdC])
                        psU, pdU = psum()
                        items, reads = proj_fm(w, wd, 256, 392, hT, hd, tcl, psU)
                        mm(items, reads, [pdU])
                        S.op("act", o_act(Cc, psC[:, :], AF.Copy), reads=[pdC], writes=[Ccd])
                        b0 = tcl * TC
                        S.op("dve", o_tt(cu[:, 2 + b0:2 + b0 + TC], psU[:, :], Cc, ALU.mult), reads=[pdU, Ccd], writes=[cud])
                        cw = VCOL["abcw"] + e * 12 + i
                        S.op("dve", o_ts(tcv, cu[:, b0:b0 + TC], vecs[:, cw:cw + 1], None, ALU.mult),
                             reads=[cud, vec_dep], writes=[tcd])
                        S.op("dve", o_stt(tcv, cu[:, b0 + 1:b0 + 1 + TC], vecs[:, cw + 4:cw + 5], tcv, ALU.mult, ALU.add),
                             reads=[cud, tcd], writes=[tcd])
                        S.op("dve", o_stt(tcv, cu[:, b0 + 2:b0 + 2 + TC], vecs[:, cw + 8:cw + 9], tcv, ALU.mult, ALU.add),
                             reads=[cud, tcd], writes=[tcd])
                        S.op("dve", o_tt(yT(4 + i, tcl), psB[:, :], tcv, ALU.mult), reads=[pdB, tcd], writes=[yTd[4 + i][tcl]])
                    if hf == 0:
                        S.op("dve", o_copy(halo[:, 2 * i:2 * i + 2], cu[:, NT:NT + 2]), reads=[cud], writes=[halod[i]])

                if _stage == "proj":
                    return
                for tcl in range(2):
                    cg = hf * 2 + tcl
                    tg = T0 + tcl * TC
                    nkb = 4 * cg + 4
                    Rcol = Nf[0:8, tg + 255:tg + 256]
                    Rdeps = [Nd[cg]]
                    cqb, cqbd = cq[cg % 2], cqd[cg % 2]
                    bT, bTd = biasT[cg % 2], biasTd[cg % 2]
                    psT, pdT = psum()
                    for seg in range(cg + 1):
                        tb_, tbd_ = tmp[2 + seg % 2], tmpd[2 + seg % 2]
                        S.op("dve", o_ts(tb_[0:8, :], Nf[0:8, seg * TC:(seg + 1) * TC], Rcol, None, ALU.subtract),
                             reads=[Nd[seg]] + Rdeps, writes=[tbd_])
                        for jb in range(4):
                            j = seg * 4 + jb
                            mm([(psT[:, j * 8:(j + 1) * 8], tb_[0:8, jb * 128:(jb + 1) * 128], vecs[0:8, VCOL["id8"]:VCOL["id8"] + 8])],
                               [tbd_, vec_dep], [pdT])
                    S.op("dve", o_copy(bT[:, 0:nkb * 8], psT[:, 0:nkb * 8]), reads=[pdT], writes=[bTd])

                    if _stage == "attn_prep":
                        continue
                    items_l = [(h, j) for h in range(8) for j in range(nkb)]
                    state = {}

                    def qk(idx):
                        h, j = items_l[idx]
                        i, hp = h // 2, h % 2
                        jj = j - 4 * cg
                        c0 = 0 if jj <= 0 else jj * 128
                        ps, pd = psum()
                        lo, hi = hp * 64, hp * 64 + 64
                        its = [(ps[:, c0:TC], kT(i, j * 128, 128)[lo:hi, :], qT(i, tcl)[lo:hi, c0:TC])]
                        if jj >= 0 and _stage not in ("attn_qk1", "attn_qk2"):
                            its.append((ps[:, c0:c0 + 128], ident, negmask))
                        mm(its, [kd[i][j // 4], qd[i][tcl], cbf_dep], [pd])
                        pb = idx % 2
                        S.op("act", o_act(PT[pb][:, c0:TC], ps[:, c0:TC], AF.Exp, scale=0.125, bias=bT[:, j * 8 + h:j * 8 + h + 1]),
                             reads=[pd, bTd], writes=[PTd[pb]])
                        state[idx] = (pb, c0)

                    def pv(idx):
                        h, j = items_l[idx]
                        i, hp = h // 2, h % 2
                        pb, c0 = state.pop(idx)
                        so = 4 + 2 * (h % 2)
                        pso, psod, psn, psnd = psb[so], psd[so], psb[so + 1], psd[so + 1]
                        mm([(pso[:, c0:TC], Vb(j)[:, i * 128:(i + 1) * 128], PT[pb][:, c0:TC])], [vd[j], PTd[pb]], [psod],
                           start=(j == 0), stop=(j == nkb - 1))
                        mm([(psn[:, c0:TC], ones_bf, PT[pb][:, c0:TC])], [PTd[pb], cbf_dep], [psnd],
                           start=(j == 0), stop=(j == nkb - 1))
                        if j == nkb - 1:
                            rb = h % 2
                            lo, hi = hp * 64, hp * 64 + 64
                            S.op("dve", o_recip(rcp[rb][lo:hi, :], psn[lo:hi, :]), reads=[psnd], writes=[rcpd[rb]])
                            S.op("dve", o_tt(yT(i, tcl)[lo:hi, :], pso[lo:hi, :], rcp[rb][lo:hi, :], ALU.mult),
                                 reads=[psod, rcpd[rb]], writes=[yTd[i][tcl]])

                    n_it = len(items_l)
                    for idx in range(n_it + 1):
                        if idx < n_it:
                            qk(idx)
                        if idx >= 1 and not _stage.startswith("attn_qk"):
                            pv(idx - 1)

                if _stage in ("attn", "attn_prep", "attn_qk", "attn_qk1", "attn_qk2"):
                    return
                S.barrier()
                if _stage == "mixop":
                    return
                outproj_postnorm(L, T0, std_panels(L, "abo", yT, yTd), "g_mix_post", yv, yd)
                if _stage in ("mix0", "op0", "op0c", "op1", "op2"):
                    return

        def odd_mixer(L):
            o = L // 2
            SB = A_SB
            halo = vf(SB + 0, 24)
            carry = vf(SB + 32, 8)
            sc1 = vf(SB + 40, 8)
            sc2 = vf(SB + 48, 8)
            spt = vf(SB + 56, 8)
            hTv = vb(SB + 128, 4096)
            ubuf = [vf(SB + 4224 + i * 1028, 1028) for i in range(2)]
            uc = [vf(SB + 6280 + i * 1024, 1024) for i in range(2)]
            yv_ = vf(SB + 128, 8192)
            ggv = vb(SB + 8328, 4096)
            yTv = vb(SB + 12424, 4096)
            ucb = [vb(SB + 16520 + i * 512, 512) for i in range(2)]
            rr0, ii0, aa, ss, bb, hs, rr1, ii1 = [vf(SB + 17544 + i * 1024, 1024) for i in range(8)]
            rrs, iis = [rr0, rr1], [ii0, ii1]
            halod = [Dep() for _ in range(8)]
            carryd = [Dep() for _ in range(8)]
            scd = Dep()
            hd = [[Dep() for _ in range(2)] for _ in range(8)]
            ubd = [Dep(), Dep()]
            ucd = [Dep(), Dep()]
            ucbd = [Dep(), Dep()]
            ggd = [[Dep() for _ in range(2)] for _ in range(8)]
            yTd = [[Dep() for _ in range(2)] for _ in range(8)]
            yd = [[Dep() for _ in range(2)] for _ in range(8)]
            aad, ssd, bbd, hsd = [Dep() for _ in range(4)]
            rrds, iids = [Dep(), Dep()], [Dep(), Dep()]

            def hT(kc, tcl):
                return hTv[:, kc * NT + tcl * TC: kc * NT + (tcl + 1) * TC]

            def gg(c, tcl):
                return ggv[:, c * NT + tcl * TC: c * NT + (tcl + 1) * TC]

            def yT(c, tcl):
                return yTv[:, c * NT + tcl * TC: c * NT + (tcl + 1) * TC]

            def yv(c, tcl):
                return yv_[:, c * NT + tcl * TC: c * NT + (tcl + 1) * TC]

            S.barrier()
            lam = vecs[:, VCOL["clam"] + o * 8: VCOL["clam"] + o * 8 + 8]
            S.op("act", o_act(spt, lam, AF.Exp, scale=-1.0), reads=[vec_dep], writes=[scd])
            S.op("act", o_act(spt, spt, AF.Ln, scale=1.0, bias=vc("one")), reads=[scd, vec_dep], writes=[scd])
            S.op("dve", o_ts(sc1, spt, -8.0, None, ALU.mult), reads=[scd], writes=[scd])
            S.op("dve", o_ts(sc2, spt, -16.0, None, ALU.mult), reads=[scd], writes=[scd])

            for hf in range(2):
                T0 = hf * NT
                if hf == 1:
                    S.barrier()
                prenorm(L, "g_mix_pre", T0, hT, hd)
                def in_proj(n):
                    w, wd = wget(L, ("cin", n))
                    for ch in range(2):
                        c = 2 * n + ch
                        if hf == 0:
                            S.op("pool", o_memset(ubuf[ch][:, 0:3], 0.0), writes=[ubd[ch]])
                        else:
                            S.op("pool", o_copy(ubuf[ch][:, 0:3], halo[:, 3 * c:3 * c + 3]), reads=[halod[c]], writes=[ubd[ch]])
                        for tcl in range(2):
                            ps, pd = psum()
                            items, reads = proj_fm(w, wd, ch * 128, 512, hT, hd, tcl, ps)
                            mm(items, reads, [pd])
                            S.op("act", o_act(gg(c, tcl), ps[:, :], AF.Gelu_apprx_tanh), reads=[pd], writes=[ggd[c][tcl]])
                            ps, pd = psum()
                            items, reads = proj_fm(w, wd, 256 + ch * 128, 512, hT, hd, tcl, ps)
                            mm(items, reads, [pd])
                            S.op("dve", o_copy(ubuf[ch][:, 3 + tcl * TC:3 + (tcl + 1) * TC], ps[:, :]), reads=[pd], writes=[ubd[ch]])

                def conv(n):
                    for ch in range(2):
                        c = 2 * n + ch
                        cw = VCOL["ccw"] + o * 32 + c
                        cbias = vc("ccb", o * 8 + c)
                        S.op("act", o_act(uc[ch], ubuf[ch][:, 0:NT], AF.Identity, scale=vecs[:, cw:cw + 1], bias=cbias),
                             reads=[ubd[ch], vec_dep], writes=[ucd[ch]])
                        for k in range(1, 4):
                            S.op("dve", o_stt(uc[ch], ubuf[ch][:, k:k + NT], vecs[:, cw + 8 * k:cw + 8 * k + 1], uc[ch], ALU.mult, ALU.add),
                                 reads=[ubd[ch], ucd[ch]], writes=[ucd[ch]])
                        if hf == 0:
                            S.op("pool", o_copy(halo[:, 3 * c:3 * c + 3], ubuf[ch][:, NT:NT + 3]), reads=[ubd[ch]], writes=[halod[c]])
                        S.op("dve", o_copy(ucb[ch], uc[ch]), reads=[ucd[ch]], writes=[ucbd[ch]])

                def gates(n):
                    wg, wgd = wget(L, ("cg", n))
                    for dch in range(2):
                        c = 2 * n + dch
                        rr, ii, rrd, iid = rrs[dch], iis[dch], rrds[dch], iids[dch]
                        for tcl in range(2):
                            ps, pd = psum()
                            its = [(ps[:, :], wg[:, cc * 256 + dch * 128: cc * 256 + (dch + 1) * 128], ucb[cc][:, tcl * TC:(tcl + 1) * TC]) for cc in range(2)]
                            mm(its, [wgd, ucbd[0], ucbd[1]], [pd])
                            S.op("act", o_act(rr[:, tcl * TC:(tcl + 1) * TC], ps[:, :], AF.Sigmoid, scale=1.0, bias=vc("cba", o * 8 + c)),
                                 reads=[pd, vec_dep], writes=[rrd])
                            ps, pd = psum()
                            its = [(ps[:, :], wg[:, 512 + cc * 256 + dch * 128: 512 + cc * 256 + (dch + 1) * 128], ucb[cc][:, tcl * TC:(tcl + 1) * TC]) for cc in range(2)]
                            mm(its, [wgd, ucbd[0], ucbd[1]], [pd])
                            S.op("act", o_act(ii[:, tcl * TC:(tcl + 1) * TC], ps[:, :], AF.Sigmoid, scale=1.0, bias=vc("cbi", o * 8 + c)),
                                 reads=[pd, vec_dep], writes=[iid])
                        S.op("act", o_act(aa, rr, AF.Exp, scale=sc1[:, c:c + 1]), reads=[rrd, scd], writes=[aad])
                        S.op("act", o_act(ss, rr, AF.Exp, scale=sc2[:, c:c + 1]), reads=[rrd, scd], writes=[ssd])
                        S.op("act", o_act(ss, ss, AF.Sqrt, scale=-1.0, bias=vc("one")), reads=[ssd, vec_dep], writes=[ssd])
                        S.op("pool", o_tt(bb, ii, uc[dch], ALU.mult), reads=[iid, ucd[dch]], writes=[bbd])
                        S.op("dve", o_tt(bb, bb, ss, ALU.mult), reads=[bbd, ssd], writes=[bbd])
                        init = carry[:, c:c + 1] if hf == 1 else 0.0
                        S.op("dve", o_scan(hs, aa, bb, init), reads=[aad, bbd] + ([carryd[c]] if hf == 1 else []), writes=[hsd])
                        if hf == 0:
                            S.op("dve", o_copy(carry[:, c:c + 1], hs[:, NT - 1:NT]), reads=[hsd], writes=[carryd[c]])
                        for tcl in range(2):
                            S.op("dve", o_tt(yT(c, tcl), gg(c, tcl), hs[:, tcl * TC:(tcl + 1) * TC], ALU.mult),
                                 reads=[ggd[c][tcl], hsd], writes=[yTd[c][tcl]])

                in_proj(0)
                conv(0)
                for n in range(4):
                    if n < 3:
                        in_proj(n + 1)
                    gates(n)
                    if n < 3:
                        conv(n + 1)
                S.barrier()
                outproj_postnorm(L, T0, std_panels(L, "co", yT, yTd), "g_mix_post", yv, yd)

        def cross(L, dmem):
            SB = A_SB
            kxv = vb(SB + 0, 1024)
            Vxv = vb(SB + 1024, 1024)
            memv = vf(SB + 2048, 2048)
            mTv = vb(SB + 4096, 1024)
            hTv = vb(SB + 2048, 4096)
            qxv = vb(SB + 6144, 4096)
            yv_ = vf(SB + 16384, 8192)
            oTv = vb(SB + 10240, 4096)
            PT = [vb(SB + 14336 + i * 512, 512) for i in range(2)]
            rcp = [vf(SB + 15360 + i * 512, 512) for i in range(2)]
            kxd = [Dep() for _ in range(8)]
            Vxd = [[Dep() for _ in range(2)] for _ in range(2)]
            memd = [Dep() for _ in range(8)]
            mTd = [Dep() for _ in range(8)]
            hd = [[Dep() for _ in range(2)] for _ in range(8)]
            qxd = [[Dep() for _ in range(2)] for _ in range(8)]
            oTd = [[Dep() for _ in range(2)] for _ in range(8)]
            yd = [[Dep() for _ in range(2)] for _ in range(8)]
            PTd = [Dep(), Dep()]
            rcpd = [Dep(), Dep()]

            def hT(kc, tcl):
                return hTv[:, kc * NT + tcl * TC: kc * NT + (tcl + 1) * TC]

            def qx(c, tcl):
                return qxv[:, c * NT + tcl * TC: c * NT + (tcl + 1) * TC]

            def oT(c, tcl):
                return oTv[:, c * NT + tcl * TC: c * NT + (tcl + 1) * TC]

            def yv(c, tcl):
                return yv_[:, c * NT + tcl * TC: c * NT + (tcl + 1) * TC]

            def memc(c):
                return memv[:, c * MEM:(c + 1) * MEM]

            def mT(c):
                return mTv[:, c * MEM:(c + 1) * MEM]

            def kx(c):
                return kxv[:, c * MEM:(c + 1) * MEM]

            S.barrier()
            for c in range(8):
                S.dma("sp", memc(c), memin[c * 128:(c + 1) * 128, :], dmem, writes=[memd[c]])
            for c in range(8):
                memd[c].w = (dmem[0], dmem[1], None)
            ps, pd = psum()
            for c in range(8):
                i = cnt["sq"] % 2
                cnt["sq"] += 1
                S.op("act", o_act(sqb[i][:, 0:MEM], memc(c), AF.Square), reads=[memd[c]], writes=[sqd[i]])
                mm([(ps[:, 0:MEM], ones_bf, sqb[i][:, 0:MEM])], [sqd[i], cbf_dep], [pd], start=(c == 0), stop=(c == 7))
            rs, rd = rstd_from(ps, pd, MEM)
            for c in range(8):
                S.op("dve", o_stt(mT(c), memc(c), vc("g_mem", L * 8 + c), rs[:, 0:MEM], ALU.mult, ALU.mult),
                     reads=[memd[c], rd, vec_dep], writes=[mTd[c]])
            for j in range(2):
                w, wd = wget(L, ("xk", j))
                for cl in range(4):
                    c = 4 * j + cl
                    ps, pd = psum()
                    its = [(ps[:, 0:MEM], w[:, kc * 512 + cl * 128: kc * 512 + (cl + 1) * 128], mT(kc)) for kc in range(8)]
                    mm(its, [wd] + mTd, [pd])
                    S.op("act", o_act(kx(c), ps[:, 0:MEM], AF.Copy), reads=[pd], writes=[kxd[c]])
            for j in range(2):
                w, wd = wget(L, ("xv", j))
                for blk in range(2):
                    ps, pd = psum()
                    its = [(ps[:, :], mT(kc)[:, blk * 128:(blk + 1) * 128], w[:, kc * 512:(kc + 1) * 512]) for kc in range(8)]
                    mm(its, [wd] + mTd, [pd])
                    S.op("dve", o_copy(Vxv[:, blk * D + j * 512: blk * D + (j + 1) * 512], ps[:, :]), reads=[pd], writes=[Vxd[blk][j]])

            S.barrier()
            prenorm(L, "g_cross_pre", 0, hT, hd)
            for hf in range(2):
                T0 = hf * NT
                for j in range(2):
                    w, wd = wget(L, ("xq", j))
                    for cl in range(4):
                        c = 4 * j + cl
                        for tcl in range(2):
                            ps, pd = psum()
                            items, reads = proj_fm(w, wd, cl * 128, 512, hT, hd, tcl, ps)
                            mm(items, reads, [pd])
                            if (cl + tcl) % 2 == 0:
                                S.op("act", o_act(qx(c, tcl), ps[:, :], AF.Copy), reads=[pd], writes=[qxd[c][tcl]])
                            else:
                                S.op("dve", o_copy(qx(c, tcl), ps[:, :]), reads=[pd], writes=[qxd[c][tcl]])
                items_l = [(tcl, h) for tcl in range(2) for h in range(4)]

                def qk(idx):
                    tcl, h = items_l[idx]
                    pb = idx % 2
                    for kb in range(2):
                        ps, pd = psum()
                        its = [(ps[:, :], kx(2 * h + dc)[:, kb * 128:(kb + 1) * 128], qx(2 * h + dc, tcl)) for dc in range(2)]
                        mm(its, [kxd[2 * h], kxd[2 * h + 1], qxd[2 * h][tcl], qxd[2 * h + 1][tcl]], [pd])
                        S.op("act", o_act(PT[pb][:, kb * TC:(kb + 1) * TC], ps[:, :], AF.Exp, scale=1.0 / 16.0),
                             reads=[pd], writes=[PTd[pb]])

                def pv(idx):
                    tcl, h = items_l[idx]
                    pb = idx % 2
                    so = 4 + 2 * (idx % 2)
                    accs = []
                    for dc in range(2):
                        po, pod = psb[so + dc], psd[so + dc]
                        its = [(po[:, :], Vxv[:, kb * D + h * 256 + dc * 128: kb * D + h * 256 + (dc + 1) * 128], PT[pb][:, kb * TC:(kb + 1) * TC]) for kb in range(2)]
                        mm(its, [Vxd[0][h // 2], Vxd[1][h // 2], PTd[pb]], [pod])
                        accs.append((po, pod))
                    pn, pnd = psum()
                    its = [(pn[:, :], ones_bf, PT[pb][:, kb * TC:(kb + 1) * TC]) for kb in range(2)]
                    mm(its, [PTd[pb], cbf_dep], [pnd])
                    S.op("act", o_act(rcp[pb], pn[:, :], AF.Ln), reads=[pnd], writes=[rcpd[pb]])
                    S.op("act", o_act(rcp[pb], rcp[pb], AF.Exp, scale=-1.0), reads=[rcpd[pb]], writes=[rcpd[pb]])
                    for dc in range(2):
                        po, pod = accs[dc]
                        S.op("dve", o_tt(oT(2 * h + dc, tcl), po[:, :], rcp[pb], ALU.mult), reads=[pod, rcpd[pb]], writes=[oTd[2 * h + dc][tcl]])

                for idx in range(len(items_l) + 1):
                    if idx < len(items_l):
                        qk(idx)
                    if idx >= 1:
                        pv(idx - 1)
                if hf == 0:
                    prenorm(L, "g_cross_pre", NT, hT, hd)
                outproj_postnorm(L, T0, std_panels(L, "xo", oT, oTd), "g_cross_post", yv, yd)

        def ffn(L):
            SB = A_SB
            hTv = vb(SB + 0, 4096)
            actv = vb(SB + 4096, NFF * NT // 2)
            sg = [vf(SB + 15360 + i * 512, 512) for i in range(2)]
            yv_ = vf(SB + 16384, 8192)
            hd = [[Dep() for _ in range(2)] for _ in range(8)]
            actd = [[Dep() for _ in range(2)] for _ in range(NFF)]
            sgd = [Dep(), Dep()]
            yd = [[Dep() for _ in range(2)] for _ in range(8)]

            def hT(kc, tcl):
                return hTv[:, kc * NT + tcl * TC: kc * NT + (tcl + 1) * TC]

            def act(f, tcl):
                return actv[:, f * NT + tcl * TC: f * NT + (tcl + 1) * TC]

            def yv(c, tcl):
                return yv_[:, c * NT + tcl * TC: c * NT + (tcl + 1) * TC]

            S.barrier()
            k = 0
            prenorm(L, "g_ffn_pre", 0, hT, hd)
            for hf in range(2):
                T0 = hf * NT
                for f in range(NFF):
                    w, wd = wget(L, ("gu", f))
                    for tcl in range(2):
                        psg, pdg = psum()
                        items, reads = proj_fm(w, wd, 0, 256, hT, hd, tcl, psg)
                        mm(items, reads, [pdg])
                        psu, pdu = psum()
                        items, reads = proj_fm(w, wd, 128, 256, hT, hd, tcl, psu)
                        mm(items, reads, [pdu])
                        b = k % 2
                        k += 1
                        S.op("act", o_act(sg[b], psg[:, :], AF.Silu), reads=[pdg], writes=[sgd[b]])
                        S.op("dve", o_tt(act(f, tcl), psu[:, :], sg[b], ALU.mult), reads=[pdu, sgd[b]], writes=[actd[f][tcl]])

                def mk(oc):
                    def getp(tcl, ps, _c={}):
                        if "w" not in _c:
                            _c["w"] = wget(L, ("dn", oc))
                        w, wd = _c["w"]
                        items = [(ps[:, :], w[:, fc * 128:(fc + 1) * 128], act(fc, tcl)) for fc in range(NFF)]
                        return items, [wd] + [actd[fc][tcl] for fc in range(NFF)]
                    return getp
                if hf == 0:
                    prenorm(L, "g_ffn_pre", NT, hT, hd)
                outproj_postnorm(L, T0, [mk(oc) for oc in range(8)], "g_ffn_post", yv, yd)

        dmem = S.dsem("mem")
        for li, L in enumerate(layers if _stage != "io" else []):
            if li > 0:
                S.barrier()
                S.new_epoch()
            if L % 2 == 0:
                even_mixer(L)
            else:
                odd_mixer(L)
            if _stage in ("pre", "proj", "attn", "mix", "attn_prep", "attn_qk", "attn_qk1", "attn_qk2", "mix0", "mixop", "op0", "op0c", "op1", "op2"):
                break
            cross(L, dmem)
            if _stage == "cross":
                break
            ffn(L)
        assert _stage != "" or ws["use"] == len(order)

        d_out = S.dsem("out")
        for c in range(8):
            S.dma("sp", out_d[c * 128:(c + 1) * 128, :], XT(c, 0, T), d_out, reads=xd[c])
        S.wait_tok("sp", d_out[0], d_out[1])
        S.finish()
    return nc


def _kp(Wm):
    K, n = Wm.shape
    return np.ascontiguousarray(Wm.reshape(K // 128, 128, n).transpose(1, 0, 2)).reshape(128, -1)


def _panel(inp, L, key):
    e = L // 2
    o = L // 2
    k0 = key[0]
    if k0 == "abq":
        return _kp(inp["ab_w_in"][e][:, 0:512])
    if k0 == "abk":
        return _kp(inp["ab_w_in"][e][:, 512:1024])
    if k0 == "abv":
        return _kp(inp["ab_w_in"][e][:, 1024:1536])
    if k0 == "abc":
        i = key[1]
        Wm = inp["ab_w_in"][e]
        cat = np.concatenate([Wm[:, 1544 + i * 128:1544 + (i + 1) * 128], Wm[:, 2056 + i * 128:2056 + (i + 1) * 128],
                              Wm[:, 2568 + i * 128:2568 + (i + 1) * 128], Wm[:, 1536:1544]], axis=1)
        return _kp(cat)
    if k0 == "abo":
        j = key[1]
        return _kp(inp["ab_w_out"][e][:, j * 512:(j + 1) * 512])
    if k0 == "cin":
        n = key[1]
        Wm = inp["c_w_in"][o]
        cat = np.concatenate([Wm[:, n * 256:(n + 1) * 256], Wm[:, 1024 + n * 256:1024 + (n + 1) * 256]], axis=1)
        return _kp(cat)
    if k0 == "cg":
        n = key[1]
        return np.concatenate([_kp(inp["c_w_a"][o][n]), _kp(inp["c_w_i"][o][n])], axis=1)
    if k0 == "co":
        j = key[1]
        return _kp(inp["c_w_out"][o][:, j * 512:(j + 1) * 512])
    if k0 == "xk":
        j = key[1]
        return _kp(inp["w_xkv"][L][:, j * 512:(j + 1) * 512])
    if k0 == "xv":
        j = key[1]
        return _kp(inp["w_xkv"][L][:, 1024 + j * 512:1024 + (j + 1) * 512])
    if k0 == "xq":
        j = key[1]
        return _kp(inp["w_xq"][L][:, j * 512:(j + 1) * 512])
    if k0 == "xo":
        j = key[1]
        return _kp(inp["w_xo"][L][:, j * 512:(j + 1) * 512])
    if k0 == "gu":
        f = key[1]
        Wm = inp["w_ffn_gu"][L]
        cat = np.concatenate([Wm[:, f * 128:(f + 1) * 128], Wm[:, DFF + f * 128:DFF + (f + 1) * 128]], axis=1)
        return _kp(cat)
    if k0 == "dn":
        oc = key[1]
        return _kp(inp["w_ffn_down"][L][:, oc * 128:(oc + 1) * 128])
    raise KeyError(key)


def pack_weights(inp, layers):
    offs, wtot = panel_offsets(layers)
    arrs = {L: np.empty((128, wtot[L]), np.float32) for L in layers}
    seen = set()
    for (L, key, n) in panel_order(layers):
        if (L, key) in seen:
            continue
        seen.add((L, key))
        p = _panel(inp, L, key)
        assert p.shape == (128, n), (key, p.shape, n)
        arrs[L][:, offs[(L, key)]:offs[(L, key)] + n] = p
    return arrs


def pack_vecs(inp):
    v = np.zeros((128, NV), np.float32)

    def fm(a):
        return np.asarray(a, np.float32).reshape(-1, 128).T

    for n in ("g_mix_pre", "g_mix_post", "g_cross_pre", "g_mem", "g_cross_post", "g_ffn_pre", "g_ffn_post"):
        for L in range(DEPTH):
            v[:, VCOL[n] + L * 8: VCOL[n] + L * 8 + 8] = fm(inp[n][L])
    for e in range(2):
        for k in range(3):
            v[:, VCOL["abcw"] + e * 12 + k * 4: VCOL["abcw"] + e * 12 + k * 4 + 4] = fm(inp["ab_conv_w"][e, k])
    for o in range(2):
        for k in range(4):
            v[:, VCOL["ccw"] + o * 32 + k * 8: VCOL["ccw"] + o * 32 + k * 8 + 8] = fm(inp["c_conv_w"][o, k])
        v[:, VCOL["ccb"] + o * 8: VCOL["ccb"] + o * 8 + 8] = fm(inp["c_conv_b"][o])
        v[:, VCOL["cba"] + o * 8: VCOL["cba"] + o * 8 + 8] = fm(np.asarray(inp["c_b_a"][o]).reshape(-1))
        v[:, VCOL["cbi"] + o * 8: VCOL["cbi"] + o * 8 + 8] = fm(np.asarray(inp["c_b_i"][o]).reshape(-1))
        v[:, VCOL["clam"] + o * 8: VCOL["clam"] + o * 8 + 8] = fm(inp["c_lam"][o])
    v[0:8, VCOL["bf"]:VCOL["bf"] + 2] = np.asarray(inp["ab_b_f"], np.float32).T
    v[:, VCOL["one"]] = 1.0
    v[:, VCOL["eps"]] = EPS
    v[0:8, VCOL["id8"]:VCOL["id8"] + 8] = np.eye(8, dtype=np.float32)
    return v


def pack_consts():
    c = np.zeros((128, NCB), np.float32)
    c[:, CB_ID:CB_ID + 128] = np.eye(128, dtype=np.float32)
    kk = np.arange(128)[:, None]
    qq = np.arange(128)[None, :]
    c[:, CB_MASK:CB_MASK + 128] = np.where(kk > qq, -30000.0, 0.0)
    c[:, CB_ONES:CB_ONES + 128] = 1.0
    for h in range(8):
        c[h, CB_SEL + h * 128:CB_SEL + (h + 1) * 128] = 1.0
        c[32 + h, CB_SEL + h * 128:CB_SEL + (h + 1) * 128] = 1.0
        c[64 + h, CB_SEL + h * 128:CB_SEL + (h + 1) * 128] = 1.0
        c[96 + h, CB_SEL + h * 128:CB_SEL + (h + 1) * 128] = 1.0
    return c


_PROG_CACHE = {}


def _get_prog(layers):
    key = tuple(layers)
    if key not in _PROG_CACHE:
        _PROG_CACHE[key] = build_program(list(layers))
    return _PROG_CACHE[key]


MODE = "fused"


def kernel(**inputs):
    inp = {k: np.asarray(v) for k, v in inputs.items()}
    x = inp["x"].astype(np.float32, copy=False)
    mem = inp["mem"].astype(np.float32, copy=False)
    B = x.shape[0]
    vecs = pack_vecs(inp)
    cbf = pack_consts()
    xT = [np.ascontiguousarray(x[b].T) for b in range(B)]
    memT = [np.ascontiguousarray(mem[b].T) for b in range(B)]
    groups = [[L] for L in range(DEPTH)] if MODE == "per_layer" else [list(range(DEPTH))]
    for layers in groups:
        nc = _get_prog(layers)
        warr = pack_weights(inp, layers)
        in_maps = []
        for b in range(B):
            m = {"xT": xT[b], "memT": memT[b], "vecs": vecs, "cbf": cbf}
            for L in layers:
                m[f"w{L}"] = warr[L]
            in_maps.append(m)
        res = run_bass_kernel_spmd(nc, in_maps, core_ids=list(range(B)))
        xT = [np.asarray(res.results[b]["outT"], np.float32) for b in range(B)]
    out = np.stack([xT[b].T for b in range(B)], axis=0)
    return np.ascontiguousarray(out.astype(np.float32))
```
